# Optimizing a Trainium2 kernel written in Bass

```python
import jax, jax.numpy as jnp
from jax import lax
import numpy as np

D_MODEL = 1024
BATCH = 8
SEQ = 2048
DEPTH = 1
DEC_BATCH = 128
DEC_SEQ = 4
PAST_LEN = 2048
PAGE_SIZE = 128

HEAD_DIM = 64
N_HEADS = 8
N_KV_HEADS = 2
GQA = N_HEADS // N_KV_HEADS
ATTN_WIDTH = N_HEADS * HEAD_DIM
D_RNN = D_MODEL - ATTN_WIDTH
RNN_BLOCKS = 8
RNN_BLOCK = D_RNN // RNN_BLOCKS
CONV_WIDTH = 4
LRU_C = 8.0
D_FF = 4 * D_MODEL
ROT_DIM = HEAD_DIM // 4
ROPE_THETA = 500000.0
CMP_BLOCK = 32
CMP_STRIDE = 16
SEL_BLOCK = 64
TOP_N = 16
WINDOW = 512
WIN_QBLK = 128
SLC_QBLK = 64
N_KV_SLOTS = 4
KV_COLS = N_KV_HEADS * HEAD_DIM
GATE_COLS = 3 * N_HEADS
IN_COLS = ATTN_WIDTH + 6 * KV_COLS + GATE_COLS + 2 * D_RNN
SPLITS = (ATTN_WIDTH, ATTN_WIDTH + KV_COLS, ATTN_WIDTH + 2 * KV_COLS, ATTN_WIDTH + 3 * KV_COLS,
          ATTN_WIDTH + 4 * KV_COLS, ATTN_WIDTH + 5 * KV_COLS, ATTN_WIDTH + 6 * KV_COLS,
          ATTN_WIDTH + 6 * KV_COLS + GATE_COLS, ATTN_WIDTH + 6 * KV_COLS + GATE_COLS + D_RNN)
EPS = 1e-6
NEG = -1e30
SEL_BONUS = 1e4

kernel_name = 'nsa_rglru_hybrid_step'


def rmsnorm(x, g):
    xf = x.astype(jnp.float32)
    y = xf * lax.rsqrt(jnp.mean(xf * xf, axis=-1, keepdims=True) + EPS) * g.astype(jnp.float32)
    return y.astype(x.dtype)


def rotary(x, pos):
    half = ROT_DIM // 2
    inv = ROPE_THETA ** (-jnp.arange(half, dtype=jnp.float32) / half)
    ang = pos.astype(jnp.float32)[:, None] * inv
    shp = (pos.shape[0],) + (1,) * (x.ndim - 3) + (half,)
    cos, sin = jnp.cos(ang).reshape(shp), jnp.sin(ang).reshape(shp)
    xf = x.astype(jnp.float32)
    x1, x2, rest = xf[..., :half], xf[..., half:ROT_DIM], xf[..., ROT_DIM:]
    return jnp.concatenate([x1 * cos - x2 * sin, x2 * cos + x1 * sin, rest], axis=-1).astype(x.dtype)


def attend(q, k, v, mask):
    s = jnp.einsum('...qhgd,...khd->...hgqk', q, k).astype(jnp.float32) * (HEAD_DIM ** -0.5)
    p = jax.nn.softmax(jnp.where(mask, s, NEG), axis=-1) * mask
    o = jnp.einsum('...hgqk,...khd->...qhgd', p.astype(v.dtype), v)
    return o, p


def compress(x, pe, w1, w2):
    T = x.shape[1]
    n_cmp = (T - CMP_BLOCK) // CMP_STRIDE + 1
    idx = np.arange(n_cmp)[:, None] * CMP_STRIDE + np.arange(CMP_BLOCK)[None, :]
    blocks = x[:, idx] + pe[:, None, :]
    hid = jax.nn.gelu(jnp.einsum('ncjhd,jde->nche', blocks, w1))
    return jnp.einsum('nche,ef->nchf', hid, w2)


def select_blocks(p_cmp, n_sel, qpos):
    n_cmp = p_cmp.shape[-1]
    c0 = np.arange(n_cmp) * CMP_STRIDE
    j0 = np.arange(n_sel) * SEL_BLOCK
    overlap = (c0[:, None] < j0[None, :] + SEL_BLOCK) & (c0[:, None] + CMP_BLOCK > j0[None, :])
    imp = jnp.einsum('nhgqc,cj->nhqj', p_cmp, jnp.asarray(overlap, jnp.float32))
    cur = qpos // SEL_BLOCK
    jj = jnp.arange(n_sel)
    valid = jj[None, :] <= cur[:, None]
    forced = (jj[None, :] == 0) | (jj[None, :] == cur[:, None]) | (jj[None, :] == cur[:, None] - 1)
    score = jnp.where(valid, imp, -SEL_BONUS) + jnp.where(forced, SEL_BONUS, 0.0)
    _, idx = lax.top_k(score, min(TOP_N, n_sel))
    return idx


def selected_attend(q, kb, vb, idx, qpos):
    gather = jax.vmap(jax.vmap(lambda blocks, ix: blocks[ix]))
    kg, vg = gather(kb, idx), gather(vb, idx)
    kpos = idx[..., None] * SEL_BLOCK + jnp.arange(SEL_BLOCK)
    mask = (kpos <= qpos[:, None, None])[:, :, None]
    s = jnp.einsum('nqhgd,nhqksd->nhgqks', q, kg).astype(jnp.float32) * (HEAD_DIM ** -0.5)
    s = jnp.where(mask, s, NEG)
    p = jax.nn.softmax(s.reshape(s.shape[:-2] + (-1,)), axis=-1).reshape(s.shape) * mask
    return jnp.einsum('nhgqks,nhqksd->nqhgd', p.astype(vg.dtype), vg)


def nsa_global(q, q_rot, kv_all, qpos, lw, q_chunk):
    N, T = kv_all.shape[:2]
    kc = compress(kv_all[:, :, 0], lw['pe_ck'], lw['w_ck1'], lw['w_ck2'])
    vc = compress(kv_all[:, :, 1], lw['pe_cv'], lw['w_cv1'], lw['w_cv2'])
    cmp_end = jnp.arange(kc.shape[1]) * CMP_STRIDE + CMP_BLOCK - 1
    o_cmp, p_cmp = attend(q, kc, vc, cmp_end[None, :] <= qpos[:, None])
    n_sel = -(-T // SEL_BLOCK)
    idx = select_blocks(p_cmp, n_sel, qpos)
    pad = n_sel * SEL_BLOCK - T

    def to_blocks(t):
        tp = jnp.pad(t, ((0, 0), (0, pad), (0, 0), (0, 0)))
        return tp.reshape(N, n_sel, SEL_BLOCK, N_KV_HEADS, HEAD_DIM).transpose(0, 3, 1, 2, 4)

    kb, vb = to_blocks(kv_all[:, :, 2]), to_blocks(kv_all[:, :, 3])
    Q = q.shape[1]
    if q_chunk is None or q_chunk >= Q:
        o_slc = selected_attend(q_rot, kb, vb, idx, qpos)
    else:
        nc = Q // q_chunk
        qs = jnp.moveaxis(q_rot.reshape((N, nc, q_chunk) + q_rot.shape[2:]), 1, 0)
        ids = jnp.moveaxis(idx.reshape(N, N_KV_HEADS, nc, q_chunk, idx.shape[-1]), 2, 0)
        ps = qpos.reshape(nc, q_chunk)
        o = lax.map(lambda a: selected_attend(a[0], kb, vb, a[1], a[2]), (qs, ids, ps))
        o_slc = jnp.moveaxis(o, 0, 1).reshape(q_rot.shape)
    return o_cmp, o_slc


def window_banded(q, k, v):
    N, S = q.shape[:2]
    nb, npv = S // WIN_QBLK, WINDOW // WIN_QBLK

    def band(t):
        tp = jnp.pad(t, ((0, 0), (WINDOW, 0), (0, 0), (0, 0)))
        tp = tp.reshape((N, nb + npv, WIN_QBLK) + t.shape[2:])
        return jnp.concatenate([tp[:, i:i + nb] for i in range(npv + 1)], axis=2)

    qb = q.reshape((N, nb, WIN_QBLK) + q.shape[2:])
    qpos = np.arange(nb)[:, None] * WIN_QBLK + np.arange(WIN_QBLK)[None, :]
    kpos = np.arange(nb)[:, None] * WIN_QBLK - WINDOW + np.arange((npv + 1) * WIN_QBLK)[None, :]
    qp, kp = qpos[:, :, None], kpos[:, None, :]
    mask = (kp <= qp) & (kp > qp - WINDOW) & (kp >= 0)
    o, _ = attend(qb, band(k), band(v), jnp.asarray(mask)[None, :, None, None])
    return o.reshape(q.shape)


def window_cached(q_rot, win_buf, win_new, qpos):
    Wc, Q = win_buf.shape[1], win_new.shape[1]
    rows = jnp.concatenate([win_buf, win_new], axis=1)
    kpos = PAST_LEN - Wc + jnp.arange(Wc + Q)
    mask = (kpos[None, :] <= qpos[:, None]) & (kpos[None, :] > qpos[:, None] - WINDOW)
    o, _ = attend(q_rot, rows[:, :, 0], rows[:, :, 1], mask)
    return o, rows[:, rows.shape[1] - Wc:]


def rglru(xr, conv_prev, h0, lw):
    N, T, _ = xr.shape
    xp = jnp.concatenate([conv_prev, xr], axis=1)
    w = lw['conv_w']
    xc = lw['conv_b'] + w[0] * xp[:, 0:T]
    for tap in range(1, CONV_WIDTH):
        xc = xc + w[tap] * xp[:, tap:tap + T]
    xb = xc.reshape(N, T, RNN_BLOCKS, RNN_BLOCK)
    r = jax.nn.sigmoid(jnp.einsum('ntbi,bij->ntbj', xb, lw['w_ra']).reshape(N, T, D_RNN) + lw['b_ra'])
    i = jax.nn.sigmoid(jnp.einsum('ntbi,bij->ntbj', xb, lw['w_ri']).reshape(N, T, D_RNN) + lw['b_ri'])
    log_a = -LRU_C * r.astype(jnp.float32) * jax.nn.softplus(-lw['lam'].astype(jnp.float32))
    a = jnp.exp(log_a)
    b = jnp.sqrt(-jnp.expm1(2.0 * log_a)) * (i * xc).astype(jnp.float32)
    b = b.at[:, 0].add(a[:, 0] * h0.astype(jnp.float32))
    _, h = lax.associative_scan(lambda l, rr: (l[0] * rr[0], rr[0] * l[1] + rr[1]), (a, b), axis=1)
    return h.astype(xr.dtype), xp[:, xp.shape[1] - (CONV_WIDTH - 1):], h[:, -1].astype(xr.dtype)


def project(u, w_in, pos):
    N, T, _ = u.shape
    z = u @ w_in
    q, kc, vc, ks, vs, kw, vw, gates, rg, rx = jnp.split(z, SPLITS, axis=-1)
    q = q.reshape(N, T, N_KV_HEADS, GQA, HEAD_DIM)
    kvr = lambda t: t.reshape(N, T, N_KV_HEADS, HEAD_DIM)
    kv_rows = jnp.stack([kvr(kc), kvr(vc), rotary(kvr(ks), pos), kvr(vs)], axis=2)
    win_rows = jnp.stack([rotary(kvr(kw), pos), kvr(vw)], axis=2)
    gates = jax.nn.sigmoid(gates).reshape(N, T, 3, N_KV_HEADS, GQA)
    return q, rotary(q, pos), kv_rows, win_rows, gates, rg, rx


def layer_step(x, pos, lw, kv_past=None, win_buf=None, conv_prev=None, h0=None):
    N, T, _ = x.shape
    u = rmsnorm(x, lw['norm_mix'])
    q, q_rot, kv_rows, win_rows, gates, rg, rx = project(u, lw['w_in'], pos)
    if kv_past is None:
        kv_all = kv_rows
        o_win = window_banded(q_rot, win_rows[:, :, 0], win_rows[:, :, 1])
        win_state = win_rows[:, T - min(WINDOW, T):]
        conv_prev = jnp.zeros((N, CONV_WIDTH - 1, D_RNN), x.dtype)
        h0 = jnp.zeros((N, D_RNN), x.dtype)
        q_chunk = SLC_QBLK
    else:
        kv_all = jnp.concatenate([kv_past, kv_rows], axis=1)
        o_win, win_state = window_cached(q_rot, win_buf, win_rows, pos)
        q_chunk = None
    o_cmp, o_slc = nsa_global(q, q_rot, kv_all, pos, lw, q_chunk)
    o_attn = (gates[:, :, 0, :, :, None] * o_cmp + gates[:, :, 1, :, :, None] * o_slc
              + gates[:, :, 2, :, :, None] * o_win).reshape(N, T, ATTN_WIDTH)
    h_rnn, conv_state, h_last = rglru(rx, conv_prev, h0, lw)
    mixed = jnp.concatenate([rmsnorm(o_attn, lw['g_attn']),
                             rmsnorm(jax.nn.gelu(rg) * h_rnn, lw['g_rnn'])], axis=-1)
    x = x + mixed @ lw['w_out']
    v = rmsnorm(x, lw['norm_mlp'])
    x = x + jnp.square(jax.nn.relu(v @ lw['w_up'])) @ lw['w_down']
    return x, kv_rows, win_state, conv_state, h_last


def setup_inputs(seed: int = 0) -> dict:
    key = jax.random.key(seed)
    ks = jax.random.split(key, 32)
    f32 = jnp.float32
    nrm = lambda k, shp, s: jax.random.normal(k, shp, f32) * s
    n_pages = PAST_LEN // PAGE_SIZE
    n_used = DEC_BATCH * n_pages
    n_pool = n_used + (n_used + 3) // 4
    perm = jax.random.permutation(ks[6], n_pool)
    page_table = perm[:n_used].reshape(DEC_BATCH, n_pages).astype(jnp.int32)
    a_base = jax.random.uniform(ks[22], (DEPTH, D_RNN), f32, 0.9 ** (1.0 / LRU_C), 0.999 ** (1.0 / LRU_C))
    return {
        'x_prompt': nrm(ks[0], (BATCH, SEQ, D_MODEL), 1.0),
        'x_sample': nrm(ks[1], (DEC_BATCH, DEC_SEQ, D_MODEL), 1.0),
        'cache_kv': nrm(ks[2], (DEPTH, n_pool, PAGE_SIZE, N_KV_SLOTS, N_KV_HEADS, HEAD_DIM), 1.0),
        'cache_win': nrm(ks[3], (DEPTH, DEC_BATCH, min(WINDOW, PAST_LEN), 2, N_KV_HEADS, HEAD_DIM), 1.0),
        'state_conv': nrm(ks[4], (DEPTH, DEC_BATCH, CONV_WIDTH - 1, D_RNN), 1.0),
        'state_rnn': nrm(ks[5], (DEPTH, DEC_BATCH, D_RNN), 0.5),
        'page_table': page_table,
        'w_in': nrm(ks[7], (DEPTH, D_MODEL, IN_COLS), D_MODEL ** -0.5),
        'pe_ck': nrm(ks[8], (DEPTH, CMP_BLOCK, HEAD_DIM), 0.5),
        'w_ck1': nrm(ks[9], (DEPTH, CMP_BLOCK, HEAD_DIM, HEAD_DIM), (CMP_BLOCK * HEAD_DIM) ** -0.5),
        'w_ck2': nrm(ks[10], (DEPTH, HEAD_DIM, HEAD_DIM), HEAD_DIM ** -0.5),
        'pe_cv': nrm(ks[11], (DEPTH, CMP_BLOCK, HEAD_DIM), 0.5),
        'w_cv1': nrm(ks[12], (DEPTH, CMP_BLOCK, HEAD_DIM, HEAD_DIM), (CMP_BLOCK * HEAD_DIM) ** -0.5),
        'w_cv2': nrm(ks[13], (DEPTH, HEAD_DIM, HEAD_DIM), HEAD_DIM ** -0.5),
        'g_attn': 1.0 + nrm(ks[14], (DEPTH, ATTN_WIDTH), 0.1),
        'g_rnn': 1.0 + nrm(ks[15], (DEPTH, D_RNN), 0.1),
        'conv_w': nrm(ks[16], (DEPTH, CONV_WIDTH, D_RNN), CONV_WIDTH ** -0.5),
        'conv_b': nrm(ks[17], (DEPTH, D_RNN), 0.01),
        'w_ra': nrm(ks[18], (DEPTH, RNN_BLOCKS, RNN_BLOCK, RNN_BLOCK), RNN_BLOCK ** -0.5),
        'b_ra': nrm(ks[19], (DEPTH, D_RNN), 0.01),
        'w_ri': nrm(ks[20], (DEPTH, RNN_BLOCKS, RNN_BLOCK, RNN_BLOCK), RNN_BLOCK ** -0.5),
        'b_ri': nrm(ks[21], (DEPTH, D_RNN), 0.01),
        'lam': jnp.log(a_base / (1.0 - a_base)),
        'w_out': nrm(ks[23], (DEPTH, D_MODEL, D_MODEL), D_MODEL ** -0.5),
        'norm_mix': 1.0 + nrm(ks[24], (DEPTH, D_MODEL), 0.1),
        'norm_mlp': 1.0 + nrm(ks[25], (DEPTH, D_MODEL), 0.1),
        'w_up': nrm(ks[26], (DEPTH, D_MODEL, D_FF), D_MODEL ** -0.5),
        'w_down': nrm(ks[27], (DEPTH, D_FF, D_MODEL), D_FF ** -0.5),
        'norm_final': 1.0 + nrm(ks[28], (D_MODEL,), 0.1),
    }


def reference(x_prompt, x_sample, cache_kv, cache_win, state_conv, state_rnn, page_table,
              w_in, pe_ck, w_ck1, w_ck2, pe_cv, w_cv1, w_cv2, g_attn, g_rnn,
              conv_w, conv_b, w_ra, b_ra, w_ri, b_ri, lam, w_out,
              norm_mix, norm_mlp, w_up, w_down, norm_final):
    pos_p = jnp.arange(x_prompt.shape[1], dtype=jnp.int32)
    pos_s = PAST_LEN + jnp.arange(x_sample.shape[1], dtype=jnp.int32)
    n_db = x_sample.shape[0]
    yp, ys = x_prompt, x_sample
    kv_p, kv_s, win_p, win_s, conv_p, conv_s, h_p, h_s = [], [], [], [], [], [], [], []
    for l in range(DEPTH):
        lw = {'w_in': w_in[l], 'pe_ck': pe_ck[l], 'w_ck1': w_ck1[l], 'w_ck2': w_ck2[l],
              'pe_cv': pe_cv[l], 'w_cv1': w_cv1[l], 'w_cv2': w_cv2[l], 'g_attn': g_attn[l], 'g_rnn': g_rnn[l],
              'conv_w': conv_w[l], 'conv_b': conv_b[l], 'w_ra': w_ra[l], 'b_ra': b_ra[l],
              'w_ri': w_ri[l], 'b_ri': b_ri[l], 'lam': lam[l], 'w_out': w_out[l],
              'norm_mix': norm_mix[l], 'norm_mlp': norm_mlp[l], 'w_up': w_up[l], 'w_down': w_down[l]}
        yp, a, b, c, d = layer_step(yp, pos_p, lw)
        kv_p.append(a); win_p.append(b); conv_p.append(c); h_p.append(d)
        past = cache_kv[l][page_table].reshape(n_db, -1, N_KV_SLOTS, N_KV_HEADS, HEAD_DIM)
        ys, a, b, c, d = layer_step(ys, pos_s, lw, past, cache_win[l], state_conv[l], state_rnn[l])
        kv_s.append(a); win_s.append(b); conv_s.append(c); h_s.append(d)
    y_prompt = rmsnorm(yp, norm_final)
    y_sample = rmsnorm(ys, norm_final)
    return (y_prompt, y_sample, jnp.stack(kv_p), jnp.stack(kv_s), jnp.stack(win_p), jnp.stack(win_s),
            jnp.stack(conv_p), jnp.stack(conv_s), jnp.stack(h_p), jnp.stack(h_s))
```

```python
import contextlib
import numpy as np
import concourse.bass as bass
import concourse.mybir as mybir
from concourse.bass_utils import run_bass_kernel_spmd

F32 = mybir.dt.float32
BF16 = mybir.dt.bfloat16
I32 = mybir.dt.int32
U32 = mybir.dt.uint32
AF = mybir.ActivationFunctionType
ALU = mybir.AluOpType
AX = mybir.AxisListType


class Buf:
    __slots__ = ("name", "last_w", "readers", "excl")

    def __init__(self, name, excl=False):
        self.name = name
        self.last_w = None
        self.readers = []
        self.excl = excl


class Prog:
    ENGS = ("pe", "act", "dve", "pool", "sp")
    NDMA = {"sp": 24, "act": 12, "pool": 24}

    def __init__(self, nc, stack):
        self.nc = nc
        self.stack = stack
        self.stream = {e: [] for e in self.ENGS}
        self.cnt = {e: 0 for e in self.ENGS}
        self.waited = {e: {} for e in self.ENGS}
        self.sems = {}
        for e in self.ENGS:
            self.sems[e] = stack.enter_context(nc.semaphore("s_" + e))
        self.dslots = {}
        self.dnext = {}
        for q, n in self.NDMA.items():
            self.dslots[q] = []
            for i in range(n):
                k = "d_%s_%d" % (q, i)
                self.sems[k] = stack.enter_context(nc.semaphore(k))
                self.dslots[q].append([k, 0])
            self.dnext[q] = 0
        self.nbuf = 0

    def buf(self, name=None, excl=False):
        self.nbuf += 1
        return Buf(name or ("b%d" % self.nbuf), excl)

    def _deps(self, e, reads, writes):
        deps = set()
        for b in reads:
            if b.last_w is not None:
                deps.add(b.last_w)
            if b.excl:
                for r in b.readers:
                    if r[0] != e:
                        deps.add(r)
        for b in writes:
            if b.last_w is not None:
                deps.add(b.last_w)
            for r in b.readers:
                deps.add(r)
        waits = []
        best = {}
        for (sk, v) in deps:
            if sk == e and e == "pe":
                continue
            if v > best.get(sk, 0):
                best[sk] = v
        for sk, v in best.items():
            if self.waited[e].get(sk, 0) >= v:
                continue
            self.waited[e][sk] = v
            waits.append((sk, v))
        return waits

    def _commit(self, tok, reads, writes):
        for b in reads:
            b.readers.append(tok)
        for b in writes:
            b.last_w = tok
            b.readers = []

    def op(self, e, fn, reads=(), writes=()):
        reads = [b for b in reads if b is not None]
        writes = [b for b in writes if b is not None]
        waits = self._deps(e, reads, writes)
        self.cnt[e] += 1
        tok = (e, self.cnt[e])
        self.stream[e].append((waits, fn, e, 1))
        self._commit(tok, reads, writes)
        return tok

    def dma(self, q, fn, reads=(), writes=()):
        reads = [b for b in reads if b is not None]
        writes = [b for b in writes if b is not None]
        waits = self._deps(q, reads, writes)
        slots = self.dslots[q]
        i = self.dnext[q]
        self.dnext[q] = (i + 1) % len(slots)
        sk, tot = slots[i]
        if tot > 0 and self.waited[q].get(sk, 0) < tot:
            self.waited[q][sk] = tot
            waits.append((sk, tot))
        slots[i][1] = tot + 16
        tok = (sk, tot + 16)
        self.stream[q].append((waits, fn, sk, 16))
        self._commit(tok, reads, writes)
        return tok

    def finish(self):
        for q in self.NDMA:
            waits = []
            for sk, tot in self.dslots[q]:
                if tot > 0 and self.waited[q].get(sk, 0) < tot:
                    self.waited[q][sk] = tot
                    waits.append((sk, tot))
            if waits:
                self.stream[q].append((waits, None, None, 0))

    def barrier(self):
        toks = [(e, self.cnt[e]) for e in self.ENGS if self.cnt[e] > 0]
        for q in self.NDMA:
            for sk, tot in self.dslots[q]:
                if tot > 0:
                    toks.append((sk, tot))
        for e in self.ENGS:
            waits = []
            for sk, v in toks:
                if sk == e:
                    continue
                if self.waited[e].get(sk, 0) >= v:
                    continue
                self.waited[e][sk] = v
                waits.append((sk, v))
            if waits:
                self.stream[e].append((waits, None, None, 0))

    def emit(self):
        nc = self.nc
        P = self
        with nc.Block() as block:
            def run(e, eng):
                for waits, fn, sk, inc in P.stream[e]:
                    for (wk, wv) in waits:
                        eng.wait_ge(P.sems[wk], wv)
                    if fn is None:
                        continue
                    ins = fn(eng)
                    ins.then_inc(P.sems[sk], inc)

            @block.tensor
            def _(eng):
                run("pe", eng)

            @block.scalar
            def _(eng):
                run("act", eng)

            @block.vector
            def _(eng):
                run("dve", eng)

            @block.gpsimd
            def _(eng):
                run("pool", eng)

            @block.sync
            def _(eng):
                run("sp", eng)
        self.stream = {e: [] for e in self.ENGS}


EPS = 1e-6
NS = 16
ST = 4
CH = 256
NCHUNK = 8
NT = 2048
NTOK = NT + NS * ST
SCALE = 0.125
BIGB = 240000.0


def build_nc(stage=99):
    nc = bass.Bass("TRN2", target_bir_lowering=False)

    used_inputs = []

    def din(name, shape, dt=F32):
        used_inputs.append(name)
        return nc.dram_tensor(name, shape, dt, kind="ExternalInput").ap()

    def dout(name, shape):
        return nc.dram_tensor(name, shape, F32, kind="ExternalOutput").ap()

    xp_d = din("xp", [NT, 1024]); xs_d = din("xs", [64, 1024])
    if stage >= 3:
        ckv_d = din("ckv", [2560 * 128, 512]); cwin_d = din("cwin", [NS, 512, 256])
    sconv_d = din("sconv", [48, 512]); srnn_d = din("srnn", [NS, 512])
    ptab_d = din("ptab", [1, 256], I32)
    wtm_d = din("wtm", [1024, 1304]); wfm_d = din("wfm", [1024, 1024])
    wout_d = din("wout", [1024, 1024]); wup_d = din("wup", [1024, 4096]); wdown_d = din("wdown", [4096, 1024])
    w1k_d = din("w1k", [128, 32, 128]); w1v_d = din("w1v", [128, 32, 128])
    w2k_d = din("w2k", [128, 2, 64]); w2v_d = din("w2v", [128, 2, 64])
    pek_d = din("pek", [128, 32]); pev_d = din("pev", [128, 32])
    wra_d = din("wra", [128, 4, 128]); wri_d = din("wri", [128, 4, 128])
    vecs_d = din("vecs", [128, 56]); nfin_d = din("nfin", [1, 1024])
    identf_d = din("identf", [128, 128]); trile_d = din("trile", [128, 128]); trigt_d = din("trigt", [128, 128])
    cm_d = din("cm", [128, NT]); ec_d = din("ec", [33, 2056]); ov_d = din("ov", [128, 33])
    ropep_d = din("ropep", [128, 16, 16]); ropes_d = din("ropes", [64, 16])
    selvp_d = din("selvp", [128, 16, 33]); selcp_d = din("selcp", [128, 16, 33])
    selvs_d = din("selvs", [4, 33]); selcs_d = din("selcs", [4, 33])
    iota_d = din("iotap", [128, 1])

    yp_d = dout("y_p", [NT, 1024]); ys_d = dout("y_s", [64, 1024])
    kvp_d = dout("kv_p", [NT, 512]); kvs_d = dout("kv_s", [64, 512])
    winp_d = dout("win_p", [512, 256]); wins_d = dout("win_s", [NS, 512, 256])
    convp_d = dout("conv_p", [3, 512]); convs_d = dout("conv_s", [48, 512])
    hp_d = dout("h_p", [4, 128]); hs_d = dout("h_s", [NS, 512])

    with contextlib.ExitStack() as glob:
        P = Prog(nc, glob)

        def sbt(stack, name, shape, dt):
            return stack.enter_context(nc.sbuf_tensor("s_" + name, shape, dt))

        banks = [glob.enter_context(nc.psum_tensor("pb%d" % i, [128, 512], F32)) for i in range(8)]
        bbank = [P.buf("bank%d" % i, excl=True) for i in range(8)]
        apool = [0]

        def abank():
            i = apool[0]
            apool[0] = (i + 1) % 2
            return banks[i], bbank[i]

        def MM(out, lhsT, rhs, start, stop, R, W):
            P.op("pe", lambda e: e.matmul(out, lhsT=lhsT, rhs=rhs, start=start, stop=stop,
                                          skip_group_check=True), R, W)

        def TR(out, in_, ident, R, W):
            P.op("pe", lambda e: e.transpose(out=out, in_=in_, identity=ident), R, W)

        def ACT(out, in_, func, R, W, bias=None, scale=1.0, accum=None):
            kw = {}
            if bias is not None:
                kw["bias"] = bias
            if accum is not None:
                kw["accum_out"] = accum
            P.op("act", lambda e: e.activation(out=out, in_=in_, func=func, scale=scale, **kw), R, W)

        def TS(eng, out, in0, s1, s2, op0, op1, R, W):
            if s2 is None:
                P.op(eng, lambda e: e.tensor_scalar(out=out, in0=in0, scalar1=s1, scalar2=None, op0=op0), R, W)
            else:
                P.op(eng, lambda e: e.tensor_scalar(out=out, in0=in0, scalar1=s1, scalar2=s2, op0=op0, op1=op1), R, W)

        def TT(eng, out, in0, in1, op, R, W):
            P.op(eng, lambda e: e.tensor_tensor(out=out, in0=in0, in1=in1, op=op), R, W)

        def STT(out, in0, scalar, in1, op0, op1, R, W, accum=None):
            kw = {} if accum is None else {"accum_out": accum}
            P.op("dve", lambda e: e.scalar_tensor_tensor(out=out, in0=in0, scalar=scalar, in1=in1,
                                                         op0=op0, op1=op1, **kw), R, W)

        def CP(eng, out, in_, R, W):
            if eng == "act":
                P.op("act", lambda e: e.copy(out=out, in_=in_), R, W)
            else:
                P.op(eng, lambda e: e.tensor_copy(out=out, in_=in_), R, W)

        def MEMSET(eng, ap, val, W):
            P.op(eng, lambda e: e.memset(ap, val), (), W)

        def DMA(q, out, in_, R, W, **kw):
            if q == "pool" and out.dtype != in_.dtype:
                kw.setdefault("max_dma_last_dim", 2048)
            P.dma(q, lambda e: e.dma_start(out=out, in_=in_, **kw), R, W)

        ident_b = sbt(glob, "ident_b", [128, 128], BF16); b_identb = P.buf()
        ident_f = sbt(glob, "ident_f", [128, 128], F32); b_identf = P.buf()
        ones_f = sbt(glob, "ones_f", [128, 128], F32); b_ones = P.buf()
        tri_le = sbt(glob, "tri_le", [128, 128], BF16); b_trile = P.buf()
        tri_gt = sbt(glob, "tri_gt", [128, 128], BF16); b_trigt = P.buf()
        vecs = sbt(glob, "vecs", [128, 56], F32); b_vecs = P.buf()
        nls = sbt(glob, "nls", [128, 4], F32); b_nls = P.buf()
        mixedT = sbt(glob, "mixedT", [128, 8, NTOK], BF16)
        b_mix = [P.buf("mix%d" % i) for i in range(NCHUNK + 1)]

        DMA("sp", ident_f[:], identf_d, (), [b_identf])
        DMA("pool", ident_b[:], identf_d, (), [b_identb])
        DMA("pool", tri_le[:], trile_d, (), [b_trile])
        DMA("pool", tri_gt[:], trigt_d, (), [b_trigt])
        DMA("sp", vecs[:], vecs_d, (), [b_vecs])
        MEMSET("pool", ones_f[:], 1.0, [b_ones])
        ACT(nls[:], vecs[:, 36:40], AF.Exp, [b_vecs], [b_nls], scale=-1.0)
        ACT(nls[:], nls[:], AF.Ln, [b_nls], [b_nls], bias=1.0)
        TS("dve", nls[:], nls[:], -8.0, None, ALU.mult, None, [b_nls], [b_nls])

        V_NMIX = vecs[:, 0:8]; V_NMLP = vecs[:, 8:16]; V_GATT = vecs[:, 16:20]

        def vcol(base, c):
            return vecs[:, base + c:base + c + 1]

        with contextlib.ExitStack() as ls:
            Wtm = sbt(ls, "Wtm", [128, 8, 1304], BF16); b_wtm = [P.buf() for _ in range(8)]
            Wfm = sbt(ls, "Wfm", [128, 8, 1024], BF16); b_wfm = [P.buf() for _ in range(8)]
            for kc in range(8):
                DMA("pool", Wtm[:, kc, :], wtm_d[kc * 128:(kc + 1) * 128, :], (), [b_wtm[kc]])
            for kc in range(8):
                DMA("pool", Wfm[:, kc, :], wfm_d[kc * 128:(kc + 1) * 128, :], (), [b_wfm[kc]])
            W1 = [sbt(ls, "W1k", [128, 32, 128], BF16), sbt(ls, "W1v", [128, 32, 128], BF16)]
            b_w1 = [P.buf(), P.buf()]
            W2 = [sbt(ls, "W2k", [128, 2, 64], BF16), sbt(ls, "W2v", [128, 2, 64], BF16)]
            b_w2 = [P.buf(), P.buf()]
            pe2 = [sbt(ls, "pek2", [128, 32], BF16), sbt(ls, "pev2", [128, 32], BF16)]
            b_pe2 = [P.buf(), P.buf()]
            pec = sbt(ls, "pec", [128, 2], F32); b_pec = P.buf()
            wra = sbt(ls, "wra", [128, 4, 128], BF16); b_wra = P.buf()
            wri = sbt(ls, "wri", [128, 4, 128], BF16); b_wri = P.buf()
            DMA("pool", W1[0][:], w1k_d, (), [b_w1[0]]); DMA("pool", W1[1][:], w1v_d, (), [b_w1[1]])
            DMA("pool", W2[0][:], w2k_d, (), [b_w2[0]]); DMA("pool", W2[1][:], w2v_d, (), [b_w2[1]])
            DMA("pool", pe2[0][:], pek_d, (), [b_pe2[0]]); DMA("pool", pe2[1][:], pev_d, (), [b_pe2[1]])
            DMA("pool", wra[:], wra_d, (), [b_wra]); DMA("pool", wri[:], wri_d, (), [b_wri])

            KS = sbt(ls, "KS", [128, 2, 2056], BF16)
            b_ks = [[P.buf() for _ in range(17)] for _ in range(2)]; b_ksE = P.buf()
            KWh = [None]; b_kw = [[P.buf() for _ in range(16)] for _ in range(2)]
            Vs = sbt(ls, "Vs", [128, 17, 2, 65], BF16); b_vs = [P.buf() for _ in range(17)]
            Vwh = [None]; b_vw = [P.buf() for _ in range(16)]
            kvcT = sbt(ls, "kvcT", [128, 2, NT], BF16); b_kvc = [P.buf() for _ in range(16)]
            hid = [sbt(ls, "hidk", [128, 128], BF16), sbt(ls, "hidv", [128, 128], BF16)]
            b_hid = [P.buf(), P.buf()]
            KcT = sbt(ls, "KcT", [64, 2, 128], BF16); b_kct = P.buf()
            Vc = sbt(ls, "Vc", [128, 2, 98], BF16); b_vc = P.buf(); b_vcc = P.buf()
            for kv in range(2):
                DMA("pool", KS[64:97, kv, :], ec_d, (), [b_ksE])
            MEMSET("pool", Vs[:, :, :, 64:65], 1.0, [b_vs[0]])
            MEMSET("pool", Vc[:, :, 64:65], 1.0, [b_vcc])
            for kv in range(2):
                DMA("pool", Vc[:, kv, 65:98], ov_d, (), [b_vcc])
            MEMSET("pool", hid[0][:], 0.0, [b_hid[0]]); MEMSET("pool", hid[1][:], 0.0, [b_hid[1]])

            for w in range(2):
                bk, bb = abank()
                for j in range(32):
                    MM(bk[:, 0:1], W1[w][:, j, :], pe2[w][:, j:j + 1], j == 0, j == 31, [b_w1[w], b_pe2[w]], [bb])
                CP("dve", pec[:, w:w + 1], bk[:, 0:1], [bb], [b_pec])

            xin = [sbt(ls, "xin0", [128, 1024], F32)]; b_xin = [P.buf(), P.buf()]
            xnb = sbt(ls, "xnb", [128, 1024], BF16); b_xnb = P.buf()
            junk = sbt(ls, "junk", [128, 1024], BF16); b_junk = P.buf()
            st4 = sbt(ls, "st4", [128, 8], F32); b_st4 = P.buf()
            uT = sbt(ls, "uT", [128, 8, CH], BF16); b_uT = [P.buf(), P.buf()]
            zsb = sbt(ls, "zsb", [128, 1280], F32); b_zsb = P.buf()
            gt = [sbt(ls, "gt%d" % i, [128, 24], F32) for i in range(2)]; b_gt = [P.buf(), P.buf()]
            rt = sbt(ls, "rt", [128, 4, 96], F32); b_rt = P.buf()
            tb = sbt(ls, "tb", [128, 1024], BF16); b_tb = P.buf()
            tbq = sbt(ls, "tbq", [128, 512], BF16); b_tbq = P.buf()
            QS = sbt(ls, "QS", [128, 2, 4, CH], BF16); b_qs = [P.buf(), P.buf()]
            QC = sbt(ls, "QC", [64, 2, 4, CH], BF16); b_qc = [P.buf(), P.buf()]
            xc = sbt(ls, "xc", [128, CH], F32); b_xc = P.buf()
            xcb = sbt(ls, "xcb", [128, CH], BF16); b_xcb = P.buf()
            rr = sbt(ls, "rr", [128, CH], F32); b_rr = P.buf()
            ig = sbt(ls, "ig", [128, CH], F32); b_ig = P.buf()
            aa = sbt(ls, "aa", [128, CH], F32); b_aa = P.buf()
            asc = sbt(ls, "asc", [128, CH], F32); b_asc = P.buf()
            t1 = sbt(ls, "t1", [128, CH], F32); b_t1 = P.buf()
            bb_ = sbt(ls, "bb_", [128, CH], F32); b_bb = P.buf()
            hh = sbt(ls, "hh", [128, CH], F32); b_hh = P.buf()
            gl = sbt(ls, "gl", [128, CH], F32); b_gl = P.buf()
            yy = sbt(ls, "yy", [128, 4, CH], F32); b_yy = [P.buf() for _ in range(4)]
            ysq = sbt(ls, "ysq", [128, CH], F32); b_ysq = P.buf()
            rsd = sbt(ls, "rsd", [128, CH], F32); b_rsd = P.buf()
            hcar = sbt(ls, "hcar", [128, 4], F32); b_hcar = P.buf()
            MEMSET("pool", hcar[:], 0.0, [b_hcar])
            NPB = 4
            Pb = [sbt(ls, "Pb%d" % i, [128, 512], BF16) for i in range(NPB)]; b_pb = [P.buf() for _ in range(NPB)]
            pbi = [0]
            occ = sbt(ls, "occ", [128, 4, 98], F32); b_occ = P.buf()
            oall = sbt(ls, "oall", [128, 4, 4, 65], F32); b_oall = [P.buf() for _ in range(4)]
            stg = sbt(ls, "stg", [16, 4, 65], F32); b_stg = P.buf()
            sm = sbt(ls, "sm", [128, 64], F32); b_sm = P.buf()
            impt = sbt(ls, "impt", [128, 4, 33], F32); b_impt = P.buf()
            sc = sbt(ls, "sc", [128, 33], F32); b_sc = P.buf()
            sc2 = sbt(ls, "sc2", [128, 33], F32); b_sc2 = P.buf()
            m8 = sbt(ls, "m8", [128, 16], F32); b_m8 = P.buf()
            sbp2 = [sbt(ls, "sbp%d" % i, [128, 97], F32) for i in range(2)]; b_sbp2 = [P.buf(), P.buf()]
            MEMSET("pool", sbp2[0][:], 0.0, [b_sbp2[0]]); MEMSET("pool", sbp2[1][:], 0.0, [b_sbp2[1]])
            oat = sbt(ls, "oat", [128, 512], F32); b_oat = P.buf()
            otm = sbt(ls, "otm", [128, 256], F32); b_otm = P.buf()
            oab2 = [sbt(ls, "oab%d" % i, [128, 512], BF16) for i in range(2)]; b_oab2 = [P.buf(), P.buf()]; oabi = [0]
            outst = sbt(ls, "outst", [128, 512], F32); b_outst = P.buf()

            def load_norm_T(src_ap, nt, xi, ucol0):
                xt, bx = xin[xi], b_xin[xi]
                DMA("sp", xt[0:nt, :], src_ap, (), [bx])
                ACT(junk[0:nt, :], xt[0:nt, :], AF.Square, [bx], [b_junk, b_st4], accum=st4[0:nt, 0:1])
                ACT(st4[0:nt, 1:2], st4[0:nt, 0:1], AF.Sqrt, [b_st4], [b_st4], bias=EPS, scale=1.0 / 1024)
                P.op("dve", lambda e: e.reciprocal(out=st4[0:nt, 2:3], in_=st4[0:nt, 1:2]), [b_st4], [b_st4])
                TS("dve", xnb[0:nt, :], xt[0:nt, :], st4[0:nt, 2:3], None, ALU.mult, None, [bx, b_st4], [b_xnb])
                bk, bbk = abank()
                pv = bk[:].bitcast(BF16).rearrange("p (c t) -> p c t", c=8)
                for c in range(8):
                    TR(pv[:, c, 0:nt], xnb[0:nt, c * 128:(c + 1) * 128], ident_b[0:nt, 0:nt], [b_xnb, b_identb], [bbk])
                ub = b_uT[(ucol0 // 128) % 2] if nt == 128 else None
                wr = [ub] if ub is not None else list(b_uT)
                TT("dve", uT[:, :, ucol0:ucol0 + nt], pv[:, :, 0:nt],
                   V_NMIX.unsqueeze(2).broadcast_to([128, 8, nt]), ALU.mult, [bbk, b_vecs], wr)
                return wr

            def proj_tile(nt, ucol0, ub, rope_ap, brope, gti, qcol0, kcol0, vtile, kvout, winout, do_kvc=True, knew=None):
                g = gt[gti]; bg = b_gt[gti]
                offs = [(0, 512), (512, 280), (792, 512)]
                pz = []
                for gi, (o, w) in enumerate(offs):
                    bk, bbk = banks[gi], bbank[gi]
                    for kc in range(8):
                        MM(bk[0:nt, 0:w], uT[:, kc, ucol0:ucol0 + nt], Wtm[:, kc, o:o + w], kc == 0, kc == 7,
                           ub + [b_wtm[kc]], [bbk])
                    pz.append((bk, bbk))
                CP("act", zsb[0:nt, 0:512], pz[0][0][0:nt, 0:512], [pz[0][1]], [b_zsb])
                CP("act", zsb[0:nt, 512:768], pz[1][0][0:nt, 0:256], [pz[1][1]], [b_zsb])
                ACT(g[0:nt, :], pz[1][0][0:nt, 256:280], AF.Sigmoid, [pz[1][1]], [bg])
                CP("dve", zsb[0:nt, 768:1280], pz[2][0][0:nt, 0:512], [pz[2][1]], [b_zsb])
                if stage < 0.215:
                    return None, None
                CP("act", tbq[0:nt, :], zsb[0:nt, 0:512], [b_zsb], [b_tbq])
                R = zsb[0:nt, 0:768].rearrange("p (h d) -> p h d", h=12)
                x1 = R[:, :, 0:8]; x2 = R[:, :, 8:16]
                cs = rope_ap[:, 0:8].unsqueeze(1).broadcast_to([nt, 12, 8])
                sn = rope_ap[:, 8:16].unsqueeze(1).broadcast_to([nt, 12, 8])
                rtv = [rt[0:nt, i, :].rearrange("p (h d) -> p h d", h=12) for i in range(4)]
                TT("dve", rtv[0], x1, cs, ALU.mult, [b_zsb, brope], [b_rt])
                TT("dve", rtv[1], x2, sn, ALU.mult, [b_zsb, brope], [b_rt])
                TT("dve", rtv[2], x2, cs, ALU.mult, [b_zsb, brope], [b_rt])
                TT("dve", rtv[3], x1, sn, ALU.mult, [b_zsb, brope], [b_rt])
                TT("dve", x1, rtv[0], rtv[1], ALU.subtract, [b_rt], [b_zsb])
                TT("dve", x2, rtv[2], rtv[3], ALU.add, [b_rt], [b_zsb])
                if stage < 0.225:
                    return None, None
                DMA("sp", kvout[:, 0:256], zsb[0:nt, 768:1024], [b_zsb], ())
                DMA("sp", kvout[:, 256:384], zsb[0:nt, 512:640], [b_zsb], ())
                DMA("sp", kvout[:, 384:512], zsb[0:nt, 1024:1152], [b_zsb], ())
                if winout is not None:
                    DMA("sp", winout[:, 0:128], zsb[0:nt, 640:768], [b_zsb], ())
                    DMA("sp", winout[:, 128:256], zsb[0:nt, 1152:1280], [b_zsb], ())
                if stage < 0.235:
                    return None, None
                CP("act", tb[0:nt, :], zsb[0:nt, 0:1024], [b_zsb], [b_tb])
                vsb = b_vs[vtile] if knew is None else None; vwb = b_vw[vtile] if (vtile < 16 and knew is None) else None
                if knew is None:
                    CP("pool", Vs[0:nt, vtile, :, 0:64], zsb[0:nt, 1024:1152].rearrange("p (h d) -> p h d", h=2),
                       [b_zsb], [vsb])
                if vwb is not None:
                    CP("pool", Vwh[0][0:nt, vtile, :, 0:64], zsb[0:nt, 1152:1280].rearrange("p (h d) -> p h d", h=2),
                       [b_zsb], [vwb])
                if stage < 0.245:
                    return None, None
                idn = ident_b[0:nt, 0:nt]
                bk, bbk = abank()
                pv = bk[:].bitcast(BF16)
                for h in range(8):
                    TR(pv[0:64, h * 128:h * 128 + nt], tb[0:nt, h * 64:(h + 1) * 64], idn, [b_tb, b_identb], [bbk])
                CP("act", QS[0:64, :, :, qcol0:qcol0 + nt],
                   pv[0:64, :].rearrange("p (k g t) -> p k g t", k=2, g=4)[:, :, :, 0:nt], [bbk], b_qs)
                bk, bbk = abank()
                pv = bk[:].bitcast(BF16)
                for h in range(8):
                    TR(pv[0:64, h * 128:h * 128 + nt], tbq[0:nt, h * 64:(h + 1) * 64], idn, [b_tbq, b_identb], [bbk])
                CP("dve", QC[0:64, :, :, qcol0:qcol0 + nt],
                   pv[0:64, :].rearrange("p (k g t) -> p k g t", k=2, g=4)[:, :, :, 0:nt], [bbk], b_qc)
                bk, bbk = abank()
                pv = bk[:].bitcast(BF16)
                for h in range(4):
                    TR(pv[0:64, h * 128:h * 128 + nt], tb[0:nt, 512 + h * 64:512 + (h + 1) * 64], idn,
                       [b_tb, b_identb], [bbk])
                if do_kvc:
                    for h in range(2):
                        TR(pv[0:128, (4 + h) * 128:(4 + h) * 128 + nt], tb[0:nt, 768 + h * 128:768 + (h + 1) * 128], idn,
                           [b_tb, b_identb], [bbk])
                kt = kcol0 // 128
                pvr = pv[:, :].rearrange("p (s t) -> p s t", s=8)
                if knew is not None:
                    KN_, KWN_, bkn = knew
                    CP("act", KN_[0:64, :, 0:nt], pvr[0:64, 0:2, 0:nt], [bbk], [bkn])
                    CP("dve", KWN_[0:64, :, 0:nt], pvr[0:64, 2:4, 0:nt], [bbk], [bkn])
                    return pvr, bbk
                CP("act", KS[0:64, :, kcol0:kcol0 + nt], pvr[0:64, 0:2, 0:nt], [bbk], [b_ks[0][kt], b_ks[1][kt]])
                if kt < 16:
                    CP("dve", KWh[0][0:64, :, kcol0:kcol0 + nt], pvr[0:64, 2:4, 0:nt], [bbk], [b_kw[0][kt], b_kw[1][kt]])
                if do_kvc:
                    CP("act", kvcT[:, :, kcol0:kcol0 + nt], pvr[:, 4:6, 0:nt], [bbk], [b_kvc[kt]])
                return pvr, bbk

            def rglru_chunk(N, ub, mcol0, bmix, xp_views, b_xp, sample=None):
                bss, bbss = banks[5], bbank[5]
                for c in range(4):
                    bkx, bbx = abank()
                    for kc in range(8):
                        MM(bkx[:, 0:N], Wfm[:, kc, 512 + c * 128:512 + (c + 1) * 128], uT[:, kc, 0:N], kc == 0, kc == 7,
                           ub + [b_wfm[kc]], [bbx])
                    bkg, bbg = abank()
                    for kc in range(8):
                        MM(bkg[:, 0:N], Wfm[:, kc, c * 128:(c + 1) * 128], uT[:, kc, 0:N], kc == 0, kc == 7,
                           ub + [b_wfm[kc]], [bbg])
                    xpv = xp_views[c]; bxp = b_xp[c]
                    if sample is None:
                        CP("act", xpv[:, 3:3 + N], bkx[:, 0:N], [bbx], [bxp])
                        taps = [xpv[:, j:j + N] for j in range(4)]
                        xco = xc[:, 0:N]
                    else:
                        CP("act", xpv[:, :, 3:7], bkx[:, 0:N].rearrange("p (s t) -> p s t", t=4), [bbx], [bxp])
                        taps = [xpv[:, :, j:j + 4] for j in range(4)]
                        xco = xc[:, 0:N].rearrange("p (s t) -> p s t", t=4)
                    ACT(gl[:, 0:N], bkg[:, 0:N], AF.Gelu_apprx_tanh, [bbg], [b_gl])
                    TS("dve", xco, taps[0], vcol(40, c * 4 + 0), vcol(24, c), ALU.mult, ALU.add, [bxp, b_vecs], [b_xc])
                    for j in range(1, 4):
                        STT(xco, taps[j], vcol(40, c * 4 + j), xco, ALU.mult, ALU.add, [bxp, b_vecs, b_xc], [b_xc])
                    if sample is None:
                        CP("pool", xpv[:, 0:3], xpv[:, N:N + 3], [bxp], [bxp])
                    CP("act", xcb[:, 0:N], xc[:, 0:N], [b_xc], [b_xcb])
                    bkr, bbr = abank()
                    MM(bkr[:, 0:N], wra[:, c, :], xcb[:, 0:N], True, True, [b_wra, b_xcb], [bbr])
                    bki, bbi = abank()
                    MM(bki[:, 0:N], wri[:, c, :], xcb[:, 0:N], True, True, [b_wri, b_xcb], [bbi])
                    ACT(rr[:, 0:N], bkr[:, 0:N], AF.Sigmoid, [bbr, b_vecs], [b_rr], bias=vcol(28, c))
                    ACT(ig[:, 0:N], bki[:, 0:N], AF.Sigmoid, [bbi, b_vecs], [b_ig], bias=vcol(32, c))
                    ACT(aa[:, 0:N], rr[:, 0:N], AF.Exp, [b_rr, b_nls], [b_aa], scale=nls[:, c:c + 1])
                    TT("pool", t1[:, 0:N], aa[:, 0:N], aa[:, 0:N], ALU.mult, [b_aa], [b_t1])
                    ACT(t1[:, 0:N], t1[:, 0:N], AF.Sqrt, [b_t1], [b_t1], bias=1.0, scale=-1.0)
                    TT("pool", bb_[:, 0:N], t1[:, 0:N], ig[:, 0:N], ALU.mult, [b_t1, b_ig], [b_bb])
                    TT("dve", bb_[:, 0:N], bb_[:, 0:N], xc[:, 0:N], ALU.mult, [b_bb, b_xc], [b_bb])
                    if sample is None:
                        P.op("dve", lambda e, c=c: e.tensor_tensor_scan(out=hh[:, 0:N], data0=aa[:, 0:N], data1=bb_[:, 0:N],
                                                                          initial=hcar[:, c:c + 1], op0=ALU.mult, op1=ALU.add),
                             [b_aa, b_bb, b_hcar], [b_hh])
                        CP("pool", hcar[:, c:c + 1], hh[:, N - 1:N], [b_hh], [b_hcar])
                    else:
                        h0T, b_h0 = sample
                        a3 = aa[:, 0:N].rearrange("p (s t) -> p s t", t=4)
                        b3 = bb_[:, 0:N].rearrange("p (s t) -> p s t", t=4)
                        CP("pool", asc[:, 0:N], aa[:, 0:N], [b_aa], [b_asc])
                        MEMSET("pool", asc[:, 0:N].rearrange("p (s t) -> p s t", t=4)[:, :, 0:1], 0.0, [b_asc])
                        TT("dve", t1[:, 0:NS], a3[:, :, 0], h0T[:, c, :], ALU.mult, [b_aa, b_h0, b_t1], [b_t1])
                        TT("dve", b3[:, :, 0], b3[:, :, 0], t1[:, 0:NS], ALU.add, [b_bb, b_t1], [b_bb])
                        P.op("dve", lambda e: e.tensor_tensor_scan(out=hh[:, 0:N], data0=asc[:, 0:N], data1=bb_[:, 0:N],
                                                                    initial=0.0, op0=ALU.mult, op1=ALU.add),
                             [b_asc, b_bb], [b_hh])
                        bkh, bbh = abank()
                        TR(bkh[0:NS, 0:128], hh[:, 3:N:4], ident_f[:, :], [b_hh, b_identf], [bbh])
                        CP("act", outst[0:NS, c * 128:(c + 1) * 128], bkh[0:NS, 0:128], [bbh], [b_outst])
                    TT("dve", yy[:, c, 0:N], gl[:, 0:N], hh[:, 0:N], ALU.mult, [b_gl, b_hh], [b_yy[c]])
                    TT("pool", ysq[:, 0:N], yy[:, c, 0:N], yy[:, c, 0:N], ALU.mult, [b_yy[c]], [b_ysq])
                    MM(bss[:, 0:N], ones_f[:, :], ysq[:, 0:N], c == 0, c == 3, [b_ones, b_ysq], [bbss])
                ACT(rsd[:, 0:N], bss[:, 0:N], AF.Sqrt, [bbss], [b_rsd], bias=EPS, scale=1.0 / 512)
                P.op("dve", lambda e: e.reciprocal(out=rsd[:, 0:N], in_=rsd[:, 0:N]), [b_rsd], [b_rsd])
                for c in range(4):
                    STT(mixedT[:, 4 + c, mcol0:mcol0 + N], yy[:, c, 0:N], vcol(20, c), rsd[:, 0:N], ALU.mult, ALU.mult,
                        [b_yy[c], b_vecs, b_rsd], [bmix])

            def compress(c0, nblk, rbufs):
                for w in range(2):
                    bk, bbk = abank()
                    for j in range(32):
                        lo = 16 * c0 + j
                        MM(bk[:, 0:nblk], W1[w][:, j, :], kvcT[:, w, lo:lo + 16 * (nblk - 1) + 1:16], j == 0, j == 31,
                           [b_w1[w]] + rbufs, [bbk])
                    ACT(hid[w][:, c0:c0 + nblk], bk[:, 0:nblk], AF.Gelu_apprx_tanh, [bbk, b_pec], [b_hid[w]],
                        bias=pec[:, w:w + 1])
                bk, bbk = abank()
                for h in range(2):
                    MM(bk[0:64, h * 128:(h + 1) * 128], W2[0][:, h, :], hid[0][:, :], True, True, [b_w2[0], b_hid[0]], [bbk])
                CP("act", KcT[:, :, :], bk[0:64, 0:256].rearrange("p (h c) -> p h c", h=2), [bbk], [b_kct])
                bk, bbk = abank()
                for h in range(2):
                    MM(bk[:, h * 64:(h + 1) * 64], hid[1][:, :], W2[1][:, h, :], h == 0, True, [b_w2[1], b_hid[1]], [bbk])
                CP("dve", Vc[:, :, 0:64], bk[:, 0:128].rearrange("p (h f) -> p h f", h=2), [bbk], [b_vc])

            def nextP():
                i = pbi[0]
                pbi[0] = (i + 1) % NPB
                return Pb[i], b_pb[i]

            def run_steps(steps, nq, qs_src, kdim, qcol0, rq, acc, bacc, merge=False):
                n = len(steps)
                W = 4 * nq
                sb = [(banks[3], bbank[3]), (banks[4], bbank[4]), (banks[2], bbank[2])]
                def emitS(i):
                    lhsT, nk, vap, mask, rb = steps[i]
                    bk, bbk = sb[i % 3]
                    MM(bk[0:nk, 0:W], lhsT, qs_src[0:kdim, :, qcol0:qcol0 + nq], True, True, rb + rq, [bbk])
                emitS(0)
                if n > 1:
                    emitS(1)
                for i in range(n):
                    if i + 2 < n:
                        emitS(i + 2)
                    lhsT, nk, vap, mask, rb = steps[i]
                    bk, bbk = sb[i % 3]
                    pt, bpt = nextP()
                    ACT(pt[0:nk, 0:W], bk[0:nk, 0:W], AF.Exp, [bbk], [bpt], scale=SCALE)
                    if mask is not None:
                        map_, mb = mask
                        p3 = pt[0:nk, 0:W].rearrange("p (g t) -> p g t", g=4)
                        TT("dve" if nq == 128 else "pool", p3, p3, map_.unsqueeze(1).broadcast_to([nk, 4, nq]), ALU.mult, [bpt, mb], [bpt])
                    ncol = vap.shape[-1]
                    if merge:
                        MM(acc[0:W, 0:ncol], pt[0:nk, 0:W], vap, i == 0, i == n - 1, [bpt] + rb, [bacc])
                        continue
                    for g in range(4):
                        MM(acc[0:nq, g * ncol:(g + 1) * ncol], pt[0:nk, g * nq:(g + 1) * nq], vap,
                           (i == 0 and g == 0), (i == n - 1), [bpt] + rb, [bacc])

            def attn_tile(nq, qcol0, cmp_mask, cmp_r, slc_steps, win_steps, selV, selC, b_sel, g_ap, bg, mcol0, bmix):
                idf = ident_f[0:nq, 0:nq]
                merge = (nq == 4)
                for kv in range(2):
                    steps = [(KcT[0:64, kv, 0:127], 127, Vc[0:127, kv, :], cmp_mask, [b_kct, b_vc, b_vcc] + cmp_r)]
                    run_steps(steps, nq, QC[:, kv], 64, qcol0, [b_qc[kv]], banks[5], bbank[5])
                    o = kv * 8
                    sbp, b_sbp = sbp2[kv], b_sbp2[kv]
                    CP("act", occ[0:nq, :, :], banks[5][0:nq, 0:392].rearrange("p (g c) -> p g c", g=4), [bbank[5]], [b_occ])
                    if kv == 0:
                        CP("dve", otm[0:nq, :].rearrange("p (g d) -> p g d", g=4), occ[0:nq, :, 0:64], [b_occ], [b_otm])
                    TS("dve", sm[0:nq, o:o + 4], occ[0:nq, :, 64], 1e-30, None, ALU.max, None, [b_occ], [b_sm])
                    P.op("dve", lambda e, o=o: e.reciprocal(out=sm[0:nq, o:o + 4], in_=sm[0:nq, o:o + 4]), [b_sm], [b_sm])
                    TT("dve", impt[0:nq], occ[0:nq, :, 65:98], sm[0:nq, o:o + 4].unsqueeze(2).broadcast_to([nq, 4, 33]),
                       ALU.mult, [b_occ, b_sm], [b_impt])
                    P.op("dve", lambda e: e.tensor_reduce(out=sc[0:nq, :], in_=impt[0:nq].rearrange("p g j -> p j g"),
                                                           axis=AX.X, op=ALU.add), [b_impt], [b_sc])
                    TT("dve", sc[0:nq, :], sc[0:nq, :], selV, ALU.mult, [b_sc, b_sel], [b_sc])
                    TT("dve", sc[0:nq, :], sc[0:nq, :], selC, ALU.add, [b_sc, b_sel], [b_sc])
                    P.op("dve", lambda e: e.max(out=m8[0:nq, 0:8], in_=sc[0:nq, :]), [b_sc], [b_m8])
                    P.op("dve", lambda e: e.match_replace(out=sc2[0:nq, :], in_to_replace=m8[0:nq, 0:8], in_values=sc[0:nq, :],
                                                           imm_value=-30000.0), [b_sc, b_m8], [b_sc2])
                    P.op("dve", lambda e: e.max(out=m8[0:nq, 8:16], in_=sc2[0:nq, :]), [b_sc2], [b_m8])
                    TS("dve", sbp[0:nq, 64:97], sc[0:nq, :], m8[0:nq, 15:16], BIGB, ALU.is_ge, ALU.mult, [b_sc, b_m8], [b_sbp])
                    TS("dve", sbp[0:nq, 64:97], sbp[0:nq, 64:97], -BIGB, None, ALU.add, None, [b_sbp], [b_sbp])
                wbank = [7, 5]
                sbank = [6, 7]
                for kv in range(2):
                    wb_ = wbank[kv]
                    run_steps(win_steps[kv], nq, QS[:, kv], 64, qcol0, [b_qs[kv]], banks[wb_], bbank[wb_], merge)
                    if merge:
                        CP("act", stg[0:16, kv * 2 + 1, :], banks[wb_][0:16, 0:65], [bbank[wb_]], [b_stg])
                    else:
                        CP("act", oall[0:nq, kv * 2 + 1], banks[wb_][0:nq, 0:260].rearrange("p (g c) -> p g c", g=4),
                           [bbank[wb_]], [b_oall[kv * 2 + 1]])
                for kv in range(2):
                    bk, bbk = abank()
                    TR(bk[0:97, 0:nq], sbp2[kv][0:nq, 0:97], idf, [b_sbp2[kv], b_identf], [bbk])
                    CP("dve", QS[64:97, kv, :, qcol0:qcol0 + nq], bk[64:97, 0:nq].unsqueeze(1).broadcast_to([33, 4, nq]),
                       [bbk], [b_qs[kv]])
                for kv in range(2):
                    sb_ = sbank[kv]
                    run_steps(slc_steps[kv], nq, QS[:, kv], 97, qcol0, [b_qs[kv]], banks[sb_], bbank[sb_], merge)
                    if merge:
                        CP("act", stg[0:16, kv * 2, :], banks[sb_][0:16, 0:65], [bbank[sb_]], [b_stg])
                    else:
                        CP("act", oall[0:nq, kv * 2], banks[sb_][0:nq, 0:260].rearrange("p (g c) -> p g c", g=4),
                           [bbank[sb_]], [b_oall[kv * 2]])
                if merge:
                    for g in range(4):
                        DMA("sp", oall[0:4, :, g, :], stg[4 * g:4 * g + 4, :, :], [b_stg], list(b_oall))
                for kv in range(2):
                    osl = oall[:, kv * 2]; owi = oall[:, kv * 2 + 1]
                    b_osl = b_oall[kv * 2]; b_owi = b_oall[kv * 2 + 1]
                    o = kv * 8
                    c1 = 16 + kv * 16
                    P.op("dve", lambda e, c1=c1, osl=osl: e.reciprocal(out=sm[0:nq, c1 + 4:c1 + 8], in_=osl[0:nq, :, 64]), [b_osl], [b_sm])
                    P.op("dve", lambda e, c1=c1, owi=owi: e.reciprocal(out=sm[0:nq, c1 + 8:c1 + 12], in_=owi[0:nq, :, 64]), [b_owi], [b_sm])
                    TT("dve", sm[0:nq, c1:c1 + 4], sm[0:nq, o:o + 4], g_ap[:, 0 * 8 + kv * 4:0 * 8 + kv * 4 + 4], ALU.mult, [b_sm, bg], [b_sm])
                    TT("dve", sm[0:nq, c1 + 4:c1 + 8], sm[0:nq, c1 + 4:c1 + 8], g_ap[:, 8 + kv * 4:8 + kv * 4 + 4], ALU.mult, [b_sm, bg], [b_sm])
                    TT("dve", sm[0:nq, c1 + 8:c1 + 12], sm[0:nq, c1 + 8:c1 + 12], g_ap[:, 16 + kv * 4:16 + kv * 4 + 4], ALU.mult, [b_sm, bg], [b_sm])
                    ov4 = oat[0:nq, kv * 256:(kv + 1) * 256].rearrange("p (g d) -> p g d", g=4)
                    cmpnum = otm[0:nq, :].rearrange("p (g d) -> p g d", g=4) if kv == 0 else occ[0:nq, :, 0:64]
                    cmpb = b_otm if kv == 0 else b_occ
                    def bc(col):
                        return sm[0:nq, col:col + 4].unsqueeze(2).broadcast_to([nq, 4, 64])
                    tmpv = outst[0:nq, 0:256].rearrange("p (g d) -> p g d", g=4)
                    TT("dve", ov4, cmpnum, bc(c1), ALU.mult, [cmpb, b_sm], [b_oat])
                    TT("pool", tmpv, osl[0:nq, :, 0:64], bc(c1 + 4), ALU.mult, [b_osl, b_sm], [b_outst])
                    TT("dve", ov4, ov4, tmpv, ALU.add, [b_oat, b_outst], [b_oat])
                    TT("pool", tmpv, owi[0:nq, :, 0:64], bc(c1 + 8), ALU.mult, [b_owi, b_sm], [b_outst])
                    TT("dve", ov4, ov4, tmpv, ALU.add, [b_oat, b_outst], [b_oat])
                oab, b_oab = oab2[oabi[0]], b_oab2[oabi[0]]
                oabi[0] = 1 - oabi[0]
                STT(junk[0:nq, 0:512], oat[0:nq, :], 1.0, oat[0:nq, :], ALU.mult, ALU.mult, [b_oat], [b_junk, b_st4],
                    accum=st4[0:nq, 4:5])
                ACT(st4[0:nq, 5:6], st4[0:nq, 4:5], AF.Ln, [b_st4], [b_st4], bias=EPS, scale=1.0 / 512)
                ACT(st4[0:nq, 6:7], st4[0:nq, 5:6], AF.Exp, [b_st4], [b_st4], scale=-0.5)
                TS("dve", oab[0:nq, :], oat[0:nq, :], st4[0:nq, 6:7], None, ALU.mult, None, [b_oat, b_st4], [b_oab])

                def fin():
                    bk, bbk = abank()
                    pv = bk[:].bitcast(BF16).rearrange("p (c t) -> p c t", c=8)
                    for c in range(4):
                        TR(pv[:, c, 0:nq], oab[0:nq, c * 128:(c + 1) * 128], ident_b[0:nq, 0:nq], [b_oab, b_identb], [bbk])
                    TT("dve", mixedT[:, 0:4, mcol0:mcol0 + nq], pv[:, 0:4, 0:nq],
                       V_GATT.unsqueeze(2).broadcast_to([128, 4, nq]), ALU.mult, [bbk, b_vecs], [bmix])
                return fin

            with contextlib.ExitStack() as lp:
                cm = sbt(lp, "cm", [128, NT], BF16); b_cm = P.buf()
                xin.append(sbt(lp, "xin1", [128, 1024], F32))
                KWh[0] = sbt(lp, "KW", [64, 2, NT], BF16)
                Vwh[0] = sbt(lp, "Vw", [128, 16, 2, 65], BF16)
                MEMSET("pool", Vwh[0][:, :, :, 64:65], 1.0, [b_vw[0]])
                ropep = sbt(lp, "ropep", [128, 16, 16], F32); b_ropep = P.buf()
                selvp = sbt(lp, "selvp", [128, 16, 33], F32); selcp = sbt(lp, "selcp", [128, 16, 33], F32); b_selp = P.buf()
                xpp = sbt(lp, "xpp", [128, 4, CH + 4], F32); b_xpp = [P.buf() for _ in range(4)]
                DMA("pool", cm[:], cm_d, (), [b_cm])
                DMA("sp", ropep[:], ropep_d, (), [b_ropep])
                DMA("sp", selvp[:], selvp_d, (), [b_selp]); DMA("sp", selcp[:], selcp_d, (), [b_selp])
                for c in range(4):
                    MEMSET("pool", xpp[:, c, 0:3], 0.0, [b_xpp[c]])
                xpv = [xpp[:, c, :] for c in range(4)]
                nchunks = NCHUNK if stage >= 1 else 1
                pend = [None]
                for ch in range(nchunks):
                    if stage < 0.2:
                        break
                    ubs = []
                    for ti in range(2):
                        tt = ch * 2 + ti
                        ubs += load_norm_T(xp_d[tt * 128:(tt + 1) * 128, :], 128, tt % 2, ti * 128)
                    for ti in range(2):
                        if stage < 0.3:
                            break
                        tt = ch * 2 + ti
                        proj_tile(128, ti * 128, [b_uT[ti]], ropep[:, tt, :], b_ropep, ti, ti * 128, tt * 128, tt,
                                  kvp_d[tt * 128:(tt + 1) * 128, :],
                                  winp_d[(tt - 12) * 128:(tt - 11) * 128, :] if tt >= 12 else None)
                    if stage >= 0.5:
                        rglru_chunk(CH, list(b_uT), ch * CH, b_mix[ch], xpv, b_xpp)
                    if stage < 2:
                        continue
                    c0 = 0 if ch == 0 else 16 * ch - 1
                    c1 = 16 * ch + 14
                    kr = [b_kvc[i] for i in range(max(0, 2 * ch - 1), 2 * ch + 2)]
                    compress(c0, c1 - c0 + 1, kr)
                    if pend[0] is not None:
                        pend[0](); pend[0] = None
                    fins = []
                    for ti in range(2):
                        tt = ch * 2 + ti
                        slc, win = [], []
                        for kv in range(2):
                            s_ = []
                            for kt in range(tt + 1):
                                mk = (tri_le[:, :], b_trile) if kt == tt else None
                                s_.append((KS[0:97, kv, kt * 128:(kt + 1) * 128], 128, Vs[:, kt, kv, :], mk,
                                           [b_ks[kv][kt], b_ksE, b_vs[kt], b_vs[0]]))
                            slc.append(s_)
                            w_ = []
                            for kt in range(max(0, tt - 4), tt + 1):
                                mk = (tri_le[:, :], b_trile) if kt == tt else ((tri_gt[:, :], b_trigt) if kt == tt - 4 else None)
                                w_.append((KWh[0][0:64, kv, kt * 128:(kt + 1) * 128], 128, Vwh[0][:, kt, kv, :], mk,
                                           [b_kw[kv][kt], b_vw[kt], b_vw[0]]))
                            win.append(w_)
                        fins.append(attn_tile(128, ti * 128, (cm[0:127, tt * 128:(tt + 1) * 128], b_cm), [], slc, win,
                                              selvp[:, tt, :], selcp[:, tt, :], b_selp, gt[ti][:, :], b_gt[ti], tt * 128, b_mix[ch]))
                        if ti == 1:
                            fins[0]()
                            pend[0] = fins[1]
                if pend[0] is not None:
                    pend[0](); pend[0] = None
                bk, bbk = abank()
                for c in range(4):
                    TR(bk[0:3, c * 128:(c + 1) * 128], xpp[:, c, 0:3], ident_f[:, :], [b_xpp[c], b_identf], [bbk])
                CP("act", outst[0:3, 0:512], bk[0:3, 0:512], [bbk], [b_outst])
                DMA("sp", convp_d, outst[0:3, 0:512], [b_outst], ())
                bk, bbk = abank()
                TR(bk[0:4, 0:128], hcar[:, 0:4], ident_f[:, :], [b_hcar, b_identf], [bbk])
                CP("act", oat[0:4, 0:128], bk[0:4, 0:128], [bbk], [b_oat])
                DMA("sp", hp_d, oat[0:4, 0:128], [b_oat], ())
                P.emit()
            P.barrier()

            if stage >= 3:
              with contextlib.ExitStack() as sp_:
                raw = sbt(sp_, "raw", [128, 8, 512], F32); b_raw = [P.buf() for _ in range(8)]
                KWh[0] = sbt(sp_, "KWs", [64, 2, 520], BF16)
                Vwh[0] = sbt(sp_, "Vws", [128, 5, 2, 65], BF16)
                MEMSET("pool", Vwh[0][:, :, :, 64:65], 1.0, [b_vw[0]])
                rawin = sbt(sp_, "rawin", [128, 4, 256], BF16); b_rawin = P.buf()
                ptb = sbt(sp_, "ptb", [128, 256], I32); b_ptb = P.buf()
                idx = sbt(sp_, "idx", [128, 256], I32); b_idx = P.buf()
                iop = sbt(sp_, "iop", [128, 1], F32); b_iop = P.buf()
                ropes = sbt(sp_, "ropes", [64, 16], F32); b_ropes = P.buf()
                KN = sbt(sp_, "KN", [128, 2, 64], BF16); KWN = sbt(sp_, "KWN", [64, 2, 64], BF16); b_kn = P.buf()
                vnew1 = sbt(sp_, "vnew", [4, 256], F32); vnew = [vnew1, vnew1]; b_vnew1 = P.buf(); b_vnew = [b_vnew1, b_vnew1]
                gsm = [sbt(sp_, "gsm%d" % i, [4, 24], F32) for i in range(2)]; b_gsm = [P.buf(), P.buf()]
                MEMSET("pool", KN[64:96, :, :], 0.0, [b_kn]); MEMSET("pool", KN[96:97, :, :], 1.0, [b_kn])
                selvs = sbt(sp_, "selvs", [4, 33], F32); selcs = sbt(sp_, "selcs", [4, 33], F32); b_sels = P.buf()
                sct = sbt(sp_, "sct", [48, 512], F32); b_sct = P.buf()
                srt = oat; b_srt = b_oat
                xps = sbt(sp_, "xps", [128, 4, NS, 7], F32); b_xps = [P.buf() for _ in range(4)]
                h0T = sbt(sp_, "h0T", [128, 4, NS], F32); b_h0 = P.buf()
                DMA("sp", ptb[:], ptab_d.broadcast_to([128, 256]), (), [b_ptb])
                DMA("sp", iop[:], iota_d, (), [b_iop])
                DMA("sp", ropes[:], ropes_d, (), [b_ropes])
                DMA("sp", selvs[:], selvs_d, (), [b_sels]); DMA("sp", selcs[:], selcs_d, (), [b_sels])
                DMA("sp", sct[:], sconv_d, (), [b_sct]); DMA("sp", srt[0:NS, :], srnn_d, (), [b_srt])
                TS("dve", idx[:], ptb[:], 128.0, iop[:, 0:1], ALU.mult, ALU.add, [b_ptb, b_iop], [b_idx])
                for c in range(4):
                    bk, bbk = abank()
                    TR(bk[:, 0:48], sct[0:48, c * 128:(c + 1) * 128], ident_f[0:48, 0:48], [b_sct, b_identf], [bbk])
                    CP("act", xps[:, c, :, 0:3], bk[:, 0:48].rearrange("p (s j) -> p s j", j=3), [bbk], [b_xps[c]])
                    bk, bbk = abank()
                    TR(bk[:, 0:NS], srt[0:NS, c * 128:(c + 1) * 128], ident_f[0:NS, 0:NS], [b_srt, b_identf], [bbk])
                    CP("act", h0T[:, c, :], bk[:, 0:NS], [bbk], [b_h0])
                ub = load_norm_T(xs_d, 64, 0, 0)
                rglru_chunk(64, ub, NT, b_mix[NCHUNK], [xps[:, c] for c in range(4)], b_xps, sample=(h0T, b_h0))
                DMA("sp", hs_d, outst[0:NS, 0:512], [b_outst], ())
                proj_tile(64, 0, ub, ropes[:, :], b_ropes, 0, 0, 2048, 16, kvs_d, None, do_kvc=False, knew=(KN, KWN, b_kn))
                convs3 = convs_d.rearrange("(s j) c -> s j c", j=3)
                dsts = [(xin[0][0:NS, 0:512], b_xin[0]), (xin[0][0:NS, 512:1024], b_xin[0]), (oat[0:NS, :], b_oat)]
                for j in range(3):
                    bk, bbk = abank()
                    for c in range(4):
                        TR(bk[0:NS, c * 128:(c + 1) * 128], xps[:, c, :, 4 + j], ident_f[:, :], [b_xps[c], b_identf], [bbk])
                    CP("act", dsts[j][0], bk[0:NS, 0:512], [bbk], [dsts[j][1]])
                    DMA("sp", convs3[:, j, :], dsts[j][0], [dsts[j][1]], ())
                def G(s, g):
                    for k in range(4):
                        j = g * 4 + k; sl = (g % 2) * 4 + k; col = s * 16 + j
                        P.dma("pool", lambda e, sl=sl, col=col: e.indirect_dma_start(
                            out=raw[:, sl, :], out_offset=None, in_=ckv_d,
                            in_offset=bass.IndirectOffsetOnAxis(ap=idx[:, col:col + 1], axis=0)),
                            [b_idx], [b_raw[sl]])

                def T(s, g):
                    for k in range(4):
                        j = g * 4 + k; sl = (g % 2) * 4 + k
                        bk, bbk2 = abank()
                        for h in range(2):
                            TR(bk[0:64, h * 128:(h + 1) * 128], raw[:, sl, 256 + h * 64:256 + (h + 1) * 64], ident_f[:, :],
                               [b_raw[sl], b_identf], [bbk2])
                        TR(bk[:, 256:384], raw[:, sl, 0:128], ident_f[:, :], [b_raw[sl], b_identf], [bbk2])
                        TR(bk[:, 384:512], raw[:, sl, 128:256], ident_f[:, :], [b_raw[sl], b_identf], [bbk2])
                        CP("act", KS[0:64, :, j * 128:(j + 1) * 128], bk[0:64, 0:256].rearrange("p (h t) -> p h t", h=2),
                           [bbk2], [b_ks[0][j], b_ks[1][j]])
                        CP("dve", kvcT[:, :, j * 128:(j + 1) * 128], bk[:, 256:512].rearrange("p (h t) -> p h t", h=2),
                           [bbk2], [b_kvc[j]])
                        CP("pool", Vs[:, j, :, 0:64], raw[:, sl, 384:512].rearrange("p (h d) -> p h d", h=2), [b_raw[sl]], [b_vs[j]])

                def issue_win(s):
                    DMA("pool", rawin[:], cwin_d[s].rearrange("(t p) c -> p t c", p=128), (), [b_rawin])
                    DMA("act", wins_d[s, 0:508, :], cwin_d[s, 4:512, :], (), ())
                pend = [None]
                G(0, 0); G(0, 1); issue_win(0)
                for s in range(NS):
                    vn, bvn = vnew[s % 2], b_vnew[s % 2]
                    gs_, bgs = gsm[s % 2], b_gsm[s % 2]
                    DMA("sp", vn[0:4, :], zsb[s * 4:(s + 1) * 4, 1024:1280], [b_zsb], [bvn])
                    DMA("sp", gs_[0:4, :], gt[0][s * 4:(s + 1) * 4, :], [b_gt[0]], [bgs])
                    CP("pool", Vs[0:4, 16, :, 0:64], vn[0:4, 0:128].rearrange("p (h d) -> p h d", h=2), [bvn], [b_vs[16]])
                    CP("pool", Vwh[0][0:4, 4, :, 0:64], vn[0:4, 128:256].rearrange("p (h d) -> p h d", h=2), [bvn], [b_vw[4]])
                    DMA("act", wins_d[s, 508:512, 0:128], zsb[s * 4:(s + 1) * 4, 640:768], [b_zsb], ())
                    DMA("act", wins_d[s, 508:512, 128:256], zsb[s * 4:(s + 1) * 4, 1152:1280], [b_zsb], ())
                    for g in range(4):
                        T(s, g)
                        if g + 2 < 4:
                            G(s, g + 2)
                        elif s + 1 < NS:
                            G(s + 1, g - 2)
                    for j in range(4):
                        bk, bbk2 = abank()
                        pv = bk[:].bitcast(BF16).rearrange("p (s t) -> p s t", s=8)
                        for h in range(2):
                            TR(pv[0:64, h, :], rawin[:, j, h * 64:(h + 1) * 64], ident_b[:, :], [b_rawin, b_identb], [bbk2])
                        CP("act", KWh[0][0:64, :, j * 128:(j + 1) * 128], pv[0:64, 0:2, :], [bbk2], [b_kw[0][j], b_kw[1][j]])
                        CP("pool", Vwh[0][:, j, :, 0:64], rawin[:, j, 128:256].rearrange("p (h d) -> p h d", h=2), [b_rawin], [b_vw[j]])
                    if s + 1 < NS:
                        issue_win(s + 1)
                    compress(0, 127, list(b_kvc))
                    if pend[0] is not None:
                        pend[0](); pend[0] = None
                    slc, win = [], []
                    for kv in range(2):
                        s_ = []
                        for kt in range(16):
                            s_.append((KS[0:97, kv, kt * 128:(kt + 1) * 128], 128, Vs[:, kt, kv, :], None,
                                       [b_ks[kv][kt], b_ksE, b_vs[kt], b_vs[0]]))
                        s_.append((KN[0:97, kv, s * 4:(s + 1) * 4], 4, Vs[0:4, 16, kv, :], (tri_le[0:4, 0:4], b_trile),
                                   [b_kn, b_vs[16], b_vs[0]]))
                        slc.append(s_)
                        w_ = []
                        for kt in range(4):
                            mk = (tri_gt[:, 0:4], b_trigt) if kt == 0 else None
                            w_.append((KWh[0][0:64, kv, kt * 128:(kt + 1) * 128], 128, Vwh[0][:, kt, kv, :], mk,
                                       [b_kw[kv][kt], b_vw[kt], b_vw[0]]))
                        w_.append((KWN[0:64, kv, s * 4:(s + 1) * 4], 4, Vwh[0][0:4, 4, kv, :], (tri_le[0:4, 0:4], b_trile),
                                   [b_kn, b_vw[4], b_vw[0]]))
                        win.append(w_)
                    pend[0] = attn_tile(4, s * 4, None, [], slc, win, selvs[:, :], selcs[:, :], b_sels, gs_[0:4, :], bgs,
                                        NT + s * 4, b_mix[NCHUNK])
                if pend[0] is not None:
                    pend[0](); pend[0] = None
                P.emit()
        P.barrier()

        if stage >= 4:
          with contextlib.ExitStack() as cs:
            NTI = 17
            x1 = sbt(cs, "x1", [128, NTI, 1024], F32); b_x1 = [P.buf() for _ in range(NTI)]
            vT = sbt(cs, "vT", [128, 8, NTOK], BF16); b_vT = [P.buf() for _ in range(NTI)]
            wout = sbt(cs, "wout", [128, 8, 1024], BF16); b_wout = [P.buf() for _ in range(8)]
            nfin = sbt(cs, "nfin", [128, 1024], F32); b_nfin = P.buf()
            wu = [sbt(cs, "wu%d" % i, [128, 8, 512], BF16) for i in range(2)]; b_wu = [[P.buf() for _ in range(8)] for _ in range(2)]
            wd = [sbt(cs, "wd%d" % i, [128, 4, 1024], BF16) for i in range(2)]; b_wd = [[P.buf() for _ in range(4)] for _ in range(2)]
            hT = [sbt(cs, "hT%d" % i, [128, 4, 512], BF16) for i in range(2)]; b_hT = [P.buf(), P.buf()]
            rl = [sbt(cs, "rl%d" % i, [128, 512], F32) for i in range(2)]; b_rl = [P.buf(), P.buf()]
            xnc = sbt(cs, "xnc", [128, 1024], BF16); b_xnc = P.buf()
            st5 = sbt(cs, "st5", [128, 8], F32); b_st5 = P.buf()
            yb1 = sbt(cs, "yb", [128, 1024], F32); yb = [yb1, yb1]; b_yb1 = P.buf(); b_yb = [b_yb1, b_yb1]
            for kc in range(8):
                DMA("pool", wout[:, kc, :], wout_d[kc * 128:(kc + 1) * 128, :], (), [b_wout[kc]])
            DMA("sp", nfin[:], nfin_d.broadcast_to([128, 1024]), (), [b_nfin])

            def load_w(q, bi):
                for kc in range(8):
                    DMA("pool", wu[bi][:, kc, :], wup_d[kc * 128:(kc + 1) * 128, q * 512:(q + 1) * 512], (), [b_wu[bi][kc]])
                for fc in range(4):
                    DMA("pool", wd[bi][:, fc, :], wdown_d[q * 512 + fc * 128:q * 512 + (fc + 1) * 128, :], (), [b_wd[bi][fc]])
            load_w(0, 0)

            def tile_info(ti):
                if ti < 16:
                    return 128, ti * 128, xp_d[ti * 128:(ti + 1) * 128, :], yp_d[ti * 128:(ti + 1) * 128, :], ti // 2
                return 64, NT, xs_d, ys_d, NCHUNK

            for ti in range(NTI):
                nt, col0, xsrc, ydst, mch = tile_info(ti)
                DMA("sp", x1[0:nt, ti, :], xsrc, (), [b_x1[ti]])
                for half in range(2):
                    bk, bbk = abank()
                    for kc in range(8):
                        MM(bk[0:nt, 0:512], mixedT[:, kc, col0:col0 + nt], wout[:, kc, half * 512:(half + 1) * 512],
                           kc == 0, kc == 7, [b_mix[mch], b_wout[kc]], [bbk])
                    TT("dve", x1[0:nt, ti, half * 512:(half + 1) * 512], x1[0:nt, ti, half * 512:(half + 1) * 512],
                       bk[0:nt, 0:512], ALU.add, [b_x1[ti], bbk], [b_x1[ti]])
                ACT(yb1[0:nt, :], x1[0:nt, ti, :], AF.Square, [b_x1[ti]], [b_yb1, b_st5], accum=st5[0:nt, 0:1])
                ACT(st5[0:nt, 1:2], st5[0:nt, 0:1], AF.Sqrt, [b_st5], [b_st5], bias=EPS, scale=1.0 / 1024)
                P.op("dve", lambda e, nt=nt: e.reciprocal(out=st5[0:nt, 2:3], in_=st5[0:nt, 1:2]), [b_st5], [b_st5])
                TS("dve", xnc[0:nt, :], x1[0:nt, ti, :], st5[0:nt, 2:3], None, ALU.mult, None, [b_x1[ti], b_st5], [b_xnc])
                bk, bbk = abank()
                pv = bk[:].bitcast(BF16).rearrange("p (c t) -> p c t", c=8)
                for c in range(8):
                    TR(pv[:, c, 0:nt], xnc[0:nt, c * 128:(c + 1) * 128], ident_b[0:nt, 0:nt], [b_xnc, b_identb], [bbk])
                TT("dve", vT[:, :, col0:col0 + nt], pv[:, :, 0:nt], V_NMLP.unsqueeze(2).broadcast_to([128, 8, nt]),
                   ALU.mult, [bbk, b_vecs], [b_vT[ti]])

            tchunks = [(0, 512, [0, 1, 2, 3]), (512, 512, [4, 5, 6, 7]), (1024, 512, [8, 9, 10, 11]),
                       (1536, 512, [12, 13, 14, 15]), (2048, 64, [16])]
            items = [(q, ci) for q in range(8) for ci in range(len(tchunks))]

            def up(i):
                q, ci = items[i]
                bi = q % 2
                c0, ncol, tiles = tchunks[ci]
                hb, bhb = hT[i % 2], b_hT[i % 2]
                for fc in range(4):
                    bk, bbk = abank()
                    for kc in range(8):
                        MM(bk[:, 0:ncol], wu[bi][:, kc, fc * 128:(fc + 1) * 128], vT[:, kc, c0:c0 + ncol], kc == 0, kc == 7,
                           [b_wu[bi][kc]] + [b_vT[t] for t in tiles], [bbk])
                    r_, br_ = rl[fc % 2], b_rl[fc % 2]
                    ACT(r_[:, 0:ncol], bk[:, 0:ncol], AF.Relu, [bbk], [br_])
                    TT("pool", hb[:, fc, 0:ncol], r_[:, 0:ncol], r_[:, 0:ncol], ALU.mult, [br_], [bhb])

            def down(i):
                q, ci = items[i]
                bi = q % 2
                c0, ncol, tiles = tchunks[ci]
                hb, bhb = hT[i % 2], b_hT[i % 2]
                for k, ti in enumerate(tiles):
                    nt = 128 if ti < 16 else 64
                    for half in range(2):
                        bk, bbk = banks[3 + (2 * k + half) % 4], bbank[3 + (2 * k + half) % 4]
                        for fc in range(4):
                            MM(bk[0:nt, 0:512], hb[:, fc, k * 128:k * 128 + nt], wd[bi][:, fc, half * 512:(half + 1) * 512],
                               fc == 0, fc == 3, [bhb, b_wd[bi][fc]], [bbk])
                        TT("dve", x1[0:nt, ti, half * 512:(half + 1) * 512], x1[0:nt, ti, half * 512:(half + 1) * 512],
                           bk[0:nt, 0:512], ALU.add, [b_x1[ti], bbk], [b_x1[ti]])
            load_w(1, 1)
            up(0)
            for i in range(len(items)):
                if i + 1 < len(items):
                    up(i + 1)
                down(i)
                q_, ci_ = items[i]
                if ci_ == len(tchunks) - 1 and q_ + 2 < 8:
                    load_w(q_ + 2, q_ % 2)
            for ti in range(NTI):
                nt, col0, xsrc, ydst, mch = tile_info(ti)
                ACT(xnc[0:nt, :], x1[0:nt, ti, :], AF.Square, [b_x1[ti]], [b_xnc, b_st5], accum=st5[0:nt, 4:5])
                ACT(st5[0:nt, 5:6], st5[0:nt, 4:5], AF.Sqrt, [b_st5], [b_st5], bias=EPS, scale=1.0 / 1024)
                P.op("dve", lambda e, nt=nt: e.reciprocal(out=st5[0:nt, 6:7], in_=st5[0:nt, 5:6]), [b_st5], [b_st5])
                y_, by_ = yb[ti % 2], b_yb[ti % 2]
                STT(y_[0:nt, :], x1[0:nt, ti, :], st5[0:nt, 6:7], nfin[0:nt, :], ALU.mult, ALU.mult,
                    [b_x1[ti], b_st5, b_nfin], [by_])
                DMA("sp", ydst, y_[0:nt, :], [by_], ())
            P.finish()
            P.emit()
        else:
            P.finish()
            P.emit()
    nc._used_inputs = used_inputs
    return nc, used_inputs


_STAGE = 99


def _consts():
    f32 = np.float32
    c = {}
    c["identf"] = np.eye(128, dtype=f32)
    k = np.arange(128)
    c["trile"] = (k[:, None] <= k[None, :]).astype(f32)
    c["trigt"] = (k[:, None] > k[None, :]).astype(f32)
    cc = np.arange(128)[:, None]
    t = np.arange(NT)[None, :]
    cm = ((16 * cc + 31) <= t).astype(f32)
    cm[127] = 0.0
    c["cm"] = cm
    j = np.arange(33)[:, None]
    kk = np.arange(2056)[None, :]
    c["ec"] = ((kk // 64) == j).astype(f32)
    c0 = (np.arange(128) * 16)[:, None]
    j0 = (np.arange(33) * 64)[None, :]
    ov = ((c0 < j0 + 64) & (c0 + 32 > j0)).astype(f32)
    ov[127] = 0.0
    c["ov"] = ov
    inv = (np.float32(500000.0) ** (-np.arange(8, dtype=f32) / np.float32(8))).astype(f32)

    def rope(pos):
        ang = pos.astype(f32)[:, None] * inv[None, :]
        return np.concatenate([np.cos(ang), np.sin(ang)], axis=1).astype(f32)
    pos = (np.arange(16)[None, :] * 128 + np.arange(128)[:, None])
    c["ropep"] = rope(pos.reshape(-1)).reshape(128, 16, 16)
    c["ropes"] = rope(2048 + (np.arange(64) % 4))

    def sel(tpos, nsel):
        cur = (tpos // 64)[:, None]
        jj = np.arange(33)[None, :]
        valid = (jj <= cur) & (jj < nsel)
        forced = ((jj == 0) | (jj == cur) | (jj == cur - 1)) & (jj < nsel)
        V = valid.astype(f32)
        C = ((V - 1.0) * 1e4 + forced.astype(f32) * 1e4).astype(f32)
        return V, C
    V, C = sel(pos.reshape(-1), 32)
    c["selvp"] = V.reshape(128, 16, 33); c["selcp"] = C.reshape(128, 16, 33)
    V, C = sel(2048 + np.arange(4), 33)
    c["selvs"] = V; c["selcs"] = C
    c["iotap"] = np.arange(128, dtype=f32).reshape(128, 1)
    return c


_NC_CACHE = {}


def kernel(**inp):
    f32 = np.float32
    g = lambda k: np.asarray(inp[k])
    w_in = g("w_in")[0].astype(f32)
    r = np.arange
    cols = np.concatenate([r(0, 512), r(768, 896), r(1024, 1152), r(1280, 1304), r(512, 640), r(640, 768),
                           r(896, 1024), r(1152, 1280)])
    shared = {}
    shared["wtm"] = np.ascontiguousarray(w_in[:, cols])
    shared["wfm"] = np.ascontiguousarray(w_in[:, 1304:2328])
    shared["wout"] = np.ascontiguousarray(g("w_out")[0].astype(f32))
    shared["wup"] = np.ascontiguousarray(g("w_up")[0].astype(f32))
    shared["wdown"] = np.ascontiguousarray(g("w_down")[0].astype(f32))
    for nm, w1n, w2n, pen in (("k", "w_ck1", "w_ck2", "pe_ck"), ("v", "w_cv1", "w_cv2", "pe_cv")):
        w1 = g(w1n)[0].astype(f32); w2 = g(w2n)[0].astype(f32); pe = g(pen)[0].astype(f32)
        w1bd = np.zeros((128, 32, 128), f32); w2p = np.zeros((128, 2, 64), f32); pe2 = np.zeros((128, 32), f32)
        for h in range(2):
            w1bd[h * 64:(h + 1) * 64, :, h * 64:(h + 1) * 64] = w1.transpose(1, 0, 2)
            w2p[h * 64:(h + 1) * 64, h, :] = w2
            pe2[h * 64:(h + 1) * 64, :] = pe.T
        shared["w1" + nm] = w1bd; shared["w2" + nm] = w2p; shared["pe" + nm] = pe2
    for nm, wn in (("wra", "w_ra"), ("wri", "w_ri")):
        w = g(wn)[0].astype(f32)
        bd = np.zeros((128, 4, 128), f32)
        for c in range(4):
            for b in range(2):
                bd[b * 64:(b + 1) * 64, c, b * 64:(b + 1) * 64] = w[2 * c + b]
        shared[nm] = bd
    vecs = np.zeros((128, 56), f32)
    T8 = lambda v: np.asarray(v, f32).reshape(-1, 128).T
    vecs[:, 0:8] = T8(g("norm_mix")[0]); vecs[:, 8:16] = T8(g("norm_mlp")[0])
    vecs[:, 16:20] = T8(g("g_attn")[0]); vecs[:, 20:24] = T8(g("g_rnn")[0])
    vecs[:, 24:28] = T8(g("conv_b")[0]); vecs[:, 28:32] = T8(g("b_ra")[0]); vecs[:, 32:36] = T8(g("b_ri")[0])
    vecs[:, 36:40] = T8(g("lam")[0])
    cw = g("conv_w")[0].astype(f32)
    for c in range(4):
        for tap in range(4):
            vecs[:, 40 + c * 4 + tap] = cw[tap, c * 128:(c + 1) * 128]
    shared["vecs"] = vecs
    shared["nfin"] = np.asarray(g("norm_final"), f32).reshape(1, 1024)
    shared["ckv"] = np.ascontiguousarray(g("cache_kv")[0].astype(f32).reshape(2560 * 128, 512))
    shared.update(_consts())

    xp = g("x_prompt").astype(f32); xs = g("x_sample").astype(f32)
    cwin = g("cache_win")[0].astype(f32).reshape(128, 512, 256)
    sconv = g("state_conv")[0].astype(f32); srnn = g("state_rnn")[0].astype(f32)
    pt = g("page_table").astype(np.int32)
    in_maps = []
    for c in range(8):
        m = dict(shared)
        m["xp"] = np.ascontiguousarray(xp[c])
        m["xs"] = np.ascontiguousarray(xs[c * 16:(c + 1) * 16].reshape(64, 1024))
        m["cwin"] = np.ascontiguousarray(cwin[c * 16:(c + 1) * 16])
        m["sconv"] = np.ascontiguousarray(sconv[c * 16:(c + 1) * 16].reshape(48, 512))
        m["srnn"] = np.ascontiguousarray(srnn[c * 16:(c + 1) * 16])
        m["ptab"] = np.ascontiguousarray(pt[c * 16:(c + 1) * 16].reshape(1, 256))
        in_maps.append(m)
    if _STAGE not in _NC_CACHE:
        _NC_CACHE[_STAGE] = build_nc(_STAGE)
    nc, used = _NC_CACHE[_STAGE]
    in_maps = [{k: v for k, v in m.items() if k in used} for m in in_maps]
    res = run_bass_kernel_spmd(nc, in_maps, core_ids=list(range(8)))
    R = res.results
    cat = lambda k: np.stack([np.asarray(R[c][k], f32) for c in range(8)])
    y_p = cat("y_p")
    y_s = cat("y_s").reshape(128, 4, 1024)
    kv_p = cat("kv_p").reshape(1, 8, 2048, 4, 2, 64)
    kv_s = cat("kv_s").reshape(1, 128, 4, 4, 2, 64)
    win_p = cat("win_p").reshape(1, 8, 512, 2, 2, 64)
    win_s = cat("win_s").reshape(1, 128, 512, 2, 2, 64)
    conv_p = cat("conv_p").reshape(1, 8, 3, 512)
    conv_s = cat("conv_s").reshape(1, 128, 3, 512)
    h_p = cat("h_p").reshape(1, 8, 512)
    h_s = cat("h_s").reshape(1, 128, 512)
    return (y_p, y_s, kv_p, kv_s, win_p, win_s, conv_p, conv_s, h_p, h_s)
```

```python
import contextlib
import numpy as np
import concourse.bass as bass
import concourse.mybir as mybir
from concourse.bass_utils import run_bass_kernel_spmd

F32 = mybir.dt.float32
BF16 = mybir.dt.bfloat16
I32 = mybir.dt.int32
U32 = mybir.dt.uint32
AF = mybir.ActivationFunctionType
ALU = mybir.AluOpType
AX = mybir.AxisListType


class Buf:
    __slots__ = ("name", "last_w", "readers", "excl")

    def __init__(self, name, excl=False):
        self.name = name
        self.last_w = None
        self.readers = []
        self.excl = excl


class Prog:
    ENGS = ("pe", "act", "dve", "pool", "sp")
    NDMA = {"sp": 24, "act": 12, "pool": 24}

    def __init__(self, nc, stack):
        self.nc = nc
        self.stack = stack
        self.stream = {e: [] for e in self.ENGS}
        self.cnt = {e: 0 for e in self.ENGS}
        self.waited = {e: {} for e in self.ENGS}
        self.sems = {}
        for e in self.ENGS:
            self.sems[e] = stack.enter_context(nc.semaphore("s_" + e))
        self.dslots = {}
        self.dnext = {}
        for q, n in self.NDMA.items():
            self.dslots[q] = []
            for i in range(n):
                k = "d_%s_%d" % (q, i)
                self.sems[k] = stack.enter_context(nc.semaphore(k))
                self.dslots[q].append([k, 0])
            self.dnext[q] = 0
        self.nbuf = 0

    def buf(self, name=None, excl=False):
        self.nbuf += 1
        return Buf(name or ("b%d" % self.nbuf), excl)

    def _deps(self, e, reads, writes):
        deps = set()
        for b in reads:
            if b.last_w is not None:
                deps.add(b.last_w)
            if b.excl:
                for r in b.readers:
                    if r[0] != e:
                        deps.add(r)
        for b in writes:
            if b.last_w is not None:
                deps.add(b.last_w)
            for r in b.readers:
                deps.add(r)
        waits = []
        best = {}
        for (sk, v) in deps:
            if sk == e and e == "pe":
                continue
            if v > best.get(sk, 0):
                best[sk] = v
        for sk, v in best.items():
            if self.waited[e].get(sk, 0) >= v:
                continue
            self.waited[e][sk] = v
            waits.append((sk, v))
        return waits

    def _commit(self, tok, reads, writes):
        for b in reads:
            b.readers.append(tok)
        for b in writes:
            b.last_w = tok
            b.readers = []

    def op(self, e, fn, reads=(), writes=()):
        reads = [b for b in reads if b is not None]
        writes = [b for b in writes if b is not None]
        waits = self._deps(e, reads, writes)
        self.cnt[e] += 1
        tok = (e, self.cnt[e])
        self.stream[e].append((waits, fn, e, 1))
        self._commit(tok, reads, writes)
        return tok

    def dma(self, q, fn, reads=(), writes=()):
        reads = [b for b in reads if b is not None]
        writes = [b for b in writes if b is not None]
        waits = self._deps(q, reads, writes)
        slots = self.dslots[q]
        i = self.dnext[q]
        self.dnext[q] = (i + 1) % len(slots)
        sk, tot = slots[i]
        if tot > 0 and self.waited[q].get(sk, 0) < tot:
            self.waited[q][sk] = tot
            waits.append((sk, tot))
        slots[i][1] = tot + 16
        tok = (sk, tot + 16)
        self.stream[q].append((waits, fn, sk, 16))
        self._commit(tok, reads, writes)
        return tok

    def finish(self):
        for q in self.NDMA:
            waits = []
            for sk, tot in self.dslots[q]:
                if tot > 0 and self.waited[q].get(sk, 0) < tot:
                    self.waited[q][sk] = tot
                    waits.append((sk, tot))
            if waits:
                self.stream[q].append((waits, None, None, 0))

    def barrier(self):
        toks = [(e, self.cnt[e]) for e in self.ENGS if self.cnt[e] > 0]
        for q in self.NDMA:
            for sk, tot in self.dslots[q]:
                if tot > 0:
                    toks.append((sk, tot))
        for e in self.ENGS:
            waits = []
            for sk, v in toks:
                if sk == e:
                    continue
                if self.waited[e].get(sk, 0) >= v:
                    continue
                self.waited[e][sk] = v
                waits.append((sk, v))
            if waits:
                self.stream[e].append((waits, None, None, 0))

    def emit(self):
        nc = self.nc
        P = self
        with nc.Block() as block:
            def run(e, eng):
                for waits, fn, sk, inc in P.stream[e]:
                    for (wk, wv) in waits:
                        eng.wait_ge(P.sems[wk], wv)
                    if fn is None:
                        continue
                    ins = fn(eng)
                    ins.then_inc(P.sems[sk], inc)

            @block.tensor
            def _(eng):
                run("pe", eng)

            @block.scalar
            def _(eng):
                run("act", eng)

            @block.vector
            def _(eng):
                run("dve", eng)

            @block.gpsimd
            def _(eng):
                run("pool", eng)

            @block.sync
            def _(eng):
                run("sp", eng)
        self.stream = {e: [] for e in self.ENGS}


EPS = 1e-6
NS = 16
ST = 4
CH = 256
NCHUNK = 8
NT = 2048
NTOK = NT + NS * ST
SCALE = 0.125
BIGB = 240000.0


def build_nc(stage=99):
    nc = bass.Bass("TRN2", target_bir_lowering=False)

    used_inputs = []

    def din(name, shape, dt=F32):
        used_inputs.append(name)
        return nc.dram_tensor(name, shape, dt, kind="ExternalInput").ap()

    def dout(name, shape):
        return nc.dram_tensor(name, shape, F32, kind="ExternalOutput").ap()

    xp_d = din("xp", [NT, 1024]); xs_d = din("xs", [64, 1024])
    if stage >= 3:
        ckv_d = din("ckv", [2560 * 128, 512]); cwin_d = din("cwin", [NS, 512, 256])
    sconv_d = din("sconv", [48, 512]); srnn_d = din("srnn", [NS, 512])
    ptab_d = din("ptab", [1, 256], I32)
    wtm_d = din("wtm", [1024, 1304]); wfm_d = din("wfm", [1024, 1024])
    wout_d = din("wout", [1024, 1024]); wup_d = din("wup", [1024, 4096]); wdown_d = din("wdown", [4096, 1024])
    w1k_d = din("w1k", [128, 32, 128]); w1v_d = din("w1v", [128, 32, 128])
    w2k_d = din("w2k", [128, 2, 64]); w2v_d = din("w2v", [128, 2, 64])
    pek_d = din("pek", [128, 32]); pev_d = din("pev", [128, 32])
    wra_d = din("wra", [128, 4, 128]); wri_d = din("wri", [128, 4, 128])
    vecs_d = din("vecs", [128, 56]); nfin_d = din("nfin", [1, 1024])
    identf_d = din("identf", [128, 128]); trile_d = din("trile", [128, 128]); trigt_d = din("trigt", [128, 128])
    cm_d = din("cm", [128, NT]); ec_d = din("ec", [33, 2056]); ov_d = din("ov", [128, 33])
    ropep_d = din("ropep", [128, 16, 16]); ropes_d = din("ropes", [64, 16])
    selvp_d = din("selvp", [128, 16, 33]); selcp_d = din("selcp", [128, 16, 33])
    selvs_d = din("selvs", [4, 33]); selcs_d = din("selcs", [4, 33])
    iota_d = din("iotap", [128, 1])

    yp_d = dout("y_p", [NT, 1024]); ys_d = dout("y_s", [64, 1024])
    kvp_d = dout("kv_p", [NT, 512]); kvs_d = dout("kv_s", [64, 512])
    winp_d = dout("win_p", [512, 256]); wins_d = dout("win_s", [NS, 512, 256])
    convp_d = dout("conv_p", [3, 512]); convs_d = dout("conv_s", [48, 512])
    hp_d = dout("h_p", [4, 128]); hs_d = dout("h_s", [NS, 512])

    with contextlib.ExitStack() as glob:
        P = Prog(nc, glob)

        def sbt(stack, name, shape, dt):
            return stack.enter_context(nc.sbuf_tensor("s_" + name, shape, dt))

        banks = [glob.enter_context(nc.psum_tensor("pb%d" % i, [128, 512], F32)) for i in range(8)]
        bbank = [P.buf("bank%d" % i, excl=True) for i in range(8)]
        apool = [0]

        def abank():
            i = apool[0]
            apool[0] = (i + 1) % 2
            return banks[i], bbank[i]

        def MM(out, lhsT, rhs, start, stop, R, W):
            P.op("pe", lambda e: e.matmul(out, lhsT=lhsT, rhs=rhs, start=start, stop=stop,
                                          skip_group_check=True), R, W)

        def TR(out, in_, ident, R, W):
            P.op("pe", lambda e: e.transpose(out=out, in_=in_, identity=ident), R, W)

        def ACT(out, in_, func, R, W, bias=None, scale=1.0, accum=None):
            kw = {}
            if bias is not None:
                kw["bias"] = bias
            if accum is not None:
                kw["accum_out"] = accum
            P.op("act", lambda e: e.activation(out=out, in_=in_, func=func, scale=scale, **kw), R, W)

        def TS(eng, out, in0, s1, s2, op0, op1, R, W):
            if s2 is None:
                P.op(eng, lambda e: e.tensor_scalar(out=out, in0=in0, scalar1=s1, scalar2=None, op0=op0), R, W)
            else:
                P.op(eng, lambda e: e.tensor_scalar(out=out, in0=in0, scalar1=s1, scalar2=s2, op0=op0, op1=op1), R, W)

        def TT(eng, out, in0, in1, op, R, W):
            P.op(eng, lambda e: e.tensor_tensor(out=out, in0=in0, in1=in1, op=op), R, W)

        def STT(out, in0, scalar, in1, op0, op1, R, W, accum=None):
            kw = {} if accum is None else {"accum_out": accum}
            P.op("dve", lambda e: e.scalar_tensor_tensor(out=out, in0=in0, scalar=scalar, in1=in1,
                                                         op0=op0, op1=op1, **kw), R, W)

        def CP(eng, out, in_, R, W):
            if eng == "act":
                P.op("act", lambda e: e.copy(out=out, in_=in_), R, W)
            else:
                P.op(eng, lambda e: e.tensor_copy(out=out, in_=in_), R, W)

        def MEMSET(eng, ap, val, W):
            P.op(eng, lambda e: e.memset(ap, val), (), W)

        def DMA(q, out, in_, R, W, **kw):
            if q == "pool" and out.dtype != in_.dtype:
                kw.setdefault("max_dma_last_dim", 2048)
            P.dma(q, lambda e: e.dma_start(out=out, in_=in_, **kw), R, W)

        ident_b = sbt(glob, "ident_b", [128, 128], BF16); b_identb = P.buf()
        ident_f = sbt(glob, "ident_f", [128, 128], F32); b_identf = P.buf()
        ones_f = sbt(glob, "ones_f", [128, 128], F32); b_ones = P.buf()
        tri_le = sbt(glob, "tri_le", [128, 128], BF16); b_trile = P.buf()
        tri_gt = sbt(glob, "tri_gt", [128, 128], BF16); b_trigt = P.buf()
        vecs = sbt(glob, "vecs", [128, 56], F32); b_vecs = P.buf()
        nls = sbt(glob, "nls", [128, 4], F32); b_nls = P.buf()
        mixedT = sbt(glob, "mixedT", [128, 8, NTOK], BF16)
        b_mix = [P.buf("mix%d" % i) for i in range(NCHUNK + 1)]

        DMA("sp", ident_f[:], identf_d, (), [b_identf])
        DMA("pool", ident_b[:], identf_d, (), [b_identb])
        DMA("pool", tri_le[:], trile_d, (), [b_trile])
        DMA("pool", tri_gt[:], trigt_d, (), [b_trigt])
        DMA("sp", vecs[:], vecs_d, (), [b_vecs])
        MEMSET("pool", ones_f[:], 1.0, [b_ones])
        ACT(nls[:], vecs[:, 36:40], AF.Exp, [b_vecs], [b_nls], scale=-1.0)
        ACT(nls[:], nls[:], AF.Ln, [b_nls], [b_nls], bias=1.0)
        TS("dve", nls[:], nls[:], -8.0, None, ALU.mult, None, [b_nls], [b_nls])

        V_NMIX = vecs[:, 0:8]; V_NMLP = vecs[:, 8:16]; V_GATT = vecs[:, 16:20]

        def vcol(base, c):
            return vecs[:, base + c:base + c + 1]

        with contextlib.ExitStack() as ls:
            Wtm = sbt(ls, "Wtm", [128, 8, 1304], BF16); b_wtm = [P.buf() for _ in range(8)]
            Wfm = sbt(ls, "Wfm", [128, 8, 1024], BF16); b_wfm = [P.buf() for _ in range(8)]
            for kc in range(8):
                DMA("pool", Wtm[:, kc, :], wtm_d[kc * 128:(kc + 1) * 128, :], (), [b_wtm[kc]])
            for kc in range(8):
                DMA("pool", Wfm[:, kc, :], wfm_d[kc * 128:(kc + 1) * 128, :], (), [b_wfm[kc]])
            W1 = [sbt(ls, "W1k", [128, 32, 128], BF16), sbt(ls, "W1v", [128, 32, 128], BF16)]
            b_w1 = [P.buf(), P.buf()]
            W2 = [sbt(ls, "W2k", [128, 2, 64], BF16), sbt(ls, "W2v", [128, 2, 64], BF16)]
            b_w2 = [P.buf(), P.buf()]
            pe2 = [sbt(ls, "pek2", [128, 32], BF16), sbt(ls, "pev2", [128, 32], BF16)]
            b_pe2 = [P.buf(), P.buf()]
            pec = sbt(ls, "pec", [128, 2], F32); b_pec = P.buf()
            wra = sbt(ls, "wra", [128, 4, 128], BF16); b_wra = P.buf()
            wri = sbt(ls, "wri", [128, 4, 128], BF16); b_wri = P.buf()
            DMA("pool", W1[0][:], w1k_d, (), [b_w1[0]]); DMA("pool", W1[1][:], w1v_d, (), [b_w1[1]])
            DMA("pool", W2[0][:], w2k_d, (), [b_w2[0]]); DMA("pool", W2[1][:], w2v_d, (), [b_w2[1]])
            DMA("pool", pe2[0][:], pek_d, (), [b_pe2[0]]); DMA("pool", pe2[1][:], pev_d, (), [b_pe2[1]])
            DMA("pool", wra[:], wra_d, (), [b_wra]); DMA("pool", wri[:], wri_d, (), [b_wri])

            KS = sbt(ls, "KS", [128, 2, 2056], BF16)
            b_ks = [[P.buf() for _ in range(17)] for _ in range(2)]; b_ksE = P.buf()
            KWh = [None]; b_kw = [[P.buf() for _ in range(16)] for _ in range(2)]
            Vs = sbt(ls, "Vs", [128, 17, 2, 65], BF16); b_vs = [P.buf() for _ in range(17)]
            Vwh = [None]; b_vw = [P.buf() for _ in range(16)]
            kvcT = sbt(ls, "kvcT", [128, 2, NT], BF16); b_kvc = [P.buf() for _ in range(16)]
            hid = [sbt(ls, "hidk", [128, 128], BF16), sbt(ls, "hidv", [128, 128], BF16)]
            b_hid = [P.buf(), P.buf()]
            KcT = sbt(ls, "KcT", [64, 2, 128], BF16); b_kct = P.buf()
            Vc = sbt(ls, "Vc", [128, 2, 98], BF16); b_vc = P.buf(); b_vcc = P.buf()
            for kv in range(2):
                DMA("pool", KS[64:97, kv, :], ec_d, (), [b_ksE])
            MEMSET("pool", Vs[:, :, :, 64:65], 1.0, [b_vs[0]])
            MEMSET("pool", Vc[:, :, 64:65], 1.0, [b_vcc])
            for kv in range(2):
                DMA("pool", Vc[:, kv, 65:98], ov_d, (), [b_vcc])
            MEMSET("pool", hid[0][:], 0.0, [b_hid[0]]); MEMSET("pool", hid[1][:], 0.0, [b_hid[1]])

            for w in range(2):
                bk, bb = abank()
                for j in range(32):
                    MM(bk[:, 0:1], W1[w][:, j, :], pe2[w][:, j:j + 1], j == 0, j == 31, [b_w1[w], b_pe2[w]], [bb])
                CP("dve", pec[:, w:w + 1], bk[:, 0:1], [bb], [b_pec])

            xin = [sbt(ls, "xin0", [128, 1024], F32)]; b_xin = [P.buf(), P.buf()]
            xnb = sbt(ls, "xnb", [128, 1024], BF16); b_xnb = P.buf()
            junk = sbt(ls, "junk", [128, 1024], BF16); b_junk = P.buf()
            st4 = sbt(ls, "st4", [128, 8], F32); b_st4 = P.buf()
            uT = sbt(ls, "uT", [128, 8, CH], BF16); b_uT = [P.buf(), P.buf()]
            zsb = sbt(ls, "zsb", [128, 1280], F32); b_zsb = P.buf()
            gt = [sbt(ls, "gt%d" % i, [128, 24], F32) for i in range(2)]; b_gt = [P.buf(), P.buf()]
            rt = sbt(ls, "rt", [128, 4, 96], F32); b_rt = P.buf()
            tb = sbt(ls, "tb", [128, 1024], BF16); b_tb = P.buf()
            tbq = sbt(ls, "tbq", [128, 512], BF16); b_tbq = P.buf()
            QS = sbt(ls, "QS", [128, 2, 4, CH], BF16); b_qs = [P.buf(), P.buf()]
            QC = sbt(ls, "QC", [64, 2, 4, CH], BF16); b_qc = [P.buf(), P.buf()]
            xc = sbt(ls, "xc", [128, CH], F32); b_xc = P.buf()
            xcb = sbt(ls, "xcb", [128, CH], BF16); b_xcb = P.buf()
            rr = sbt(ls, "rr", [128, CH], F32); b_rr = P.buf()
            ig = sbt(ls, "ig", [128, CH], F32); b_ig = P.buf()
            aa = sbt(ls, "aa", [128, CH], F32); b_aa = P.buf()
            asc = sbt(ls, "asc", [128, CH], F32); b_asc = P.buf()
            t1 = sbt(ls, "t1", [128, CH], F32); b_t1 = P.buf()
            bb_ = sbt(ls, "bb_", [128, CH], F32); b_bb = P.buf()
            hh = sbt(ls, "hh", [128, CH], F32); b_hh = P.buf()
            gl = sbt(ls, "gl", [128, CH], F32); b_gl = P.buf()
            yy = sbt(ls, "yy", [128, 4, CH], F32); b_yy = [P.buf() for _ in range(4)]
            ysq = sbt(ls, "ysq", [128, CH], F32); b_ysq = P.buf()
            rsd = sbt(ls, "rsd", [128, CH], F32); b_rsd = P.buf()
            hcar = sbt(ls, "hcar", [128, 4], F32); b_hcar = P.buf()
            MEMSET("pool", hcar[:], 0.0, [b_hcar])
            NPB = 4
            Pb = [sbt(ls, "Pb%d" % i, [128, 512], BF16) for i in range(NPB)]; b_pb = [P.buf() for _ in range(NPB)]
            pbi = [0]
            rbi = [0]
            occ = sbt(ls, "occ", [128, 4, 98], F32); b_occ = P.buf()
            oall = sbt(ls, "oall", [128, 4, 4, 65], F32); b_oall = [P.buf() for _ in range(4)]
            stg = sbt(ls, "stg", [16, 4, 65], F32); b_stg = P.buf()
            sm = sbt(ls, "sm", [128, 64], F32); b_sm = P.buf()
            impt = sbt(ls, "impt", [128, 4, 33], F32); b_impt = P.buf()
            sc = sbt(ls, "sc", [128, 33], F32); b_sc = P.buf()
            sc2 = sbt(ls, "sc2", [128, 33], F32); b_sc2 = P.buf()
            m8 = sbt(ls, "m8", [128, 16], F32); b_m8 = P.buf()
            sbp2 = [sbt(ls, "sbp%d" % i, [128, 97], F32) for i in range(2)]; b_sbp2 = [P.buf(), P.buf()]
            MEMSET("pool", sbp2[0][:], 0.0, [b_sbp2[0]]); MEMSET("pool", sbp2[1][:], 0.0, [b_sbp2[1]])
            oat = sbt(ls, "oat", [128, 512], F32); b_oat = P.buf()
            otm = sbt(ls, "otm", [128, 256], F32); b_otm = P.buf()
            oab2 = [sbt(ls, "oab%d" % i, [128, 512], BF16) for i in range(2)]; b_oab2 = [P.buf(), P.buf()]; oabi = [0]
            outst = sbt(ls, "outst", [128, 512], F32); b_outst = P.buf()

            def load_norm_T(src_ap, nt, xi, ucol0):
                xt, bx = xin[xi], b_xin[xi]
                DMA("sp", xt[0:nt, :], src_ap, (), [bx])
                ACT(junk[0:nt, :], xt[0:nt, :], AF.Square, [bx], [b_junk, b_st4], accum=st4[0:nt, 0:1])
                ACT(st4[0:nt, 1:2], st4[0:nt, 0:1], AF.Sqrt, [b_st4], [b_st4], bias=EPS, scale=1.0 / 1024)
                P.op("dve", lambda e: e.reciprocal(out=st4[0:nt, 2:3], in_=st4[0:nt, 1:2]), [b_st4], [b_st4])
                TS("dve", xnb[0:nt, :], xt[0:nt, :], st4[0:nt, 2:3], None, ALU.mult, None, [bx, b_st4], [b_xnb])
                bk, bbk = abank()
                pv = bk[:].bitcast(BF16).rearrange("p (c t) -> p c t", c=8)
                for c in range(8):
                    TR(pv[:, c, 0:nt], xnb[0:nt, c * 128:(c + 1) * 128], ident_b[0:nt, 0:nt], [b_xnb, b_identb], [bbk])
                ub = b_uT[(ucol0 // 128) % 2] if nt == 128 else None
                wr = [ub] if ub is not None else list(b_uT)
                TT("dve", uT[:, :, ucol0:ucol0 + nt], pv[:, :, 0:nt],
                   V_NMIX.unsqueeze(2).broadcast_to([128, 8, nt]), ALU.mult, [bbk, b_vecs], wr)
                return wr

            def proj_tile(nt, ucol0, ub, rope_ap, brope, gti, qcol0, kcol0, vtile, kvout, winout, do_kvc=True, knew=None):
                g = gt[gti]; bg = b_gt[gti]
                offs = [(0, 512), (512, 280), (792, 512)]
                pz = []
                for gi, (o, w) in enumerate(offs):
                    bk, bbk = banks[gi], bbank[gi]
                    for kc in range(8):
                        MM(bk[0:nt, 0:w], uT[:, kc, ucol0:ucol0 + nt], Wtm[:, kc, o:o + w], kc == 0, kc == 7,
                           ub + [b_wtm[kc]], [bbk])
                    pz.append((bk, bbk))
                CP("act", zsb[0:nt, 0:512], pz[0][0][0:nt, 0:512], [pz[0][1]], [b_zsb])
                CP("act", zsb[0:nt, 512:768], pz[1][0][0:nt, 0:256], [pz[1][1]], [b_zsb])
                ACT(g[0:nt, :], pz[1][0][0:nt, 256:280], AF.Sigmoid, [pz[1][1]], [bg])
                CP("dve", zsb[0:nt, 768:1280], pz[2][0][0:nt, 0:512], [pz[2][1]], [b_zsb])
                if stage < 0.215:
                    return None, None
                CP("act", tbq[0:nt, :], zsb[0:nt, 0:512], [b_zsb], [b_tbq])
                R = zsb[0:nt, 0:768].rearrange("p (h d) -> p h d", h=12)
                x1 = R[:, :, 0:8]; x2 = R[:, :, 8:16]
                cs = rope_ap[:, 0:8].unsqueeze(1).broadcast_to([nt, 12, 8])
                sn = rope_ap[:, 8:16].unsqueeze(1).broadcast_to([nt, 12, 8])
                rtv = [rt[0:nt, i, :].rearrange("p (h d) -> p h d", h=12) for i in range(4)]
                TT("dve", rtv[0], x1, cs, ALU.mult, [b_zsb, brope], [b_rt])
                TT("dve", rtv[1], x2, sn, ALU.mult, [b_zsb, brope], [b_rt])
                TT("dve", rtv[2], x2, cs, ALU.mult, [b_zsb, brope], [b_rt])
                TT("dve", rtv[3], x1, sn, ALU.mult, [b_zsb, brope], [b_rt])
                TT("dve", x1, rtv[0], rtv[1], ALU.subtract, [b_rt], [b_zsb])
                TT("dve", x2, rtv[2], rtv[3], ALU.add, [b_rt], [b_zsb])
                if stage < 0.225:
                    return None, None
                DMA("sp", kvout[:, 0:256], zsb[0:nt, 768:1024], [b_zsb], ())
                DMA("sp", kvout[:, 256:384], zsb[0:nt, 512:640], [b_zsb], ())
                DMA("sp", kvout[:, 384:512], zsb[0:nt, 1024:1152], [b_zsb], ())
                if winout is not None:
                    DMA("sp", winout[:, 0:128], zsb[0:nt, 640:768], [b_zsb], ())
                    DMA("sp", winout[:, 128:256], zsb[0:nt, 1152:1280], [b_zsb], ())
                if stage < 0.235:
                    return None, None
                CP("act", tb[0:nt, :], zsb[0:nt, 0:1024], [b_zsb], [b_tb])
                vsb = b_vs[vtile] if knew is None else None; vwb = b_vw[vtile] if (vtile < 16 and knew is None) else None
                if knew is None:
                    CP("pool", Vs[0:nt, vtile, :, 0:64], zsb[0:nt, 1024:1152].rearrange("p (h d) -> p h d", h=2),
                       [b_zsb], [vsb])
                if vwb is not None:
                    CP("pool", Vwh[0][0:nt, vtile, :, 0:64], zsb[0:nt, 1152:1280].rearrange("p (h d) -> p h d", h=2),
                       [b_zsb], [vwb])
                if stage < 0.245:
                    return None, None
                idn = ident_b[0:nt, 0:nt]
                bk, bbk = abank()
                pv = bk[:].bitcast(BF16)
                for h in range(8):
                    TR(pv[0:64, h * 128:h * 128 + nt], tb[0:nt, h * 64:(h + 1) * 64], idn, [b_tb, b_identb], [bbk])
                CP("act", QS[0:64, :, :, qcol0:qcol0 + nt],
                   pv[0:64, :].rearrange("p (k g t) -> p k g t", k=2, g=4)[:, :, :, 0:nt], [bbk], b_qs)
                bk, bbk = abank()
                pv = bk[:].bitcast(BF16)
                for h in range(8):
                    TR(pv[0:64, h * 128:h * 128 + nt], tbq[0:nt, h * 64:(h + 1) * 64], idn, [b_tbq, b_identb], [bbk])
                CP("dve", QC[0:64, :, :, qcol0:qcol0 + nt],
                   pv[0:64, :].rearrange("p (k g t) -> p k g t", k=2, g=4)[:, :, :, 0:nt], [bbk], b_qc)
                bk, bbk = abank()
                pv = bk[:].bitcast(BF16)
                for h in range(4):
                    TR(pv[0:64, h * 128:h * 128 + nt], tb[0:nt, 512 + h * 64:512 + (h + 1) * 64], idn,
                       [b_tb, b_identb], [bbk])
                if do_kvc:
                    for h in range(2):
                        TR(pv[0:128, (4 + h) * 128:(4 + h) * 128 + nt], tb[0:nt, 768 + h * 128:768 + (h + 1) * 128], idn,
                           [b_tb, b_identb], [bbk])
                kt = kcol0 // 128
                pvr = pv[:, :].rearrange("p (s t) -> p s t", s=8)
                if knew is not None:
                    KN_, KWN_, bkn = knew
                    CP("act", KN_[0:64, :, 0:nt], pvr[0:64, 0:2, 0:nt], [bbk], [bkn])
                    CP("dve", KWN_[0:64, :, 0:nt], pvr[0:64, 2:4, 0:nt], [bbk], [bkn])
                    return pvr, bbk
                CP("act", KS[0:64, :, kcol0:kcol0 + nt], pvr[0:64, 0:2, 0:nt], [bbk], [b_ks[0][kt], b_ks[1][kt]])
                if kt < 16:
                    CP("dve", KWh[0][0:64, :, kcol0:kcol0 + nt], pvr[0:64, 2:4, 0:nt], [bbk], [b_kw[0][kt], b_kw[1][kt]])
                if do_kvc:
                    CP("act", kvcT[:, :, kcol0:kcol0 + nt], pvr[:, 4:6, 0:nt], [bbk], [b_kvc[kt]])
                return pvr, bbk

            def rglru_chunk(N, ub, mcol0, bmix, xp_views, b_xp, sample=None):
                bss, bbss = banks[5], bbank[5]
                rlist = [0, 1, 2, 3, 4, 6, 7]

                def rbank():
                    i = rbi[0]
                    rbi[0] = (i + 1) % len(rlist)
                    return banks[rlist[i]], bbank[rlist[i]]

                def partA(c):
                    bkx, bbx = rbank()
                    for kc in range(8):
                        MM(bkx[:, 0:N], Wfm[:, kc, 512 + c * 128:512 + (c + 1) * 128], uT[:, kc, 0:N], kc == 0, kc == 7,
                           ub + [b_wfm[kc]], [bbx])
                    bkg, bbg = rbank()
                    for kc in range(8):
                        MM(bkg[:, 0:N], Wfm[:, kc, c * 128:(c + 1) * 128], uT[:, kc, 0:N], kc == 0, kc == 7,
                           ub + [b_wfm[kc]], [bbg])
                    return bkx, bbx, bkg, bbg
                pa = partA(0)
                for c in range(4):
                    nxt = partA(c + 1) if c + 1 < 4 else None
                    bkx, bbx, bkg, bbg = pa
                    pa = nxt
                    xpv = xp_views[c]; bxp = b_xp[c]
                    if sample is None:
                        CP("act", xpv[:, 3:3 + N], bkx[:, 0:N], [bbx], [bxp])
                        taps = [xpv[:, j:j + N] for j in range(4)]
                        xco = xc[:, 0:N]
                    else:
                        CP("act", xpv[:, :, 3:7], bkx[:, 0:N].rearrange("p (s t) -> p s t", t=4), [bbx], [bxp])
                        taps = [xpv[:, :, j:j + 4] for j in range(4)]
                        xco = xc[:, 0:N].rearrange("p (s t) -> p s t", t=4)
                    ACT(gl[:, 0:N], bkg[:, 0:N], AF.Gelu_apprx_tanh, [bbg], [b_gl])
                    TS("dve", xco, taps[0], vcol(40, c * 4 + 0), vcol(24, c), ALU.mult, ALU.add, [bxp, b_vecs], [b_xc])
                    for j in range(1, 4):
                        STT(xco, taps[j], vcol(40, c * 4 + j), xco, ALU.mult, ALU.add, [bxp, b_vecs, b_xc], [b_xc])
                    if sample is None:
                        CP("pool", xpv[:, 0:3], xpv[:, N:N + 3], [bxp], [bxp])
                    CP("act", xcb[:, 0:N], xc[:, 0:N], [b_xc], [b_xcb])
                    bkr, bbr = rbank()
                    MM(bkr[:, 0:N], wra[:, c, :], xcb[:, 0:N], True, True, [b_wra, b_xcb], [bbr])
                    bki, bbi = rbank()
                    MM(bki[:, 0:N], wri[:, c, :], xcb[:, 0:N], True, True, [b_wri, b_xcb], [bbi])
                    ACT(rr[:, 0:N], bkr[:, 0:N], AF.Sigmoid, [bbr, b_vecs], [b_rr], bias=vcol(28, c))
                    ACT(ig[:, 0:N], bki[:, 0:N], AF.Sigmoid, [bbi, b_vecs], [b_ig], bias=vcol(32, c))
                    ACT(aa[:, 0:N], rr[:, 0:N], AF.Exp, [b_rr, b_nls], [b_aa], scale=nls[:, c:c + 1])
                    TT("pool", t1[:, 0:N], aa[:, 0:N], aa[:, 0:N], ALU.mult, [b_aa], [b_t1])
                    ACT(t1[:, 0:N], t1[:, 0:N], AF.Sqrt, [b_t1], [b_t1], bias=1.0, scale=-1.0)
                    TT("pool", bb_[:, 0:N], t1[:, 0:N], ig[:, 0:N], ALU.mult, [b_t1, b_ig], [b_bb])
                    TT("dve", bb_[:, 0:N], bb_[:, 0:N], xc[:, 0:N], ALU.mult, [b_bb, b_xc], [b_bb])
                    if sample is None:
                        P.op("dve", lambda e, c=c: e.tensor_tensor_scan(out=hh[:, 0:N], data0=aa[:, 0:N], data1=bb_[:, 0:N],
                                                                          initial=hcar[:, c:c + 1], op0=ALU.mult, op1=ALU.add),
                             [b_aa, b_bb, b_hcar], [b_hh])
                        CP("pool", hcar[:, c:c + 1], hh[:, N - 1:N], [b_hh], [b_hcar])
                    else:
                        h0T, b_h0 = sample
                        a3 = aa[:, 0:N].rearrange("p (s t) -> p s t", t=4)
                        b3 = bb_[:, 0:N].rearrange("p (s t) -> p s t", t=4)
                        CP("pool", asc[:, 0:N], aa[:, 0:N], [b_aa], [b_asc])
                        MEMSET("pool", asc[:, 0:N].rearrange("p (s t) -> p s t", t=4)[:, :, 0:1], 0.0, [b_asc])
                        TT("dve", t1[:, 0:NS], a3[:, :, 0], h0T[:, c, :], ALU.mult, [b_aa, b_h0, b_t1], [b_t1])
                        TT("dve", b3[:, :, 0], b3[:, :, 0], t1[:, 0:NS], ALU.add, [b_bb, b_t1], [b_bb])
                        P.op("dve", lambda e: e.tensor_tensor_scan(out=hh[:, 0:N], data0=asc[:, 0:N], data1=bb_[:, 0:N],
                                                                    initial=0.0, op0=ALU.mult, op1=ALU.add),
                             [b_asc, b_bb], [b_hh])
                        bkh, bbh = rbank()
                        TR(bkh[0:NS, 0:128], hh[:, 3:N:4], ident_f[:, :], [b_hh, b_identf], [bbh])
                        CP("act", outst[0:NS, c * 128:(c + 1) * 128], bkh[0:NS, 0:128], [bbh], [b_outst])
                    TT("dve", yy[:, c, 0:N], gl[:, 0:N], hh[:, 0:N], ALU.mult, [b_gl, b_hh], [b_yy[c]])
                    TT("pool", ysq[:, 0:N], yy[:, c, 0:N], yy[:, c, 0:N], ALU.mult, [b_yy[c]], [b_ysq])
                    MM(bss[:, 0:N], ones_f[:, :], ysq[:, 0:N], c == 0, c == 3, [b_ones, b_ysq], [bbss])
                ACT(rsd[:, 0:N], bss[:, 0:N], AF.Sqrt, [bbss], [b_rsd], bias=EPS, scale=1.0 / 512)
                P.op("dve", lambda e: e.reciprocal(out=rsd[:, 0:N], in_=rsd[:, 0:N]), [b_rsd], [b_rsd])
                for c in range(4):
                    STT(mixedT[:, 4 + c, mcol0:mcol0 + N], yy[:, c, 0:N], vcol(20, c), rsd[:, 0:N], ALU.mult, ALU.mult,
                        [b_yy[c], b_vecs, b_rsd], [bmix])

            def compress(c0, nblk, rbufs):
                for w in range(2):
                    bk, bbk = abank()
                    for j in range(32):
                        lo = 16 * c0 + j
                        MM(bk[:, 0:nblk], W1[w][:, j, :], kvcT[:, w, lo:lo + 16 * (nblk - 1) + 1:16], j == 0, j == 31,
                           [b_w1[w]] + rbufs, [bbk])
                    ACT(hid[w][:, c0:c0 + nblk], bk[:, 0:nblk], AF.Gelu_apprx_tanh, [bbk, b_pec], [b_hid[w]],
                        bias=pec[:, w:w + 1])
                bk, bbk = abank()
                for h in range(2):
                    MM(bk[0:64, h * 128:(h + 1) * 128], W2[0][:, h, :], hid[0][:, :], True, True, [b_w2[0], b_hid[0]], [bbk])
                CP("act", KcT[:, :, :], bk[0:64, 0:256].rearrange("p (h c) -> p h c", h=2), [bbk], [b_kct])
                bk, bbk = abank()
                for h in range(2):
                    MM(bk[:, h * 64:(h + 1) * 64], hid[1][:, :], W2[1][:, h, :], h == 0, True, [b_w2[1], b_hid[1]], [bbk])
                CP("dve", Vc[:, :, 0:64], bk[:, 0:128].rearrange("p (h f) -> p h f", h=2), [bbk], [b_vc])

            def nextP():
                i = pbi[0]
                pbi[0] = (i + 1) % NPB
                return Pb[i], b_pb[i]

            def run_steps(steps, nq, qs_src, kdim, qcol0, rq, acc, bacc, merge=False):
                n = len(steps)
                W = 4 * nq
                sb = [(banks[3], bbank[3]), (banks[4], bbank[4]), (banks[2], bbank[2])]
                def emitS(i):
                    lhsT, nk, vap, mask, rb = steps[i]
                    bk, bbk = sb[i % 3]
                    MM(bk[0:nk, 0:W], lhsT, qs_src[0:kdim, :, qcol0:qcol0 + nq], True, True, rb + rq, [bbk])
                emitS(0)
                if n > 1:
                    emitS(1)
                for i in range(n):
                    if i + 2 < n:
                        emitS(i + 2)
                    lhsT, nk, vap, mask, rb = steps[i]
                    bk, bbk = sb[i % 3]
                    pt, bpt = nextP()
                    ACT(pt[0:nk, 0:W], bk[0:nk, 0:W], AF.Exp, [bbk], [bpt], scale=SCALE)
                    if mask is not None:
                        map_, mb = mask
                        p3 = pt[0:nk, 0:W].rearrange("p (g t) -> p g t", g=4)
                        TT("dve" if nq == 128 else "pool", p3, p3, map_.unsqueeze(1).broadcast_to([nk, 4, nq]), ALU.mult, [bpt, mb], [bpt])
                    ncol = vap.shape[-1]
                    if merge:
                        MM(acc[0:W, 0:ncol], pt[0:nk, 0:W], vap, i == 0, i == n - 1, [bpt] + rb, [bacc])
                        continue
                    for g in range(4):
                        MM(acc[0:nq, g * ncol:(g + 1) * ncol], pt[0:nk, g * nq:(g + 1) * nq], vap,
                           (i == 0 and g == 0), (i == n - 1), [bpt] + rb, [bacc])

            def attn_tile(nq, qcol0, cmp_mask, cmp_r, slc_steps, win_steps, selV, selC, b_sel, g_ap, bg, mcol0, bmix):
                idf = ident_f[0:nq, 0:nq]
                merge = (nq == 4)
                for kv in range(2):
                    steps = [(KcT[0:64, kv, 0:127], 127, Vc[0:127, kv, :], cmp_mask, [b_kct, b_vc, b_vcc] + cmp_r)]
                    run_steps(steps, nq, QC[:, kv], 64, qcol0, [b_qc[kv]], banks[5], bbank[5])
                    o = kv * 8
                    sbp, b_sbp = sbp2[kv], b_sbp2[kv]
                    CP("act", occ[0:nq, :, :], banks[5][0:nq, 0:392].rearrange("p (g c) -> p g c", g=4), [bbank[5]], [b_occ])
                    if kv == 0:
                        CP("dve", otm[0:nq, :].rearrange("p (g d) -> p g d", g=4), occ[0:nq, :, 0:64], [b_occ], [b_otm])
                    TS("dve", sm[0:nq, o:o + 4], occ[0:nq, :, 64], 1e-30, None, ALU.max, None, [b_occ], [b_sm])
                    P.op("dve", lambda e, o=o: e.reciprocal(out=sm[0:nq, o:o + 4], in_=sm[0:nq, o:o + 4]), [b_sm], [b_sm])
                    TT("dve", impt[0:nq], occ[0:nq, :, 65:98], sm[0:nq, o:o + 4].unsqueeze(2).broadcast_to([nq, 4, 33]),
                       ALU.mult, [b_occ, b_sm], [b_impt])
                    P.op("dve", lambda e: e.tensor_reduce(out=sc[0:nq, :], in_=impt[0:nq].rearrange("p g j -> p j g"),
                                                           axis=AX.X, op=ALU.add), [b_impt], [b_sc])
                    TT("dve", sc[0:nq, :], sc[0:nq, :], selV, ALU.mult, [b_sc, b_sel], [b_sc])
                    TT("dve", sc[0:nq, :], sc[0:nq, :], selC, ALU.add, [b_sc, b_sel], [b_sc])
                    P.op("dve", lambda e: e.max(out=m8[0:nq, 0:8], in_=sc[0:nq, :]), [b_sc], [b_m8])
                    P.op("dve", lambda e: e.match_replace(out=sc2[0:nq, :], in_to_replace=m8[0:nq, 0:8], in_values=sc[0:nq, :],
                                                           imm_value=-30000.0), [b_sc, b_m8], [b_sc2])
                    P.op("dve", lambda e: e.max(out=m8[0:nq, 8:16], in_=sc2[0:nq, :]), [b_sc2], [b_m8])
                    TS("dve", sbp[0:nq, 64:97], sc[0:nq, :], m8[0:nq, 15:16], BIGB, ALU.is_ge, ALU.mult, [b_sc, b_m8], [b_sbp])
                    TS("dve", sbp[0:nq, 64:97], sbp[0:nq, 64:97], -BIGB, None, ALU.add, None, [b_sbp], [b_sbp])
                wbank = [7, 5]
                sbank = [6, 7]
                for kv in range(2):
                    wb_ = wbank[kv]
                    run_steps(win_steps[kv], nq, QS[:, kv], 64, qcol0, [b_qs[kv]], banks[wb_], bbank[wb_], merge)
                    if merge:
                        CP("act", stg[0:16, kv * 2 + 1, :], banks[wb_][0:16, 0:65], [bbank[wb_]], [b_stg])
                    else:
                        CP("act", oall[0:nq, kv * 2 + 1], banks[wb_][0:nq, 0:260].rearrange("p (g c) -> p g c", g=4),
                           [bbank[wb_]], [b_oall[kv * 2 + 1]])
                for kv in range(2):
                    bk, bbk = abank()
                    TR(bk[0:97, 0:nq], sbp2[kv][0:nq, 0:97], idf, [b_sbp2[kv], b_identf], [bbk])
                    CP("dve", QS[64:97, kv, :, qcol0:qcol0 + nq], bk[64:97, 0:nq].unsqueeze(1).broadcast_to([33, 4, nq]),
                       [bbk], [b_qs[kv]])
                for kv in range(2):
                    sb_ = sbank[kv]
                    run_steps(slc_steps[kv], nq, QS[:, kv], 97, qcol0, [b_qs[kv]], banks[sb_], bbank[sb_], merge)
                    if merge:
                        CP("act", stg[0:16, kv * 2, :], banks[sb_][0:16, 0:65], [bbank[sb_]], [b_stg])
                    else:
                        CP("act", oall[0:nq, kv * 2], banks[sb_][0:nq, 0:260].rearrange("p (g c) -> p g c", g=4),
                           [bbank[sb_]], [b_oall[kv * 2]])
                if merge:
                    for g in range(4):
                        DMA("sp", oall[0:4, :, g, :], stg[4 * g:4 * g + 4, :, :], [b_stg], list(b_oall))
                return lambda: attn_epi(nq, g_ap, bg, mcol0, bmix)

            def attn_epi(nq, g_ap, bg, mcol0, bmix):
                for kv in range(2):
                    osl = oall[:, kv * 2]; owi = oall[:, kv * 2 + 1]
                    b_osl = b_oall[kv * 2]; b_owi = b_oall[kv * 2 + 1]
                    o = kv * 8
                    c1 = 16 + kv * 16
                    P.op("dve", lambda e, c1=c1, osl=osl: e.reciprocal(out=sm[0:nq, c1 + 4:c1 + 8], in_=osl[0:nq, :, 64]), [b_osl], [b_sm])
                    P.op("dve", lambda e, c1=c1, owi=owi: e.reciprocal(out=sm[0:nq, c1 + 8:c1 + 12], in_=owi[0:nq, :, 64]), [b_owi], [b_sm])
                    TT("dve", sm[0:nq, c1:c1 + 4], sm[0:nq, o:o + 4], g_ap[:, 0 * 8 + kv * 4:0 * 8 + kv * 4 + 4], ALU.mult, [b_sm, bg], [b_sm])
                    TT("dve", sm[0:nq, c1 + 4:c1 + 8], sm[0:nq, c1 + 4:c1 + 8], g_ap[:, 8 + kv * 4:8 + kv * 4 + 4], ALU.mult, [b_sm, bg], [b_sm])
                    TT("dve", sm[0:nq, c1 + 8:c1 + 12], sm[0:nq, c1 + 8:c1 + 12], g_ap[:, 16 + kv * 4:16 + kv * 4 + 4], ALU.mult, [b_sm, bg], [b_sm])
                    ov4 = oat[0:nq, kv * 256:(kv + 1) * 256].rearrange("p (g d) -> p g d", g=4)
                    cmpnum = otm[0:nq, :].rearrange("p (g d) -> p g d", g=4) if kv == 0 else occ[0:nq, :, 0:64]
                    cmpb = b_otm if kv == 0 else b_occ
                    def bc(col):
                        return sm[0:nq, col:col + 4].unsqueeze(2).broadcast_to([nq, 4, 64])
                    tmpv = outst[0:nq, 0:256].rearrange("p (g d) -> p g d", g=4)
                    TT("dve", ov4, cmpnum, bc(c1), ALU.mult, [cmpb, b_sm], [b_oat])
                    TT("pool", tmpv, osl[0:nq, :, 0:64], bc(c1 + 4), ALU.mult, [b_osl, b_sm], [b_outst])
                    TT("dve", ov4, ov4, tmpv, ALU.add, [b_oat, b_outst], [b_oat])
                    TT("pool", tmpv, owi[0:nq, :, 0:64], bc(c1 + 8), ALU.mult, [b_owi, b_sm], [b_outst])
                    TT("dve", ov4, ov4, tmpv, ALU.add, [b_oat, b_outst], [b_oat])
                oab, b_oab = oab2[oabi[0]], b_oab2[oabi[0]]
                oabi[0] = 1 - oabi[0]
                STT(junk[0:nq, 0:512], oat[0:nq, :], 1.0, oat[0:nq, :], ALU.mult, ALU.mult, [b_oat], [b_junk, b_st4],
                    accum=st4[0:nq, 4:5])
                ACT(st4[0:nq, 5:6], st4[0:nq, 4:5], AF.Ln, [b_st4], [b_st4], bias=EPS, scale=1.0 / 512)
                ACT(st4[0:nq, 6:7], st4[0:nq, 5:6], AF.Exp, [b_st4], [b_st4], scale=-0.5)
                TS("dve", oab[0:nq, :], oat[0:nq, :], st4[0:nq, 6:7], None, ALU.mult, None, [b_oat, b_st4], [b_oab])

                def fin():
                    bk, bbk = abank()
                    pv = bk[:].bitcast(BF16).rearrange("p (c t) -> p c t", c=8)
                    for c in range(4):
                        TR(pv[:, c, 0:nq], oab[0:nq, c * 128:(c + 1) * 128], ident_b[0:nq, 0:nq], [b_oab, b_identb], [bbk])
                    TT("dve", mixedT[:, 0:4, mcol0:mcol0 + nq], pv[:, 0:4, 0:nq],
                       V_GATT.unsqueeze(2).broadcast_to([128, 4, nq]), ALU.mult, [bbk, b_vecs], [bmix])
                return fin

            with contextlib.ExitStack() as lp:
                cm = sbt(lp, "cm", [128, NT], BF16); b_cm = P.buf()
                xin.append(sbt(lp, "xin1", [128, 1024], F32))
                KWh[0] = sbt(lp, "KW", [64, 2, NT], BF16)
                Vwh[0] = sbt(lp, "Vw", [128, 16, 2, 65], BF16)
                MEMSET("pool", Vwh[0][:, :, :, 64:65], 1.0, [b_vw[0]])
                ropep = sbt(lp, "ropep", [128, 16, 16], F32); b_ropep = P.buf()
                selvp = sbt(lp, "selvp", [128, 16, 33], F32); selcp = sbt(lp, "selcp", [128, 16, 33], F32); b_selp = P.buf()
                xpp = sbt(lp, "xpp", [128, 4, CH + 4], F32); b_xpp = [P.buf() for _ in range(4)]
                DMA("pool", cm[:], cm_d, (), [b_cm])
                DMA("sp", ropep[:], ropep_d, (), [b_ropep])
                DMA("sp", selvp[:], selvp_d, (), [b_selp]); DMA("sp", selcp[:], selcp_d, (), [b_selp])
                for c in range(4):
                    MEMSET("pool", xpp[:, c, 0:3], 0.0, [b_xpp[c]])
                xpv = [xpp[:, c, :] for c in range(4)]
                nchunks = NCHUNK if stage >= 1 else 1
                pend = [None]
                for ch in range(nchunks):
                    if stage < 0.2:
                        break
                    ubs = []
                    for ti in range(2):
                        tt = ch * 2 + ti
                        ubs += load_norm_T(xp_d[tt * 128:(tt + 1) * 128, :], 128, tt % 2, ti * 128)
                    for ti in range(2):
                        if stage < 0.3:
                            break
                        tt = ch * 2 + ti
                        proj_tile(128, ti * 128, [b_uT[ti]], ropep[:, tt, :], b_ropep, ti, ti * 128, tt * 128, tt,
                                  kvp_d[tt * 128:(tt + 1) * 128, :],
                                  winp_d[(tt - 12) * 128:(tt - 11) * 128, :] if tt >= 12 else None)
                    if stage >= 0.5:
                        rglru_chunk(CH, list(b_uT), ch * CH, b_mix[ch], xpv, b_xpp)
                    if stage < 2:
                        continue
                    c0 = 0 if ch == 0 else 16 * ch - 1
                    c1 = 16 * ch + 14
                    kr = [b_kvc[i] for i in range(max(0, 2 * ch - 1), 2 * ch + 2)]
                    compress(c0, c1 - c0 + 1, kr)
                    if pend[0] is not None:
                        pend[0](); pend[0] = None
                    fins = []
                    for ti in range(2):
                        tt = ch * 2 + ti
                        slc, win = [], []
                        for kv in range(2):
                            s_ = []
                            for kt in range(tt + 1):
                                mk = (tri_le[:, :], b_trile) if kt == tt else None
                                s_.append((KS[0:97, kv, kt * 128:(kt + 1) * 128], 128, Vs[:, kt, kv, :], mk,
                                           [b_ks[kv][kt], b_ksE, b_vs[kt], b_vs[0]]))
                            slc.append(s_)
                            w_ = []
                            for kt in range(max(0, tt - 4), tt + 1):
                                mk = (tri_le[:, :], b_trile) if kt == tt else ((tri_gt[:, :], b_trigt) if kt == tt - 4 else None)
                                w_.append((KWh[0][0:64, kv, kt * 128:(kt + 1) * 128], 128, Vwh[0][:, kt, kv, :], mk,
                                           [b_kw[kv][kt], b_vw[kt], b_vw[0]]))
                            win.append(w_)
                        fins.append(attn_tile(128, ti * 128, (cm[0:127, tt * 128:(tt + 1) * 128], b_cm), [], slc, win,
                                              selvp[:, tt, :], selcp[:, tt, :], b_selp, gt[ti][:, :], b_gt[ti], tt * 128, b_mix[ch])())
                        if ti == 1:
                            fins[0]()
                            pend[0] = fins[1]
                if pend[0] is not None:
                    pend[0](); pend[0] = None
                bk, bbk = abank()
                for c in range(4):
                    TR(bk[0:3, c * 128:(c + 1) * 128], xpp[:, c, 0:3], ident_f[:, :], [b_xpp[c], b_identf], [bbk])
                CP("act", outst[0:3, 0:512], bk[0:3, 0:512], [bbk], [b_outst])
                DMA("sp", convp_d, outst[0:3, 0:512], [b_outst], ())
                bk, bbk = abank()
                TR(bk[0:4, 0:128], hcar[:, 0:4], ident_f[:, :], [b_hcar, b_identf], [bbk])
                CP("act", oat[0:4, 0:128], bk[0:4, 0:128], [bbk], [b_oat])
                DMA("sp", hp_d, oat[0:4, 0:128], [b_oat], ())
                P.emit()
            P.barrier()

            if stage >= 3:
              with contextlib.ExitStack() as sp_:
                raw = sbt(sp_, "raw", [128, 16, 512], BF16); b_raw = [P.buf() for _ in range(16)]
                KWh[0] = sbt(sp_, "KWs", [64, 2, 520], BF16)
                Vwh[0] = sbt(sp_, "Vws", [128, 5, 2, 65], BF16)
                MEMSET("pool", Vwh[0][:, :, :, 64:65], 1.0, [b_vw[0]])
                rawin = sbt(sp_, "rawin", [128, 4, 256], BF16); b_rawin = P.buf()
                ptb = sbt(sp_, "ptb", [128, 256], I32); b_ptb = P.buf()
                idx = sbt(sp_, "idx", [128, 256], I32); b_idx = P.buf()
                iop = sbt(sp_, "iop", [128, 1], F32); b_iop = P.buf()
                ropes = sbt(sp_, "ropes", [64, 16], F32); b_ropes = P.buf()
                KN = sbt(sp_, "KN", [128, 2, 64], BF16); KWN = sbt(sp_, "KWN", [64, 2, 64], BF16); b_kn = P.buf()
                vnew1 = sbt(sp_, "vnew", [4, 256], F32); vnew = [vnew1, vnew1]; b_vnew1 = P.buf(); b_vnew = [b_vnew1, b_vnew1]
                gsm = [sbt(sp_, "gsm%d" % i, [4, 24], F32) for i in range(2)]; b_gsm = [P.buf(), P.buf()]
                MEMSET("pool", KN[64:96, :, :], 0.0, [b_kn]); MEMSET("pool", KN[96:97, :, :], 1.0, [b_kn])
                selvs = sbt(sp_, "selvs", [4, 33], F32); selcs = sbt(sp_, "selcs", [4, 33], F32); b_sels = P.buf()
                sct = sbt(sp_, "sct", [48, 512], F32); b_sct = P.buf()
                srt = oat; b_srt = b_oat
                xps = sbt(sp_, "xps", [128, 4, NS, 7], F32); b_xps = [P.buf() for _ in range(4)]
                h0T = sbt(sp_, "h0T", [128, 4, NS], F32); b_h0 = P.buf()
                DMA("sp", ptb[:], ptab_d.broadcast_to([128, 256]), (), [b_ptb])
                DMA("sp", iop[:], iota_d, (), [b_iop])
                DMA("sp", ropes[:], ropes_d, (), [b_ropes])
                DMA("sp", selvs[:], selvs_d, (), [b_sels]); DMA("sp", selcs[:], selcs_d, (), [b_sels])
                DMA("sp", sct[:], sconv_d, (), [b_sct]); DMA("sp", srt[0:NS, :], srnn_d, (), [b_srt])
                TS("dve", idx[:], ptb[:], 128.0, iop[:, 0:1], ALU.mult, ALU.add, [b_ptb, b_iop], [b_idx])
                for c in range(4):
                    bk, bbk = abank()
                    TR(bk[:, 0:48], sct[0:48, c * 128:(c + 1) * 128], ident_f[0:48, 0:48], [b_sct, b_identf], [bbk])
                    CP("act", xps[:, c, :, 0:3], bk[:, 0:48].rearrange("p (s j) -> p s j", j=3), [bbk], [b_xps[c]])
                    bk, bbk = abank()
                    TR(bk[:, 0:NS], srt[0:NS, c * 128:(c + 1) * 128], ident_f[0:NS, 0:NS], [b_srt, b_identf], [bbk])
                    CP("act", h0T[:, c, :], bk[:, 0:NS], [bbk], [b_h0])
                ub = load_norm_T(xs_d, 64, 0, 0)
                rglru_chunk(64, ub, NT, b_mix[NCHUNK], [xps[:, c] for c in range(4)], b_xps, sample=(h0T, b_h0))
                DMA("sp", hs_d, outst[0:NS, 0:512], [b_outst], ())
                proj_tile(64, 0, ub, ropes[:, :], b_ropes, 0, 0, 2048, 16, kvs_d, None, do_kvc=False, knew=(KN, KWN, b_kn))
                convs3 = convs_d.rearrange("(s j) c -> s j c", j=3)
                dsts = [(xin[0][0:NS, 0:512], b_xin[0]), (xin[0][0:NS, 512:1024], b_xin[0]), (oat[0:NS, :], b_oat)]
                for j in range(3):
                    bk, bbk = abank()
                    for c in range(4):
                        TR(bk[0:NS, c * 128:(c + 1) * 128], xps[:, c, :, 4 + j], ident_f[:, :], [b_xps[c], b_identf], [bbk])
                    CP("act", dsts[j][0], bk[0:NS, 0:512], [bbk], [dsts[j][1]])
                    DMA("sp", convs3[:, j, :], dsts[j][0], [dsts[j][1]], ())
                def issue_gathers(s):
                    for j in range(16):
                        col = s * 16 + j
                        P.dma("pool", lambda e, j=j, col=col: e.indirect_dma_start(
                            out=raw[:, j, :], out_offset=None, in_=ckv_d,
                            in_offset=bass.IndirectOffsetOnAxis(ap=idx[:, col:col + 1], axis=0)),
                            [b_idx], [b_raw[j]])
                    DMA("pool", rawin[:], cwin_d[s].rearrange("(t p) c -> p t c", p=128), (), [b_rawin])
                    DMA("act", wins_d[s, 0:508, :], cwin_d[s, 4:512, :], (), ())
                def prep(s):
                        for j in range(16):
                            bk, bbk2 = abank()
                            pv = bk[:].bitcast(BF16).rearrange("p (s t) -> p s t", s=8)
                            for h in range(2):
                                TR(pv[0:64, h, :], raw[:, j, 256 + h * 64:256 + (h + 1) * 64], ident_b[:, :], [b_raw[j], b_identb], [bbk2])
                            TR(pv[:, 2, :], raw[:, j, 0:128], ident_b[:, :], [b_raw[j], b_identb], [bbk2])
                            TR(pv[:, 3, :], raw[:, j, 128:256], ident_b[:, :], [b_raw[j], b_identb], [bbk2])
                            CP("act", KS[0:64, :, j * 128:(j + 1) * 128], pv[0:64, 0:2, :], [bbk2], [b_ks[0][j], b_ks[1][j]])
                            CP("dve", kvcT[:, :, j * 128:(j + 1) * 128], pv[:, 2:4, :], [bbk2], [b_kvc[j]])
                            CP("pool", Vs[:, j, :, 0:64], raw[:, j, 384:512].rearrange("p (h d) -> p h d", h=2), [b_raw[j]], [b_vs[j]])
                        for j in range(4):
                            bk, bbk2 = abank()
                            pv = bk[:].bitcast(BF16).rearrange("p (s t) -> p s t", s=8)
                            for h in range(2):
                                TR(pv[0:64, h, :], rawin[:, j, h * 64:(h + 1) * 64], ident_b[:, :], [b_rawin, b_identb], [bbk2])
                            CP("act", KWh[0][0:64, :, j * 128:(j + 1) * 128], pv[0:64, 0:2, :], [bbk2], [b_kw[0][j], b_kw[1][j]])
                            CP("pool", Vwh[0][:, j, :, 0:64], rawin[:, j, 128:256].rearrange("p (h d) -> p h d", h=2), [b_rawin], [b_vw[j]])
                        if s + 1 < NS:
                            issue_gathers(s + 1)
                        compress(0, 127, list(b_kvc))
                pend = [None]
                issue_gathers(0)
                prep(0)
                for s in range(NS):
                    vn, bvn = vnew[s % 2], b_vnew[s % 2]
                    gs_, bgs = gsm[s % 2], b_gsm[s % 2]
                    DMA("sp", vn[0:4, :], zsb[s * 4:(s + 1) * 4, 1024:1280], [b_zsb], [bvn])
                    DMA("sp", gs_[0:4, :], gt[0][s * 4:(s + 1) * 4, :], [b_gt[0]], [bgs])
                    CP("pool", Vs[0:4, 16, :, 0:64], vn[0:4, 0:128].rearrange("p (h d) -> p h d", h=2), [bvn], [b_vs[16]])
                    CP("pool", Vwh[0][0:4, 4, :, 0:64], vn[0:4, 128:256].rearrange("p (h d) -> p h d", h=2), [bvn], [b_vw[4]])
                    DMA("act", wins_d[s, 508:512, 0:128], zsb[s * 4:(s + 1) * 4, 640:768], [b_zsb], ())
                    DMA("act", wins_d[s, 508:512, 128:256], zsb[s * 4:(s + 1) * 4, 1152:1280], [b_zsb], ())
                    if pend[0] is not None:
                        pend[0](); pend[0] = None
                    slc, win = [], []
                    for kv in range(2):
                        s_ = []
                        for kt in range(16):
                            s_.append((KS[0:97, kv, kt * 128:(kt + 1) * 128], 128, Vs[:, kt, kv, :], None,
                                       [b_ks[kv][kt], b_ksE, b_vs[kt], b_vs[0]]))
                        s_.append((KN[0:97, kv, s * 4:(s + 1) * 4], 4, Vs[0:4, 16, kv, :], (tri_le[0:4, 0:4], b_trile),
                                   [b_kn, b_vs[16], b_vs[0]]))
                        slc.append(s_)
                        w_ = []
                        for kt in range(4):
                            mk = (tri_gt[:, 0:4], b_trigt) if kt == 0 else None
                            w_.append((KWh[0][0:64, kv, kt * 128:(kt + 1) * 128], 128, Vwh[0][:, kt, kv, :], mk,
                                       [b_kw[kv][kt], b_vw[kt], b_vw[0]]))
                        w_.append((KWN[0:64, kv, s * 4:(s + 1) * 4], 4, Vwh[0][0:4, 4, kv, :], (tri_le[0:4, 0:4], b_trile),
                                   [b_kn, b_vw[4], b_vw[0]]))
                        win.append(w_)
                    epi_ = attn_tile(4, s * 4, None, [], slc, win, selvs[:, :], selcs[:, :], b_sels, gs_[0:4, :], bgs,
                                        NT + s * 4, b_mix[NCHUNK])
                    if s + 1 < NS:
                        prep(s + 1)
                    pend[0] = epi_()
                if pend[0] is not None:
                    pend[0](); pend[0] = None
                P.emit()
        P.barrier()

        if stage >= 4:
          with contextlib.ExitStack() as cs:
            NTI = 17
            x1 = sbt(cs, "x1", [128, NTI, 1024], F32); b_x1 = [P.buf() for _ in range(NTI)]
            vT = sbt(cs, "vT", [128, 8, NTOK], BF16); b_vT = [P.buf() for _ in range(NTI)]
            wout = sbt(cs, "wout", [128, 8, 1024], BF16); b_wout = [P.buf() for _ in range(8)]
            nfin = sbt(cs, "nfin", [128, 1024], F32); b_nfin = P.buf()
            wu = [sbt(cs, "wu%d" % i, [128, 8, 512], BF16) for i in range(2)]; b_wu = [[P.buf() for _ in range(8)] for _ in range(2)]
            wd = [sbt(cs, "wd%d" % i, [128, 4, 1024], BF16) for i in range(2)]; b_wd = [[P.buf() for _ in range(4)] for _ in range(2)]
            hT = [sbt(cs, "hT%d" % i, [128, 4, 512], BF16) for i in range(2)]; b_hT = [P.buf(), P.buf()]
            rl = [sbt(cs, "rl%d" % i, [128, 512], F32) for i in range(2)]; b_rl = [P.buf(), P.buf()]
            xnc = sbt(cs, "xnc", [128, 1024], BF16); b_xnc = P.buf()
            st5 = sbt(cs, "st5", [128, 8], F32); b_st5 = P.buf()
            yb1 = sbt(cs, "yb", [128, 1024], F32); yb = [yb1, yb1]; b_yb1 = P.buf(); b_yb = [b_yb1, b_yb1]
            for kc in range(8):
                DMA("pool", wout[:, kc, :], wout_d[kc * 128:(kc + 1) * 128, :], (), [b_wout[kc]])
            DMA("sp", nfin[:], nfin_d.broadcast_to([128, 1024]), (), [b_nfin])

            def load_w(q, bi):
                for kc in range(8):
                    DMA("pool", wu[bi][:, kc, :], wup_d[kc * 128:(kc + 1) * 128, q * 512:(q + 1) * 512], (), [b_wu[bi][kc]])
                for fc in range(4):
                    DMA("pool", wd[bi][:, fc, :], wdown_d[q * 512 + fc * 128:q * 512 + (fc + 1) * 128, :], (), [b_wd[bi][fc]])
            load_w(0, 0)

            def tile_info(ti):
                if ti < 16:
                    return 128, ti * 128, xp_d[ti * 128:(ti + 1) * 128, :], yp_d[ti * 128:(ti + 1) * 128, :], ti // 2
                return 64, NT, xs_d, ys_d, NCHUNK

            for ti in range(NTI):
                nt, col0, xsrc, ydst, mch = tile_info(ti)
                DMA("sp", x1[0:nt, ti, :], xsrc, (), [b_x1[ti]])
                for half in range(2):
                    bk, bbk = abank()
                    for kc in range(8):
                        MM(bk[0:nt, 0:512], mixedT[:, kc, col0:col0 + nt], wout[:, kc, half * 512:(half + 1) * 512],
                           kc == 0, kc == 7, [b_mix[mch], b_wout[kc]], [bbk])
                    TT("dve", x1[0:nt, ti, half * 512:(half + 1) * 512], x1[0:nt, ti, half * 512:(half + 1) * 512],
                       bk[0:nt, 0:512], ALU.add, [b_x1[ti], bbk], [b_x1[ti]])
                ACT(yb1[0:nt, :], x1[0:nt, ti, :], AF.Square, [b_x1[ti]], [b_yb1, b_st5], accum=st5[0:nt, 0:1])
                ACT(st5[0:nt, 1:2], st5[0:nt, 0:1], AF.Sqrt, [b_st5], [b_st5], bias=EPS, scale=1.0 / 1024)
                P.op("dve", lambda e, nt=nt: e.reciprocal(out=st5[0:nt, 2:3], in_=st5[0:nt, 1:2]), [b_st5], [b_st5])
                TS("dve", xnc[0:nt, :], x1[0:nt, ti, :], st5[0:nt, 2:3], None, ALU.mult, None, [b_x1[ti], b_st5], [b_xnc])
                bk, bbk = abank()
                pv = bk[:].bitcast(BF16).rearrange("p (c t) -> p c t", c=8)
                for c in range(8):
                    TR(pv[:, c, 0:nt], xnc[0:nt, c * 128:(c + 1) * 128], ident_b[0:nt, 0:nt], [b_xnc, b_identb], [bbk])
                TT("dve", vT[:, :, col0:col0 + nt], pv[:, :, 0:nt], V_NMLP.unsqueeze(2).broadcast_to([128, 8, nt]),
                   ALU.mult, [bbk, b_vecs], [b_vT[ti]])

            tchunks = [(0, 512, [0, 1, 2, 3]), (512, 512, [4, 5, 6, 7]), (1024, 512, [8, 9, 10, 11]),
                       (1536, 512, [12, 13, 14, 15]), (2048, 64, [16])]
            items = [(q, ci) for q in range(8) for ci in range(len(tchunks))]

            def up(i):
                q, ci = items[i]
                bi = q % 2
                c0, ncol, tiles = tchunks[ci]
                hb, bhb = hT[i % 2], b_hT[i % 2]
                for fc in range(4):
                    bk, bbk = abank()
                    for kc in range(8):
                        MM(bk[:, 0:ncol], wu[bi][:, kc, fc * 128:(fc + 1) * 128], vT[:, kc, c0:c0 + ncol], kc == 0, kc == 7,
                           [b_wu[bi][kc]] + [b_vT[t] for t in tiles], [bbk])
                    r_, br_ = rl[fc % 2], b_rl[fc % 2]
                    ACT(r_[:, 0:ncol], bk[:, 0:ncol], AF.Relu, [bbk], [br_])
                    TT("pool", hb[:, fc, 0:ncol], r_[:, 0:ncol], r_[:, 0:ncol], ALU.mult, [br_], [bhb])

            def down(i):
                q, ci = items[i]
                bi = q % 2
                c0, ncol, tiles = tchunks[ci]
                hb, bhb = hT[i % 2], b_hT[i % 2]
                for k, ti in enumerate(tiles):
                    nt = 128 if ti < 16 else 64
                    for half in range(2):
                        bk, bbk = banks[3 + (2 * k + half) % 4], bbank[3 + (2 * k + half) % 4]
                        for fc in range(4):
                            MM(bk[0:nt, 0:512], hb[:, fc, k * 128:k * 128 + nt], wd[bi][:, fc, half * 512:(half + 1) * 512],
                               fc == 0, fc == 3, [bhb, b_wd[bi][fc]], [bbk])
                        TT("dve", x1[0:nt, ti, half * 512:(half + 1) * 512], x1[0:nt, ti, half * 512:(half + 1) * 512],
                           bk[0:nt, 0:512], ALU.add, [b_x1[ti], bbk], [b_x1[ti]])
            load_w(1, 1)
            up(0)
            for i in range(len(items)):
                if i + 1 < len(items):
                    up(i + 1)
                down(i)
                q_, ci_ = items[i]
                if ci_ == len(tchunks) - 1 and q_ + 2 < 8:
                    load_w(q_ + 2, q_ % 2)
            for ti in range(NTI):
                nt, col0, xsrc, ydst, mch = tile_info(ti)
                ACT(xnc[0:nt, :], x1[0:nt, ti, :], AF.Square, [b_x1[ti]], [b_xnc, b_st5], accum=st5[0:nt, 4:5])
                ACT(st5[0:nt, 5:6], st5[0:nt, 4:5], AF.Sqrt, [b_st5], [b_st5], bias=EPS, scale=1.0 / 1024)
                P.op("dve", lambda e, nt=nt: e.reciprocal(out=st5[0:nt, 6:7], in_=st5[0:nt, 5:6]), [b_st5], [b_st5])
                y_, by_ = yb[ti % 2], b_yb[ti % 2]
                STT(y_[0:nt, :], x1[0:nt, ti, :], st5[0:nt, 6:7], nfin[0:nt, :], ALU.mult, ALU.mult,
                    [b_x1[ti], b_st5, b_nfin], [by_])
                DMA("sp", ydst, y_[0:nt, :], [by_], ())
            P.finish()
            P.emit()
        else:
            P.finish()
            P.emit()
    nc._used_inputs = used_inputs
    return nc, used_inputs


_STAGE = 99


def _consts():
    f32 = np.float32
    c = {}
    c["identf"] = np.eye(128, dtype=f32)
    k = np.arange(128)
    c["trile"] = (k[:, None] <= k[None, :]).astype(f32)
    c["trigt"] = (k[:, None] > k[None, :]).astype(f32)
    cc = np.arange(128)[:, None]
    t = np.arange(NT)[None, :]
    cm = ((16 * cc + 31) <= t).astype(f32)
    cm[127] = 0.0
    c["cm"] = cm
    j = np.arange(33)[:, None]
    kk = np.arange(2056)[None, :]
    c["ec"] = ((kk // 64) == j).astype(f32)
    c0 = (np.arange(128) * 16)[:, None]
    j0 = (np.arange(33) * 64)[None, :]
    ov = ((c0 < j0 + 64) & (c0 + 32 > j0)).astype(f32)
    ov[127] = 0.0
    c["ov"] = ov
    inv = (np.float32(500000.0) ** (-np.arange(8, dtype=f32) / np.float32(8))).astype(f32)

    def rope(pos):
        ang = pos.astype(f32)[:, None] * inv[None, :]
        return np.concatenate([np.cos(ang), np.sin(ang)], axis=1).astype(f32)
    pos = (np.arange(16)[None, :] * 128 + np.arange(128)[:, None])
    c["ropep"] = rope(pos.reshape(-1)).reshape(128, 16, 16)
    c["ropes"] = rope(2048 + (np.arange(64) % 4))

    def sel(tpos, nsel):
        cur = (tpos // 64)[:, None]
        jj = np.arange(33)[None, :]
        valid = (jj <= cur) & (jj < nsel)
        forced = ((jj == 0) | (jj == cur) | (jj == cur - 1)) & (jj < nsel)
        V = valid.astype(f32)
        C = ((V - 1.0) * 1e4 + forced.astype(f32) * 1e4).astype(f32)
        return V, C
    V, C = sel(pos.reshape(-1), 32)
    c["selvp"] = V.reshape(128, 16, 33); c["selcp"] = C.reshape(128, 16, 33)
    V, C = sel(2048 + np.arange(4), 33)
    c["selvs"] = V; c["selcs"] = C
    c["iotap"] = np.arange(128, dtype=f32).reshape(128, 1)
    return c


_NC_CACHE = {}


def kernel(**inp):
    f32 = np.float32
    g = lambda k: np.asarray(inp[k])
    w_in = g("w_in")[0].astype(f32)
    r = np.arange
    cols = np.concatenate([r(0, 512), r(768, 896), r(1024, 1152), r(1280, 1304), r(512, 640), r(640, 768),
                           r(896, 1024), r(1152, 1280)])
    shared = {}
    shared["wtm"] = np.ascontiguousarray(w_in[:, cols])
    shared["wfm"] = np.ascontiguousarray(w_in[:, 1304:2328])
    shared["wout"] = np.ascontiguousarray(g("w_out")[0].astype(f32))
    shared["wup"] = np.ascontiguousarray(g("w_up")[0].astype(f32))
    shared["wdown"] = np.ascontiguousarray(g("w_down")[0].astype(f32))
    for nm, w1n, w2n, pen in (("k", "w_ck1", "w_ck2", "pe_ck"), ("v", "w_cv1", "w_cv2", "pe_cv")):
        w1 = g(w1n)[0].astype(f32); w2 = g(w2n)[0].astype(f32); pe = g(pen)[0].astype(f32)
        w1bd = np.zeros((128, 32, 128), f32); w2p = np.zeros((128, 2, 64), f32); pe2 = np.zeros((128, 32), f32)
        for h in range(2):
            w1bd[h * 64:(h + 1) * 64, :, h * 64:(h + 1) * 64] = w1.transpose(1, 0, 2)
            w2p[h * 64:(h + 1) * 64, h, :] = w2
            pe2[h * 64:(h + 1) * 64, :] = pe.T
        shared["w1" + nm] = w1bd; shared["w2" + nm] = w2p; shared["pe" + nm] = pe2
    for nm, wn in (("wra", "w_ra"), ("wri", "w_ri")):
        w = g(wn)[0].astype(f32)
        bd = np.zeros((128, 4, 128), f32)
        for c in range(4):
            for b in range(2):
                bd[b * 64:(b + 1) * 64, c, b * 64:(b + 1) * 64] = w[2 * c + b]
        shared[nm] = bd
    vecs = np.zeros((128, 56), f32)
    T8 = lambda v: np.asarray(v, f32).reshape(-1, 128).T
    vecs[:, 0:8] = T8(g("norm_mix")[0]); vecs[:, 8:16] = T8(g("norm_mlp")[0])
    vecs[:, 16:20] = T8(g("g_attn")[0]); vecs[:, 20:24] = T8(g("g_rnn")[0])
    vecs[:, 24:28] = T8(g("conv_b")[0]); vecs[:, 28:32] = T8(g("b_ra")[0]); vecs[:, 32:36] = T8(g("b_ri")[0])
    vecs[:, 36:40] = T8(g("lam")[0])
    cw = g("conv_w")[0].astype(f32)
    for c in range(4):
        for tap in range(4):
            vecs[:, 40 + c * 4 + tap] = cw[tap, c * 128:(c + 1) * 128]
    shared["vecs"] = vecs
    shared["nfin"] = np.asarray(g("norm_final"), f32).reshape(1, 1024)
    shared["ckv"] = np.ascontiguousarray(g("cache_kv")[0].astype(f32).reshape(2560 * 128, 512))
    shared.update(_consts())

    xp = g("x_prompt").astype(f32); xs = g("x_sample").astype(f32)
    cwin = g("cache_win")[0].astype(f32).reshape(128, 512, 256)
    sconv = g("state_conv")[0].astype(f32); srnn = g("state_rnn")[0].astype(f32)
    pt = g("page_table").astype(np.int32)
    in_maps = []
    for c in range(8):
        m = dict(shared)
        m["xp"] = np.ascontiguousarray(xp[c])
        m["xs"] = np.ascontiguousarray(xs[c * 16:(c + 1) * 16].reshape(64, 1024))
        m["cwin"] = np.ascontiguousarray(cwin[c * 16:(c + 1) * 16])
        m["sconv"] = np.ascontiguousarray(sconv[c * 16:(c + 1) * 16].reshape(48, 512))
        m["srnn"] = np.ascontiguousarray(srnn[c * 16:(c + 1) * 16])
        m["ptab"] = np.ascontiguousarray(pt[c * 16:(c + 1) * 16].reshape(1, 256))
        in_maps.append(m)
    if _STAGE not in _NC_CACHE:
        _NC_CACHE[_STAGE] = build_nc(_STAGE)
    nc, used = _NC_CACHE[_STAGE]
    in_maps = [{k: v for k, v in m.items() if k in used} for m in in_maps]
    res = run_bass_kernel_spmd(nc, in_maps, core_ids=list(range(8)))
    R = res.results
    cat = lambda k: np.stack([np.asarray(R[c][k], f32) for c in range(8)])
    y_p = cat("y_p")
    y_s = cat("y_s").reshape(128, 4, 1024)
    kv_p = cat("kv_p").reshape(1, 8, 2048, 4, 2, 64)
    kv_s = cat("kv_s").reshape(1, 128, 4, 4, 2, 64)
    win_p = cat("win_p").reshape(1, 8, 512, 2, 2, 64)
    win_s = cat("win_s").reshape(1, 128, 512, 2, 2, 64)
    conv_p = cat("conv_p").reshape(1, 8, 3, 512)
    conv_s = cat("conv_s").reshape(1, 128, 3, 512)
    h_p = cat("h_p").reshape(1, 8, 512)
    h_s = cat("h_s").reshape(1, 128, 512)
    return (y_p, y_s, kv_p, kv_s, win_p, win_s, conv_p, conv_s, h_p, h_s)
```

```python
import contextlib
import numpy as np
import concourse.bass as bass
import concourse.mybir as mybir
from concourse.bass_utils import run_bass_kernel_spmd

F32 = mybir.dt.float32
BF16 = mybir.dt.bfloat16
I32 = mybir.dt.int32
U32 = mybir.dt.uint32
AF = mybir.ActivationFunctionType
ALU = mybir.AluOpType
AX = mybir.AxisListType


class Buf:
    __slots__ = ("name", "last_w", "readers", "excl")

    def __init__(self, name, excl=False):
        self.name = name
        self.last_w = None
        self.readers = []
        self.excl = excl


class Prog:
    ENGS = ("pe", "act", "dve", "pool", "sp")
    NDMA = {"sp": 24, "act": 12, "pool": 24}

    def __init__(self, nc, stack):
        self.nc = nc
        self.stack = stack
        self.stream = {e: [] for e in self.ENGS}
        self.cnt = {e: 0 for e in self.ENGS}
        self.waited = {e: {} for e in self.ENGS}
        self.sems = {}
        for e in self.ENGS:
            self.sems[e] = stack.enter_context(nc.semaphore("s_" + e))
        self.dslots = {}
        self.dnext = {}
        for q, n in self.NDMA.items():
            self.dslots[q] = []
            for i in range(n):
                k = "d_%s_%d" % (q, i)
                self.sems[k] = stack.enter_context(nc.semaphore(k))
                self.dslots[q].append([k, 0])
            self.dnext[q] = 0
        self.nbuf = 0

    def buf(self, name=None, excl=False):
        self.nbuf += 1
        return Buf(name or ("b%d" % self.nbuf), excl)

    def _deps(self, e, reads, writes):
        deps = set()
        for b in reads:
            if b.last_w is not None:
                deps.add(b.last_w)
            if b.excl:
                for r in b.readers:
                    if r[0] != e:
                        deps.add(r)
        for b in writes:
            if b.last_w is not None:
                deps.add(b.last_w)
            for r in b.readers:
                deps.add(r)
        waits = []
        best = {}
        for (sk, v) in deps:
            if sk == e and e == "pe":
                continue
            if v > best.get(sk, 0):
                best[sk] = v
        for sk, v in best.items():
            if self.waited[e].get(sk, 0) >= v:
                continue
            self.waited[e][sk] = v
            waits.append((sk, v))
        return waits

    def _commit(self, tok, reads, writes):
        for b in reads:
            b.readers.append(tok)
        for b in writes:
            b.last_w = tok
            b.readers = []

    def op(self, e, fn, reads=(), writes=()):
        reads = [b for b in reads if b is not None]
        writes = [b for b in writes if b is not None]
        waits = self._deps(e, reads, writes)
        self.cnt[e] += 1
        tok = (e, self.cnt[e])
        self.stream[e].append((waits, fn, e, 1))
        self._commit(tok, reads, writes)
        return tok

    def dma(self, q, fn, reads=(), writes=()):
        reads = [b for b in reads if b is not None]
        writes = [b for b in writes if b is not None]
        waits = self._deps(q, reads, writes)
        slots = self.dslots[q]
        i = self.dnext[q]
        self.dnext[q] = (i + 1) % len(slots)
        sk, tot = slots[i]
        if tot > 0 and self.waited[q].get(sk, 0) < tot:
            self.waited[q][sk] = tot
            waits.append((sk, tot))
        slots[i][1] = tot + 16
        tok = (sk, tot + 16)
        self.stream[q].append((waits, fn, sk, 16))
        self._commit(tok, reads, writes)
        return tok

    def finish(self):
        for q in self.NDMA:
            waits = []
            for sk, tot in self.dslots[q]:
                if tot > 0 and self.waited[q].get(sk, 0) < tot:
                    self.waited[q][sk] = tot
                    waits.append((sk, tot))
            if waits:
                self.stream[q].append((waits, None, None, 0))

    def barrier(self):
        toks = [(e, self.cnt[e]) for e in self.ENGS if self.cnt[e] > 0]
        for q in self.NDMA:
            for sk, tot in self.dslots[q]:
                if tot > 0:
                    toks.append((sk, tot))
        for e in self.ENGS:
            waits = []
            for sk, v in toks:
                if sk == e:
                    continue
                if self.waited[e].get(sk, 0) >= v:
                    continue
                self.waited[e][sk] = v
                waits.append((sk, v))
            if waits:
                self.stream[e].append((waits, None, None, 0))

    def emit(self):
        nc = self.nc
        P = self
        with nc.Block() as block:
            def run(e, eng):
                for waits, fn, sk, inc in P.stream[e]:
                    for (wk, wv) in waits:
                        eng.wait_ge(P.sems[wk], wv)
                    if fn is None:
                        continue
                    ins = fn(eng)
                    ins.then_inc(P.sems[sk], inc)

            @block.tensor
            def _(eng):
                run("pe", eng)

            @block.scalar
            def _(eng):
                run("act", eng)

            @block.vector
            def _(eng):
                run("dve", eng)

            @block.gpsimd
            def _(eng):
                run("pool", eng)

            @block.sync
            def _(eng):
                run("sp", eng)
        self.stream = {e: [] for e in self.ENGS}


EPS = 1e-6
NS = 16
ST = 4
CH = 256
NCHUNK = 8
NT = 2048
NTOK = NT + NS * ST
SCALE = 0.125
BIGB = 240000.0


def build_nc(stage=99):
    nc = bass.Bass("TRN2", target_bir_lowering=False)

    used_inputs = []

    def din(name, shape, dt=F32):
        used_inputs.append(name)
        return nc.dram_tensor(name, shape, dt, kind="ExternalInput").ap()

    def dout(name, shape):
        return nc.dram_tensor(name, shape, F32, kind="ExternalOutput").ap()

    xp_d = din("xp", [NT, 1024]); xs_d = din("xs", [64, 1024])
    if stage >= 3:
        ckv_d = din("ckv", [2560 * 128, 512]); cwin_d = din("cwin", [NS, 512, 256])
    sconv_d = din("sconv", [48, 512]); srnn_d = din("srnn", [NS, 512])
    ptab_d = din("ptab", [1, 256], I32)
    wtm_d = din("wtm", [1024, 1304]); wfm_d = din("wfm", [1024, 1024])
    wout_d = din("wout", [1024, 1024]); wup_d = din("wup", [1024, 4096]); wdown_d = din("wdown", [4096, 1024])
    w1k_d = din("w1k", [128, 32, 128]); w1v_d = din("w1v", [128, 32, 128])
    w2k_d = din("w2k", [128, 2, 64]); w2v_d = din("w2v", [128, 2, 64])
    pek_d = din("pek", [128, 32]); pev_d = din("pev", [128, 32])
    wra_d = din("wra", [128, 4, 128]); wri_d = din("wri", [128, 4, 128])
    vecs_d = din("vecs", [128, 56]); nfin_d = din("nfin", [1, 1024])
    identf_d = din("identf", [128, 128]); trile_d = din("trile", [128, 128]); trigt_d = din("trigt", [128, 128])
    cm_d = din("cm", [128, NT]); ec_d = din("ec", [33, 2056]); ov_d = din("ov", [128, 33])
    ropep_d = din("ropep", [128, 16, 16]); ropes_d = din("ropes", [64, 16])
    selvp_d = din("selvp", [128, 16, 33]); selcp_d = din("selcp", [128, 16, 33])
    selvs_d = din("selvs", [4, 33]); selcs_d = din("selcs", [4, 33])
    iota_d = din("iotap", [128, 1])

    yp_d = dout("y_p", [NT, 1024]); ys_d = dout("y_s", [64, 1024])
    kvp_d = dout("kv_p", [NT, 512]); kvs_d = dout("kv_s", [64, 512])
    winp_d = dout("win_p", [512, 256]); wins_d = dout("win_s", [NS, 512, 256])
    convp_d = dout("conv_p", [3, 512]); convs_d = dout("conv_s", [48, 512])
    hp_d = dout("h_p", [4, 128]); hs_d = dout("h_s", [NS, 512])

    with contextlib.ExitStack() as glob:
        P = Prog(nc, glob)

        def sbt(stack, name, shape, dt):
            return stack.enter_context(nc.sbuf_tensor("s_" + name, shape, dt))

        banks = [glob.enter_context(nc.psum_tensor("pb%d" % i, [128, 512], F32)) for i in range(8)]
        bbank = [P.buf("bank%d" % i, excl=True) for i in range(8)]
        apool = [0]

        def abank():
            i = apool[0]
            apool[0] = (i + 1) % 2
            return banks[i], bbank[i]

        def MM(out, lhsT, rhs, start, stop, R, W):
            P.op("pe", lambda e: e.matmul(out, lhsT=lhsT, rhs=rhs, start=start, stop=stop,
                                          skip_group_check=True), R, W)

        def TR(out, in_, ident, R, W):
            P.op("pe", lambda e: e.transpose(out=out, in_=in_, identity=ident), R, W)

        def ACT(out, in_, func, R, W, bias=None, scale=1.0, accum=None):
            kw = {}
            if bias is not None:
                kw["bias"] = bias
            if accum is not None:
                kw["accum_out"] = accum
            P.op("act", lambda e: e.activation(out=out, in_=in_, func=func, scale=scale, **kw), R, W)

        def TS(eng, out, in0, s1, s2, op0, op1, R, W):
            if s2 is None:
                P.op(eng, lambda e: e.tensor_scalar(out=out, in0=in0, scalar1=s1, scalar2=None, op0=op0), R, W)
            else:
                P.op(eng, lambda e: e.tensor_scalar(out=out, in0=in0, scalar1=s1, scalar2=s2, op0=op0, op1=op1), R, W)

        def TT(eng, out, in0, in1, op, R, W):
            P.op(eng, lambda e: e.tensor_tensor(out=out, in0=in0, in1=in1, op=op), R, W)

        def STT(out, in0, scalar, in1, op0, op1, R, W, accum=None):
            kw = {} if accum is None else {"accum_out": accum}
            P.op("dve", lambda e: e.scalar_tensor_tensor(out=out, in0=in0, scalar=scalar, in1=in1,
                                                         op0=op0, op1=op1, **kw), R, W)

        def CP(eng, out, in_, R, W):
            if eng == "act":
                P.op("act", lambda e: e.copy(out=out, in_=in_), R, W)
            else:
                P.op(eng, lambda e: e.tensor_copy(out=out, in_=in_), R, W)

        def MEMSET(eng, ap, val, W):
            P.op(eng, lambda e: e.memset(ap, val), (), W)

        def DMA(q, out, in_, R, W, **kw):
            if q == "pool" and out.dtype != in_.dtype:
                kw.setdefault("max_dma_last_dim", 2048)
            P.dma(q, lambda e: e.dma_start(out=out, in_=in_, **kw), R, W)

        ident_b = sbt(glob, "ident_b", [128, 128], BF16); b_identb = P.buf()
        ident_f = sbt(glob, "ident_f", [128, 128], F32); b_identf = P.buf()
        ones_f = sbt(glob, "ones_f", [128, 128], F32); b_ones = P.buf()
        tri_le = sbt(glob, "tri_le", [128, 128], BF16); b_trile = P.buf()
        tri_gt = sbt(glob, "tri_gt", [128, 128], BF16); b_trigt = P.buf()
        vecs = sbt(glob, "vecs", [128, 56], F32); b_vecs = P.buf()
        nls = sbt(glob, "nls", [128, 4], F32); b_nls = P.buf()
        mixedT = sbt(glob, "mixedT", [128, 8, NTOK], BF16)
        b_mix = [P.buf("mix%d" % i) for i in range(NCHUNK + 1)]

        DMA("sp", ident_f[:], identf_d, (), [b_identf])
        DMA("pool", ident_b[:], identf_d, (), [b_identb])
        DMA("pool", tri_le[:], trile_d, (), [b_trile])
        DMA("pool", tri_gt[:], trigt_d, (), [b_trigt])
        DMA("sp", vecs[:], vecs_d, (), [b_vecs])
        MEMSET("pool", ones_f[:], 1.0, [b_ones])
        ACT(nls[:], vecs[:, 36:40], AF.Exp, [b_vecs], [b_nls], scale=-1.0)
        ACT(nls[:], nls[:], AF.Ln, [b_nls], [b_nls], bias=1.0)
        TS("dve", nls[:], nls[:], -8.0, None, ALU.mult, None, [b_nls], [b_nls])
        hb = sbt(glob, "hb", [128, 12], F32); b_hb = P.buf()
        TS("dve", hb[:, 0:8], vecs[:, 28:36], 0.5, None, ALU.mult, None, [b_vecs], [b_hb])
        TS("dve", hb[:, 8:12], nls[:], 0.5, None, ALU.mult, None, [b_nls, b_hb], [b_hb])

        V_NMIX = vecs[:, 0:8]; V_NMLP = vecs[:, 8:16]; V_GATT = vecs[:, 16:20]

        def vcol(base, c):
            return vecs[:, base + c:base + c + 1]

        with contextlib.ExitStack() as ls:
            Wtm = sbt(ls, "Wtm", [128, 8, 1304], BF16); b_wtm = [P.buf() for _ in range(8)]
            Wfm = sbt(ls, "Wfm", [128, 8, 1024], BF16); b_wfm = [P.buf() for _ in range(8)]
            for kc in range(8):
                DMA("pool", Wtm[:, kc, :], wtm_d[kc * 128:(kc + 1) * 128, :], (), [b_wtm[kc]])
            for kc in range(8):
                DMA("pool", Wfm[:, kc, :], wfm_d[kc * 128:(kc + 1) * 128, :], (), [b_wfm[kc]])
            W1 = [sbt(ls, "W1k", [128, 32, 128], BF16), sbt(ls, "W1v", [128, 32, 128], BF16)]
            b_w1 = [P.buf(), P.buf()]
            W2 = [sbt(ls, "W2k", [128, 2, 64], BF16), sbt(ls, "W2v", [128, 2, 64], BF16)]
            b_w2 = [P.buf(), P.buf()]
            pe2 = [sbt(ls, "pek2", [128, 32], BF16), sbt(ls, "pev2", [128, 32], BF16)]
            b_pe2 = [P.buf(), P.buf()]
            pec = sbt(ls, "pec", [128, 2], F32); b_pec = P.buf()
            wra = sbt(ls, "wra", [128, 4, 128], BF16); b_wra = P.buf()
            wri = sbt(ls, "wri", [128, 4, 128], BF16); b_wri = P.buf()
            DMA("pool", W1[0][:], w1k_d, (), [b_w1[0]]); DMA("pool", W1[1][:], w1v_d, (), [b_w1[1]])
            DMA("pool", W2[0][:], w2k_d, (), [b_w2[0]]); DMA("pool", W2[1][:], w2v_d, (), [b_w2[1]])
            DMA("pool", pe2[0][:], pek_d, (), [b_pe2[0]]); DMA("pool", pe2[1][:], pev_d, (), [b_pe2[1]])
            DMA("pool", wra[:], wra_d, (), [b_wra]); DMA("pool", wri[:], wri_d, (), [b_wri])

            KS = sbt(ls, "KS", [128, 2, 2056], BF16)
            b_ks = [[P.buf() for _ in range(17)] for _ in range(2)]; b_ksE = P.buf()
            KWh = [None]; b_kw = [[P.buf() for _ in range(16)] for _ in range(2)]
            Vs = sbt(ls, "Vs", [128, 17, 2, 65], BF16); b_vs = [P.buf() for _ in range(17)]
            Vwh = [None]; b_vw = [P.buf() for _ in range(16)]
            kvcT = sbt(ls, "kvcT", [128, 2, NT], BF16); b_kvc = [P.buf() for _ in range(16)]
            hid = [sbt(ls, "hidk", [128, 128], BF16), sbt(ls, "hidv", [128, 128], BF16)]
            b_hid = [P.buf(), P.buf()]
            KcT = sbt(ls, "KcT", [64, 2, 128], BF16); b_kct = P.buf()
            Vc = sbt(ls, "Vc", [128, 2, 98], BF16); b_vc = P.buf(); b_vcc = P.buf()
            for kv in range(2):
                DMA("pool", KS[64:97, kv, :], ec_d, (), [b_ksE])
            MEMSET("pool", Vs[:, :, :, 64:65], 1.0, [b_vs[0]])
            MEMSET("pool", Vc[:, :, 64:65], 1.0, [b_vcc])
            for kv in range(2):
                DMA("pool", Vc[:, kv, 65:98], ov_d, (), [b_vcc])
            MEMSET("pool", hid[0][:], 0.0, [b_hid[0]]); MEMSET("pool", hid[1][:], 0.0, [b_hid[1]])

            for w in range(2):
                bk, bb = abank()
                for j in range(32):
                    MM(bk[:, 0:1], W1[w][:, j, :], pe2[w][:, j:j + 1], j == 0, j == 31, [b_w1[w], b_pe2[w]], [bb])
                CP("dve", pec[:, w:w + 1], bk[:, 0:1], [bb], [b_pec])

            xin = [sbt(ls, "xin0", [128, 1024], F32)]; b_xin = [P.buf(), P.buf()]
            xnb = sbt(ls, "xnb", [128, 1024], BF16); b_xnb = P.buf()
            junk = sbt(ls, "junk", [128, 1024], BF16); b_junk = P.buf()
            st4 = sbt(ls, "st4", [128, 8], F32); b_st4 = P.buf()
            uT = sbt(ls, "uT", [128, 8, CH], BF16); b_uT = [P.buf(), P.buf()]
            zsb = sbt(ls, "zsb", [128, 1280], F32); b_zsb = P.buf()
            gt = [sbt(ls, "gt%d" % i, [128, 24], F32) for i in range(2)]; b_gt = [P.buf(), P.buf()]
            rt = sbt(ls, "rt", [128, 4, 96], F32); b_rt = P.buf()
            tb = sbt(ls, "tb", [128, 1024], BF16); b_tb = P.buf()
            tbq = sbt(ls, "tbq", [128, 512], BF16); b_tbq = P.buf()
            QS = sbt(ls, "QS", [128, 2, 4, CH], BF16); b_qs = [P.buf(), P.buf()]
            QC = sbt(ls, "QC", [64, 2, 4, CH], BF16); b_qc = [P.buf(), P.buf()]
            xc = sbt(ls, "xc", [128, CH], F32); b_xc = P.buf()
            xcb = sbt(ls, "xcb", [128, CH], BF16); b_xcb = P.buf()
            rr = sbt(ls, "rr", [128, CH], F32); b_rr = P.buf()
            ig = sbt(ls, "ig", [128, CH], F32); b_ig = P.buf()
            aa = sbt(ls, "aa", [128, CH], F32); b_aa = P.buf()
            asc = sbt(ls, "asc", [128, CH], F32); b_asc = P.buf()
            t1 = sbt(ls, "t1", [128, CH], F32); b_t1 = P.buf()
            bb_ = sbt(ls, "bb_", [128, CH], F32); b_bb = P.buf()
            hh = sbt(ls, "hh", [128, CH], F32); b_hh = P.buf()
            gl = sbt(ls, "gl", [128, CH], F32); b_gl = P.buf()
            yy = sbt(ls, "yy", [128, 4, CH], F32); b_yy = [P.buf() for _ in range(4)]
            ysq = sbt(ls, "ysq", [128, CH], F32); b_ysq = P.buf()
            rsd = sbt(ls, "rsd", [128, CH], F32); b_rsd = P.buf()
            hcar = sbt(ls, "hcar", [128, 4], F32); b_hcar = P.buf()
            MEMSET("pool", hcar[:], 0.0, [b_hcar])
            NPB = 4
            Pb = [sbt(ls, "Pb%d" % i, [128, 512], BF16) for i in range(NPB)]; b_pb = [P.buf() for _ in range(NPB)]
            pbi = [0]
            rbi = [0]
            occ = sbt(ls, "occ", [128, 4, 98], F32); b_occ = P.buf()
            oall = sbt(ls, "oall", [128, 4, 4, 65], F32); b_oall = [P.buf() for _ in range(4)]
            stg = sbt(ls, "stg", [16, 4, 65], F32); b_stg = P.buf()
            sm = sbt(ls, "sm", [128, 64], F32); b_sm = P.buf()
            impt = sbt(ls, "impt", [128, 4, 33], F32); b_impt = P.buf()
            sc = sbt(ls, "sc", [128, 33], F32); b_sc = P.buf()
            sc2 = sbt(ls, "sc2", [128, 33], F32); b_sc2 = P.buf()
            m8 = sbt(ls, "m8", [128, 16], F32); b_m8 = P.buf()
            sbp2 = [sbt(ls, "sbp%d" % i, [128, 97], F32) for i in range(2)]; b_sbp2 = [P.buf(), P.buf()]
            MEMSET("pool", sbp2[0][:], 0.0, [b_sbp2[0]]); MEMSET("pool", sbp2[1][:], 0.0, [b_sbp2[1]])
            oat = sbt(ls, "oat", [128, 512], F32); b_oat = P.buf()
            otm = sbt(ls, "otm", [128, 256], F32); b_otm = P.buf()
            oab2 = [sbt(ls, "oab%d" % i, [128, 512], BF16) for i in range(2)]; b_oab2 = [P.buf(), P.buf()]; oabi = [0]
            outst = sbt(ls, "outst", [128, 512], F32); b_outst = P.buf()

            def load_norm_T(src_ap, nt, xi, ucol0):
                xt, bx = xin[xi], b_xin[xi]
                DMA("sp", xt[0:nt, :], src_ap, (), [bx])
                ACT(junk[0:nt, :], xt[0:nt, :], AF.Square, [bx], [b_junk, b_st4], accum=st4[0:nt, 0:1])
                ACT(st4[0:nt, 1:2], st4[0:nt, 0:1], AF.Sqrt, [b_st4], [b_st4], bias=EPS, scale=1.0 / 1024)
                P.op("dve", lambda e: e.reciprocal(out=st4[0:nt, 2:3], in_=st4[0:nt, 1:2]), [b_st4], [b_st4])
                TS("dve", xnb[0:nt, :], xt[0:nt, :], st4[0:nt, 2:3], None, ALU.mult, None, [bx, b_st4], [b_xnb])
                bk, bbk = abank()
                pv = bk[:].bitcast(BF16).rearrange("p (c t) -> p c t", c=8)
                for c in range(8):
                    TR(pv[:, c, 0:nt], xnb[0:nt, c * 128:(c + 1) * 128], ident_b[0:nt, 0:nt], [b_xnb, b_identb], [bbk])
                ub = b_uT[(ucol0 // 128) % 2] if nt == 128 else None
                wr = [ub] if ub is not None else list(b_uT)
                TT("dve", uT[:, :, ucol0:ucol0 + nt], pv[:, :, 0:nt],
                   V_NMIX.unsqueeze(2).broadcast_to([128, 8, nt]), ALU.mult, [bbk, b_vecs], wr)
                return wr

            def proj_tile(nt, ucol0, ub, rope_ap, brope, gti, qcol0, kcol0, vtile, kvout, winout, do_kvc=True, knew=None):
                g = gt[gti]; bg = b_gt[gti]
                offs = [(0, 512), (512, 280), (792, 512)]
                pz = []
                for gi, (o, w) in enumerate(offs):
                    bk, bbk = banks[gi], bbank[gi]
                    for kc in range(8):
                        MM(bk[0:nt, 0:w], uT[:, kc, ucol0:ucol0 + nt], Wtm[:, kc, o:o + w], kc == 0, kc == 7,
                           ub + [b_wtm[kc]], [bbk])
                    pz.append((bk, bbk))
                CP("act", zsb[0:nt, 0:512], pz[0][0][0:nt, 0:512], [pz[0][1]], [b_zsb])
                CP("act", zsb[0:nt, 512:768], pz[1][0][0:nt, 0:256], [pz[1][1]], [b_zsb])
                ACT(g[0:nt, :], pz[1][0][0:nt, 256:280], AF.Sigmoid, [pz[1][1]], [bg])
                CP("dve", zsb[0:nt, 768:1280], pz[2][0][0:nt, 0:512], [pz[2][1]], [b_zsb])
                if stage < 0.215:
                    return None, None
                CP("act", tbq[0:nt, :], zsb[0:nt, 0:512], [b_zsb], [b_tbq])
                R = zsb[0:nt, 0:768].rearrange("p (h d) -> p h d", h=12)
                x1 = R[:, :, 0:8]; x2 = R[:, :, 8:16]
                cs = rope_ap[:, 0:8].unsqueeze(1).broadcast_to([nt, 12, 8])
                sn = rope_ap[:, 8:16].unsqueeze(1).broadcast_to([nt, 12, 8])
                rtv = [rt[0:nt, i, :].rearrange("p (h d) -> p h d", h=12) for i in range(4)]
                TT("dve", rtv[0], x1, cs, ALU.mult, [b_zsb, brope], [b_rt])
                TT("dve", rtv[1], x2, sn, ALU.mult, [b_zsb, brope], [b_rt])
                TT("dve", rtv[2], x2, cs, ALU.mult, [b_zsb, brope], [b_rt])
                TT("dve", rtv[3], x1, sn, ALU.mult, [b_zsb, brope], [b_rt])
                TT("dve", x1, rtv[0], rtv[1], ALU.subtract, [b_rt], [b_zsb])
                TT("dve", x2, rtv[2], rtv[3], ALU.add, [b_rt], [b_zsb])
                if stage < 0.225:
                    return None, None
                DMA("sp", kvout[:, 0:256], zsb[0:nt, 768:1024], [b_zsb], ())
                DMA("sp", kvout[:, 256:384], zsb[0:nt, 512:640], [b_zsb], ())
                DMA("sp", kvout[:, 384:512], zsb[0:nt, 1024:1152], [b_zsb], ())
                if winout is not None:
                    DMA("sp", winout[:, 0:128], zsb[0:nt, 640:768], [b_zsb], ())
                    DMA("sp", winout[:, 128:256], zsb[0:nt, 1152:1280], [b_zsb], ())
                if stage < 0.235:
                    return None, None
                CP("act", tb[0:nt, :], zsb[0:nt, 0:1024], [b_zsb], [b_tb])
                vsb = b_vs[vtile] if knew is None else None; vwb = b_vw[vtile] if (vtile < 16 and knew is None) else None
                if knew is None:
                    CP("pool", Vs[0:nt, vtile, :, 0:64], zsb[0:nt, 1024:1152].rearrange("p (h d) -> p h d", h=2),
                       [b_zsb], [vsb])
                if vwb is not None:
                    CP("pool", Vwh[0][0:nt, vtile, :, 0:64], zsb[0:nt, 1152:1280].rearrange("p (h d) -> p h d", h=2),
                       [b_zsb], [vwb])
                if stage < 0.245:
                    return None, None
                idn = ident_b[0:nt, 0:nt]
                bk, bbk = abank()
                pv = bk[:].bitcast(BF16)
                for h in range(8):
                    TR(pv[0:64, h * 128:h * 128 + nt], tb[0:nt, h * 64:(h + 1) * 64], idn, [b_tb, b_identb], [bbk])
                CP("act", QS[0:64, :, :, qcol0:qcol0 + nt],
                   pv[0:64, :].rearrange("p (k g t) -> p k g t", k=2, g=4)[:, :, :, 0:nt], [bbk], b_qs)
                bk, bbk = abank()
                pv = bk[:].bitcast(BF16)
                for h in range(8):
                    TR(pv[0:64, h * 128:h * 128 + nt], tbq[0:nt, h * 64:(h + 1) * 64], idn, [b_tbq, b_identb], [bbk])
                CP("dve", QC[0:64, :, :, qcol0:qcol0 + nt],
                   pv[0:64, :].rearrange("p (k g t) -> p k g t", k=2, g=4)[:, :, :, 0:nt], [bbk], b_qc)
                bk, bbk = abank()
                pv = bk[:].bitcast(BF16)
                for h in range(4):
                    TR(pv[0:64, h * 128:h * 128 + nt], tb[0:nt, 512 + h * 64:512 + (h + 1) * 64], idn,
                       [b_tb, b_identb], [bbk])
                if do_kvc:
                    for h in range(2):
                        TR(pv[0:128, (4 + h) * 128:(4 + h) * 128 + nt], tb[0:nt, 768 + h * 128:768 + (h + 1) * 128], idn,
                           [b_tb, b_identb], [bbk])
                kt = kcol0 // 128
                pvr = pv[:, :].rearrange("p (s t) -> p s t", s=8)
                if knew is not None:
                    KN_, KWN_, bkn = knew
                    CP("act", KN_[0:64, :, 0:nt], pvr[0:64, 0:2, 0:nt], [bbk], [bkn])
                    CP("dve", KWN_[0:64, :, 0:nt], pvr[0:64, 2:4, 0:nt], [bbk], [bkn])
                    return pvr, bbk
                CP("act", KS[0:64, :, kcol0:kcol0 + nt], pvr[0:64, 0:2, 0:nt], [bbk], [b_ks[0][kt], b_ks[1][kt]])
                if kt < 16:
                    CP("dve", KWh[0][0:64, :, kcol0:kcol0 + nt], pvr[0:64, 2:4, 0:nt], [bbk], [b_kw[0][kt], b_kw[1][kt]])
                if do_kvc:
                    CP("act", kvcT[:, :, kcol0:kcol0 + nt], pvr[:, 4:6, 0:nt], [bbk], [b_kvc[kt]])
                return pvr, bbk

            def rglru_chunk(N, ub, mcol0, bmix, xp_views, b_xp, sample=None):
                bss, bbss = banks[5], bbank[5]
                rlist = [0, 1, 2, 3, 4, 6, 7]

                def rbank():
                    i = rbi[0]
                    rbi[0] = (i + 1) % len(rlist)
                    return banks[rlist[i]], bbank[rlist[i]]

                def partA(c):
                    bkx, bbx = rbank()
                    for kc in range(8):
                        MM(bkx[:, 0:N], Wfm[:, kc, 512 + c * 128:512 + (c + 1) * 128], uT[:, kc, 0:N], kc == 0, kc == 7,
                           ub + [b_wfm[kc]], [bbx])
                    bkg, bbg = rbank()
                    for kc in range(8):
                        MM(bkg[:, 0:N], Wfm[:, kc, c * 128:(c + 1) * 128], uT[:, kc, 0:N], kc == 0, kc == 7,
                           ub + [b_wfm[kc]], [bbg])
                    xpv = xp_views[c]; bxp = b_xp[c]
                    if sample is None:
                        CP("act", xpv[:, 3:3 + N], bkx[:, 0:N], [bbx], [bxp])
                    else:
                        CP("act", xpv[:, :, 3:7], bkx[:, 0:N].rearrange("p (s t) -> p s t", t=4), [bbx], [bxp])
                    ACT(yy[:, c, 0:N], bkg[:, 0:N], AF.Gelu_apprx_tanh, [bbg], [b_yy[c]])
                for c in range(4):
                    partA(c)
                for c in range(4):
                    xpv = xp_views[c]; bxp = b_xp[c]
                    if sample is None:
                        taps = [xpv[:, j:j + N] for j in range(4)]
                        xco = xc[:, 0:N]
                    else:
                        taps = [xpv[:, :, j:j + 4] for j in range(4)]
                        xco = xc[:, 0:N].rearrange("p (s t) -> p s t", t=4)
                    TS("dve", xco, taps[0], vcol(40, c * 4 + 0), vcol(24, c), ALU.mult, ALU.add, [bxp, b_vecs], [b_xc])
                    for j in range(1, 4):
                        STT(xco, taps[j], vcol(40, c * 4 + j), xco, ALU.mult, ALU.add, [bxp, b_vecs, b_xc], [b_xc])
                    if sample is None:
                        CP("pool", xpv[:, 0:3], xpv[:, N:N + 3], [bxp], [bxp])
                    CP("act", xcb[:, 0:N], xc[:, 0:N], [b_xc], [b_xcb])
                    bkr, bbr = rbank()
                    MM(bkr[:, 0:N], wra[:, c, :], xcb[:, 0:N], True, True, [b_wra, b_xcb], [bbr])
                    bki, bbi = rbank()
                    MM(bki[:, 0:N], wri[:, c, :], xcb[:, 0:N], True, True, [b_wri, b_xcb], [bbi])
                    ACT(rr[:, 0:N], bkr[:, 0:N], AF.Tanh, [bbr, b_hb], [b_rr], bias=hb[:, c:c + 1], scale=0.5)
                    ACT(ig[:, 0:N], bki[:, 0:N], AF.Tanh, [bbi, b_hb], [b_ig], bias=hb[:, 4 + c:5 + c], scale=0.5)
                    ACT(aa[:, 0:N], rr[:, 0:N], AF.Exp, [b_rr, b_hb], [b_aa], bias=hb[:, 8 + c:9 + c], scale=hb[:, 8 + c:9 + c])
                    TT("pool", t1[:, 0:N], aa[:, 0:N], aa[:, 0:N], ALU.mult, [b_aa], [b_t1])
                    ACT(t1[:, 0:N], t1[:, 0:N], AF.Sqrt, [b_t1], [b_t1], bias=1.0, scale=-1.0)
                    STT(bb_[:, 0:N], ig[:, 0:N], 1.0, t1[:, 0:N], ALU.add, ALU.mult, [b_ig, b_t1], [b_bb])
                    STT(bb_[:, 0:N], bb_[:, 0:N], 0.5, xc[:, 0:N], ALU.mult, ALU.mult, [b_bb, b_xc], [b_bb])
                    if sample is None:
                        P.op("dve", lambda e, c=c: e.tensor_tensor_scan(out=hh[:, 0:N], data0=aa[:, 0:N], data1=bb_[:, 0:N],
                                                                          initial=hcar[:, c:c + 1], op0=ALU.mult, op1=ALU.add),
                             [b_aa, b_bb, b_hcar], [b_hh])
                        CP("pool", hcar[:, c:c + 1], hh[:, N - 1:N], [b_hh], [b_hcar])
                    else:
                        h0T, b_h0 = sample
                        a3 = aa[:, 0:N].rearrange("p (s t) -> p s t", t=4)
                        b3 = bb_[:, 0:N].rearrange("p (s t) -> p s t", t=4)
                        CP("pool", asc[:, 0:N], aa[:, 0:N], [b_aa], [b_asc])
                        MEMSET("pool", asc[:, 0:N].rearrange("p (s t) -> p s t", t=4)[:, :, 0:1], 0.0, [b_asc])
                        TT("dve", t1[:, 0:NS], a3[:, :, 0], h0T[:, c, :], ALU.mult, [b_aa, b_h0, b_t1], [b_t1])
                        TT("dve", b3[:, :, 0], b3[:, :, 0], t1[:, 0:NS], ALU.add, [b_bb, b_t1], [b_bb])
                        P.op("dve", lambda e: e.tensor_tensor_scan(out=hh[:, 0:N], data0=asc[:, 0:N], data1=bb_[:, 0:N],
                                                                    initial=0.0, op0=ALU.mult, op1=ALU.add),
                             [b_asc, b_bb], [b_hh])
                        bkh, bbh = rbank()
                        TR(bkh[0:NS, 0:128], hh[:, 3:N:4], ident_f[:, :], [b_hh, b_identf], [bbh])
                        CP("act", outst[0:NS, c * 128:(c + 1) * 128], bkh[0:NS, 0:128], [bbh], [b_outst])
                    TT("dve", yy[:, c, 0:N], yy[:, c, 0:N], hh[:, 0:N], ALU.mult, [b_yy[c], b_hh], [b_yy[c]])
                    TT("pool", ysq[:, 0:N], yy[:, c, 0:N], yy[:, c, 0:N], ALU.mult, [b_yy[c]], [b_ysq])
                    MM(bss[:, 0:N], ones_f[:, :], ysq[:, 0:N], c == 0, c == 3, [b_ones, b_ysq], [bbss])
                ACT(rsd[:, 0:N], bss[:, 0:N], AF.Sqrt, [bbss], [b_rsd], bias=EPS, scale=1.0 / 512)
                P.op("dve", lambda e: e.reciprocal(out=rsd[:, 0:N], in_=rsd[:, 0:N]), [b_rsd], [b_rsd])
                for c in range(4):
                    STT(mixedT[:, 4 + c, mcol0:mcol0 + N], yy[:, c, 0:N], vcol(20, c), rsd[:, 0:N], ALU.mult, ALU.mult,
                        [b_yy[c], b_vecs, b_rsd], [bmix])

            def compress(c0, nblk, rbufs):
                for w in range(2):
                    bk, bbk = abank()
                    for j in range(32):
                        lo = 16 * c0 + j
                        MM(bk[:, 0:nblk], W1[w][:, j, :], kvcT[:, w, lo:lo + 16 * (nblk - 1) + 1:16], j == 0, j == 31,
                           [b_w1[w]] + rbufs, [bbk])
                    ACT(hid[w][:, c0:c0 + nblk], bk[:, 0:nblk], AF.Gelu_apprx_tanh, [bbk, b_pec], [b_hid[w]],
                        bias=pec[:, w:w + 1])
                bk, bbk = abank()
                for h in range(2):
                    MM(bk[0:64, h * 128:(h + 1) * 128], W2[0][:, h, :], hid[0][:, :], True, True, [b_w2[0], b_hid[0]], [bbk])
                CP("act", KcT[:, :, :], bk[0:64, 0:256].rearrange("p (h c) -> p h c", h=2), [bbk], [b_kct])
                bk, bbk = abank()
                for h in range(2):
                    MM(bk[:, h * 64:(h + 1) * 64], hid[1][:, :], W2[1][:, h, :], h == 0, True, [b_w2[1], b_hid[1]], [bbk])
                CP("dve", Vc[:, :, 0:64], bk[:, 0:128].rearrange("p (h f) -> p h f", h=2), [bbk], [b_vc])

            def nextP():
                i = pbi[0]
                pbi[0] = (i + 1) % NPB
                return Pb[i], b_pb[i]

            def run_steps(steps, nq, qs_src, kdim, qcol0, rq, acc, bacc, merge=False):
                n = len(steps)
                W = 4 * nq
                sb = [(banks[3], bbank[3]), (banks[4], bbank[4]), (banks[2], bbank[2])]
                def emitS(i):
                    lhsT, nk, vap, mask, rb = steps[i]
                    bk, bbk = sb[i % 3]
                    MM(bk[0:nk, 0:W], lhsT, qs_src[0:kdim, :, qcol0:qcol0 + nq], True, True, rb + rq, [bbk])
                emitS(0)
                if n > 1:
                    emitS(1)
                for i in range(n):
                    if i + 2 < n:
                        emitS(i + 2)
                    lhsT, nk, vap, mask, rb = steps[i]
                    bk, bbk = sb[i % 3]
                    pt, bpt = nextP()
                    ACT(pt[0:nk, 0:W], bk[0:nk, 0:W], AF.Exp, [bbk], [bpt], scale=SCALE)
                    if mask is not None:
                        map_, mb = mask
                        p3 = pt[0:nk, 0:W].rearrange("p (g t) -> p g t", g=4)
                        TT("dve" if nq == 128 else "pool", p3, p3, map_.unsqueeze(1).broadcast_to([nk, 4, nq]), ALU.mult, [bpt, mb], [bpt])
                    ncol = vap.shape[-1]
                    if merge:
                        MM(acc[0:W, 0:ncol], pt[0:nk, 0:W], vap, i == 0, i == n - 1, [bpt] + rb, [bacc])
                        continue
                    for g in range(4):
                        MM(acc[0:nq, g * ncol:(g + 1) * ncol], pt[0:nk, g * nq:(g + 1) * nq], vap,
                           (i == 0 and g == 0), (i == n - 1), [bpt] + rb, [bacc])

            def attn_tile(nq, qcol0, cmp_mask, cmp_r, slc_steps, win_steps, selV, selC, b_sel, g_ap, bg, mcol0, bmix):
                idf = ident_f[0:nq, 0:nq]
                merge = (nq == 4)
                for kv in range(2):
                    steps = [(KcT[0:64, kv, 0:127], 127, Vc[0:127, kv, :], cmp_mask, [b_kct, b_vc, b_vcc] + cmp_r)]
                    run_steps(steps, nq, QC[:, kv], 64, qcol0, [b_qc[kv]], banks[5], bbank[5])
                    o = kv * 8
                    sbp, b_sbp = sbp2[kv], b_sbp2[kv]
                    CP("act", occ[0:nq, :, :], banks[5][0:nq, 0:392].rearrange("p (g c) -> p g c", g=4), [bbank[5]], [b_occ])
                    if kv == 0:
                        CP("dve", otm[0:nq, :].rearrange("p (g d) -> p g d", g=4), occ[0:nq, :, 0:64], [b_occ], [b_otm])
                    TS("dve", sm[0:nq, o:o + 4], occ[0:nq, :, 64], 1e-30, None, ALU.max, None, [b_occ], [b_sm])
                    P.op("dve", lambda e, o=o: e.reciprocal(out=sm[0:nq, o:o + 4], in_=sm[0:nq, o:o + 4]), [b_sm], [b_sm])
                    TT("dve", impt[0:nq], occ[0:nq, :, 65:98], sm[0:nq, o:o + 4].unsqueeze(2).broadcast_to([nq, 4, 33]),
                       ALU.mult, [b_occ, b_sm], [b_impt])
                    P.op("dve", lambda e: e.tensor_reduce(out=sc[0:nq, :], in_=impt[0:nq].rearrange("p g j -> p j g"),
                                                           axis=AX.X, op=ALU.add), [b_impt], [b_sc])
                    TT("dve", sc[0:nq, :], sc[0:nq, :], selV, ALU.mult, [b_sc, b_sel], [b_sc])
                    TT("dve", sc[0:nq, :], sc[0:nq, :], selC, ALU.add, [b_sc, b_sel], [b_sc])
                    P.op("dve", lambda e: e.max(out=m8[0:nq, 0:8], in_=sc[0:nq, :]), [b_sc], [b_m8])
                    P.op("dve", lambda e: e.match_replace(out=sc2[0:nq, :], in_to_replace=m8[0:nq, 0:8], in_values=sc[0:nq, :],
                                                           imm_value=-30000.0), [b_sc, b_m8], [b_sc2])
                    P.op("dve", lambda e: e.max(out=m8[0:nq, 8:16], in_=sc2[0:nq, :]), [b_sc2], [b_m8])
                    TS("dve", sbp[0:nq, 64:97], sc[0:nq, :], m8[0:nq, 15:16], BIGB, ALU.is_ge, ALU.mult, [b_sc, b_m8], [b_sbp])
                    TS("dve", sbp[0:nq, 64:97], sbp[0:nq, 64:97], -BIGB, None, ALU.add, None, [b_sbp], [b_sbp])
                wbank = [7, 5]
                sbank = [6, 7]
                for kv in range(2):
                    wb_ = wbank[kv]
                    run_steps(win_steps[kv], nq, QS[:, kv], 64, qcol0, [b_qs[kv]], banks[wb_], bbank[wb_], merge)
                    if merge:
                        CP("act", stg[0:16, kv * 2 + 1, :], banks[wb_][0:16, 0:65], [bbank[wb_]], [b_stg])
                    else:
                        CP("act", oall[0:nq, kv * 2 + 1], banks[wb_][0:nq, 0:260].rearrange("p (g c) -> p g c", g=4),
                           [bbank[wb_]], [b_oall[kv * 2 + 1]])
                for kv in range(2):
                    bk, bbk = abank()
                    TR(bk[0:97, 0:nq], sbp2[kv][0:nq, 0:97], idf, [b_sbp2[kv], b_identf], [bbk])
                    CP("dve", QS[64:97, kv, :, qcol0:qcol0 + nq], bk[64:97, 0:nq].unsqueeze(1).broadcast_to([33, 4, nq]),
                       [bbk], [b_qs[kv]])
                for kv in range(2):
                    sb_ = sbank[kv]
                    run_steps(slc_steps[kv], nq, QS[:, kv], 97, qcol0, [b_qs[kv]], banks[sb_], bbank[sb_], merge)
                    if merge:
                        CP("act", stg[0:16, kv * 2, :], banks[sb_][0:16, 0:65], [bbank[sb_]], [b_stg])
                    else:
                        CP("act", oall[0:nq, kv * 2], banks[sb_][0:nq, 0:260].rearrange("p (g c) -> p g c", g=4),
                           [bbank[sb_]], [b_oall[kv * 2]])
                if merge:
                    for g in range(4):
                        DMA("sp", oall[0:4, :, g, :], stg[4 * g:4 * g + 4, :, :], [b_stg], list(b_oall))
                return lambda: attn_epi(nq, g_ap, bg, mcol0, bmix)

            def attn_epi(nq, g_ap, bg, mcol0, bmix):
                for kv in range(2):
                    osl = oall[:, kv * 2]; owi = oall[:, kv * 2 + 1]
                    b_osl = b_oall[kv * 2]; b_owi = b_oall[kv * 2 + 1]
                    o = kv * 8
                    c1 = 16 + kv * 16
                    P.op("dve", lambda e, c1=c1, osl=osl: e.reciprocal(out=sm[0:nq, c1 + 4:c1 + 8], in_=osl[0:nq, :, 64]), [b_osl], [b_sm])
                    P.op("dve", lambda e, c1=c1, owi=owi: e.reciprocal(out=sm[0:nq, c1 + 8:c1 + 12], in_=owi[0:nq, :, 64]), [b_owi], [b_sm])
                    TT("dve", sm[0:nq, c1:c1 + 4], sm[0:nq, o:o + 4], g_ap[:, 0 * 8 + kv * 4:0 * 8 + kv * 4 + 4], ALU.mult, [b_sm, bg], [b_sm])
                    TT("dve", sm[0:nq, c1 + 4:c1 + 8], sm[0:nq, c1 + 4:c1 + 8], g_ap[:, 8 + kv * 4:8 + kv * 4 + 4], ALU.mult, [b_sm, bg], [b_sm])
                    TT("dve", sm[0:nq, c1 + 8:c1 + 12], sm[0:nq, c1 + 8:c1 + 12], g_ap[:, 16 + kv * 4:16 + kv * 4 + 4], ALU.mult, [b_sm, bg], [b_sm])
                    ov4 = oat[0:nq, kv * 256:(kv + 1) * 256].rearrange("p (g d) -> p g d", g=4)
                    cmpnum = otm[0:nq, :].rearrange("p (g d) -> p g d", g=4) if kv == 0 else occ[0:nq, :, 0:64]
                    cmpb = b_otm if kv == 0 else b_occ
                    def bc(col):
                        return sm[0:nq, col:col + 4].unsqueeze(2).broadcast_to([nq, 4, 64])
                    tmpv = outst[0:nq, 0:256].rearrange("p (g d) -> p g d", g=4)
                    TT("dve", ov4, cmpnum, bc(c1), ALU.mult, [cmpb, b_sm], [b_oat])
                    TT("pool", tmpv, osl[0:nq, :, 0:64], bc(c1 + 4), ALU.mult, [b_osl, b_sm], [b_outst])
                    TT("dve", ov4, ov4, tmpv, ALU.add, [b_oat, b_outst], [b_oat])
                    TT("pool", tmpv, owi[0:nq, :, 0:64], bc(c1 + 8), ALU.mult, [b_owi, b_sm], [b_outst])
                    TT("dve", ov4, ov4, tmpv, ALU.add, [b_oat, b_outst], [b_oat])
                oab, b_oab = oab2[oabi[0]], b_oab2[oabi[0]]
                oabi[0] = 1 - oabi[0]
                STT(junk[0:nq, 0:512], oat[0:nq, :], 1.0, oat[0:nq, :], ALU.mult, ALU.mult, [b_oat], [b_junk, b_st4],
                    accum=st4[0:nq, 4:5])
                ACT(st4[0:nq, 5:6], st4[0:nq, 4:5], AF.Ln, [b_st4], [b_st4], bias=EPS, scale=1.0 / 512)
                ACT(st4[0:nq, 6:7], st4[0:nq, 5:6], AF.Exp, [b_st4], [b_st4], scale=-0.5)
                TS("dve", oab[0:nq, :], oat[0:nq, :], st4[0:nq, 6:7], None, ALU.mult, None, [b_oat, b_st4], [b_oab])

                def fin():
                    bk, bbk = abank()
                    pv = bk[:].bitcast(BF16).rearrange("p (c t) -> p c t", c=8)
                    for c in range(4):
                        TR(pv[:, c, 0:nq], oab[0:nq, c * 128:(c + 1) * 128], ident_b[0:nq, 0:nq], [b_oab, b_identb], [bbk])
                    TT("dve", mixedT[:, 0:4, mcol0:mcol0 + nq], pv[:, 0:4, 0:nq],
                       V_GATT.unsqueeze(2).broadcast_to([128, 4, nq]), ALU.mult, [bbk, b_vecs], [bmix])
                return fin

            with contextlib.ExitStack() as lp:
                cm = sbt(lp, "cm", [128, NT], BF16); b_cm = P.buf()
                xin.append(sbt(lp, "xin1", [128, 1024], F32))
                KWh[0] = sbt(lp, "KW", [64, 2, NT], BF16)
                Vwh[0] = sbt(lp, "Vw", [128, 16, 2, 65], BF16)
                MEMSET("pool", Vwh[0][:, :, :, 64:65], 1.0, [b_vw[0]])
                ropep = sbt(lp, "ropep", [128, 16, 16], F32); b_ropep = P.buf()
                selvp = sbt(lp, "selvp", [128, 16, 33], F32); selcp = sbt(lp, "selcp", [128, 16, 33], F32); b_selp = P.buf()
                xpp = sbt(lp, "xpp", [128, 4, CH + 4], F32); b_xpp = [P.buf() for _ in range(4)]
                DMA("pool", cm[:], cm_d, (), [b_cm])
                DMA("sp", ropep[:], ropep_d, (), [b_ropep])
                DMA("sp", selvp[:], selvp_d, (), [b_selp]); DMA("sp", selcp[:], selcp_d, (), [b_selp])
                for c in range(4):
                    MEMSET("pool", xpp[:, c, 0:3], 0.0, [b_xpp[c]])
                xpv = [xpp[:, c, :] for c in range(4)]
                nchunks = NCHUNK if stage >= 1 else 1
                pend = [None]
                for ch in range(nchunks):
                    if stage < 0.2:
                        break
                    ubs = []
                    for ti in range(2):
                        tt = ch * 2 + ti
                        ubs += load_norm_T(xp_d[tt * 128:(tt + 1) * 128, :], 128, tt % 2, ti * 128)
                    for ti in range(2):
                        if stage < 0.3:
                            break
                        tt = ch * 2 + ti
                        proj_tile(128, ti * 128, [b_uT[ti]], ropep[:, tt, :], b_ropep, ti, ti * 128, tt * 128, tt,
                                  kvp_d[tt * 128:(tt + 1) * 128, :],
                                  winp_d[(tt - 12) * 128:(tt - 11) * 128, :] if tt >= 12 else None)
                    if stage >= 0.5:
                        rglru_chunk(CH, list(b_uT), ch * CH, b_mix[ch], xpv, b_xpp)
                    if stage < 2:
                        continue
                    c0 = 0 if ch == 0 else 16 * ch - 1
                    c1 = 16 * ch + 14
                    kr = [b_kvc[i] for i in range(max(0, 2 * ch - 1), 2 * ch + 2)]
                    compress(c0, c1 - c0 + 1, kr)
                    if pend[0] is not None:
                        pend[0](); pend[0] = None
                    fins = []
                    for ti in range(2):
                        tt = ch * 2 + ti
                        slc, win = [], []
                        for kv in range(2):
                            s_ = []
                            for kt in range(tt + 1):
                                mk = (tri_le[:, :], b_trile) if kt == tt else None
                                s_.append((KS[0:97, kv, kt * 128:(kt + 1) * 128], 128, Vs[:, kt, kv, :], mk,
                                           [b_ks[kv][kt], b_ksE, b_vs[kt], b_vs[0]]))
                            slc.append(s_)
                            w_ = []
                            for kt in range(max(0, tt - 4), tt + 1):
                                mk = (tri_le[:, :], b_trile) if kt == tt else ((tri_gt[:, :], b_trigt) if kt == tt - 4 else None)
                                w_.append((KWh[0][0:64, kv, kt * 128:(kt + 1) * 128], 128, Vwh[0][:, kt, kv, :], mk,
                                           [b_kw[kv][kt], b_vw[kt], b_vw[0]]))
                            win.append(w_)
                        fins.append(attn_tile(128, ti * 128, (cm[0:127, tt * 128:(tt + 1) * 128], b_cm), [], slc, win,
                                              selvp[:, tt, :], selcp[:, tt, :], b_selp, gt[ti][:, :], b_gt[ti], tt * 128, b_mix[ch])())
                        if ti == 1:
                            fins[0]()
                            pend[0] = fins[1]
                if pend[0] is not None:
                    pend[0](); pend[0] = None
                bk, bbk = abank()
                for c in range(4):
                    TR(bk[0:3, c * 128:(c + 1) * 128], xpp[:, c, 0:3], ident_f[:, :], [b_xpp[c], b_identf], [bbk])
                CP("act", outst[0:3, 0:512], bk[0:3, 0:512], [bbk], [b_outst])
                DMA("sp", convp_d, outst[0:3, 0:512], [b_outst], ())
                bk, bbk = abank()
                TR(bk[0:4, 0:128], hcar[:, 0:4], ident_f[:, :], [b_hcar, b_identf], [bbk])
                CP("act", oat[0:4, 0:128], bk[0:4, 0:128], [bbk], [b_oat])
                DMA("sp", hp_d, oat[0:4, 0:128], [b_oat], ())
                P.emit()
            P.barrier()

            if stage >= 3:
              with contextlib.ExitStack() as sp_:
                raw = sbt(sp_, "raw", [128, 16, 512], BF16); b_raw = [P.buf() for _ in range(16)]
                KWh[0] = sbt(sp_, "KWs", [64, 2, 520], BF16)
                Vwh[0] = sbt(sp_, "Vws", [128, 5, 2, 65], BF16)
                MEMSET("pool", Vwh[0][:, :, :, 64:65], 1.0, [b_vw[0]])
                rawin = sbt(sp_, "rawin", [128, 4, 256], BF16); b_rawin = P.buf()
                ptb = sbt(sp_, "ptb", [128, 256], I32); b_ptb = P.buf()
                idx = sbt(sp_, "idx", [128, 256], I32); b_idx = P.buf()
                iop = sbt(sp_, "iop", [128, 1], F32); b_iop = P.buf()
                ropes = sbt(sp_, "ropes", [64, 16], F32); b_ropes = P.buf()
                KN = sbt(sp_, "KN", [128, 2, 64], BF16); KWN = sbt(sp_, "KWN", [64, 2, 64], BF16); b_kn = P.buf()
                vnew1 = sbt(sp_, "vnew", [4, 256], F32); vnew = [vnew1, vnew1]; b_vnew1 = P.buf(); b_vnew = [b_vnew1, b_vnew1]
                gsm = [sbt(sp_, "gsm%d" % i, [4, 24], F32) for i in range(2)]; b_gsm = [P.buf(), P.buf()]
                MEMSET("pool", KN[64:96, :, :], 0.0, [b_kn]); MEMSET("pool", KN[96:97, :, :], 1.0, [b_kn])
                selvs = sbt(sp_, "selvs", [4, 33], F32); selcs = sbt(sp_, "selcs", [4, 33], F32); b_sels = P.buf()
                sct = sbt(sp_, "sct", [48, 512], F32); b_sct = P.buf()
                srt = oat; b_srt = b_oat
                xps = sbt(sp_, "xps", [128, 4, NS, 7], F32); b_xps = [P.buf() for _ in range(4)]
                h0T = sbt(sp_, "h0T", [128, 4, NS], F32); b_h0 = P.buf()
                DMA("sp", ptb[:], ptab_d.broadcast_to([128, 256]), (), [b_ptb])
                DMA("sp", iop[:], iota_d, (), [b_iop])
                DMA("sp", ropes[:], ropes_d, (), [b_ropes])
                DMA("sp", selvs[:], selvs_d, (), [b_sels]); DMA("sp", selcs[:], selcs_d, (), [b_sels])
                DMA("sp", sct[:], sconv_d, (), [b_sct]); DMA("sp", srt[0:NS, :], srnn_d, (), [b_srt])
                TS("dve", idx[:], ptb[:], 128.0, iop[:, 0:1], ALU.mult, ALU.add, [b_ptb, b_iop], [b_idx])
                for c in range(4):
                    bk, bbk = abank()
                    TR(bk[:, 0:48], sct[0:48, c * 128:(c + 1) * 128], ident_f[0:48, 0:48], [b_sct, b_identf], [bbk])
                    CP("act", xps[:, c, :, 0:3], bk[:, 0:48].rearrange("p (s j) -> p s j", j=3), [bbk], [b_xps[c]])
                    bk, bbk = abank()
                    TR(bk[:, 0:NS], srt[0:NS, c * 128:(c + 1) * 128], ident_f[0:NS, 0:NS], [b_srt, b_identf], [bbk])
                    CP("act", h0T[:, c, :], bk[:, 0:NS], [bbk], [b_h0])
                ub = load_norm_T(xs_d, 64, 0, 0)
                rglru_chunk(64, ub, NT, b_mix[NCHUNK], [xps[:, c] for c in range(4)], b_xps, sample=(h0T, b_h0))
                DMA("sp", hs_d, outst[0:NS, 0:512], [b_outst], ())
                proj_tile(64, 0, ub, ropes[:, :], b_ropes, 0, 0, 2048, 16, kvs_d, None, do_kvc=False, knew=(KN, KWN, b_kn))
                convs3 = convs_d.rearrange("(s j) c -> s j c", j=3)
                dsts = [(xin[0][0:NS, 0:512], b_xin[0]), (xin[0][0:NS, 512:1024], b_xin[0]), (oat[0:NS, :], b_oat)]
                for j in range(3):
                    bk, bbk = abank()
                    for c in range(4):
                        TR(bk[0:NS, c * 128:(c + 1) * 128], xps[:, c, :, 4 + j], ident_f[:, :], [b_xps[c], b_identf], [bbk])
                    CP("act", dsts[j][0], bk[0:NS, 0:512], [bbk], [dsts[j][1]])
                    DMA("sp", convs3[:, j, :], dsts[j][0], [dsts[j][1]], ())
                def issue_gathers(s):
                    for j in range(16):
                        col = s * 16 + j
                        P.dma("pool", lambda e, j=j, col=col: e.indirect_dma_start(
                            out=raw[:, j, :], out_offset=None, in_=ckv_d,
                            in_offset=bass.IndirectOffsetOnAxis(ap=idx[:, col:col + 1], axis=0)),
                            [b_idx], [b_raw[j]])
                    DMA("pool", rawin[:], cwin_d[s].rearrange("(t p) c -> p t c", p=128), (), [b_rawin])
                    DMA("act", wins_d[s, 0:508, :], cwin_d[s, 4:512, :], (), ())
                def prep(s):
                        for j in range(16):
                            bk, bbk2 = abank()
                            pv = bk[:].bitcast(BF16).rearrange("p (s t) -> p s t", s=8)
                            for h in range(2):
                                TR(pv[0:64, h, :], raw[:, j, 256 + h * 64:256 + (h + 1) * 64], ident_b[:, :], [b_raw[j], b_identb], [bbk2])
                            TR(pv[:, 2, :], raw[:, j, 0:128], ident_b[:, :], [b_raw[j], b_identb], [bbk2])
                            TR(pv[:, 3, :], raw[:, j, 128:256], ident_b[:, :], [b_raw[j], b_identb], [bbk2])
                            CP("act", KS[0:64, :, j * 128:(j + 1) * 128], pv[0:64, 0:2, :], [bbk2], [b_ks[0][j], b_ks[1][j]])
                            CP("dve", kvcT[:, :, j * 128:(j + 1) * 128], pv[:, 2:4, :], [bbk2], [b_kvc[j]])
                            CP("pool", Vs[:, j, :, 0:64], raw[:, j, 384:512].rearrange("p (h d) -> p h d", h=2), [b_raw[j]], [b_vs[j]])
                        for j in range(4):
                            bk, bbk2 = abank()
                            pv = bk[:].bitcast(BF16).rearrange("p (s t) -> p s t", s=8)
                            for h in range(2):
                                TR(pv[0:64, h, :], rawin[:, j, h * 64:(h + 1) * 64], ident_b[:, :], [b_rawin, b_identb], [bbk2])
                            CP("act", KWh[0][0:64, :, j * 128:(j + 1) * 128], pv[0:64, 0:2, :], [bbk2], [b_kw[0][j], b_kw[1][j]])
                            CP("pool", Vwh[0][:, j, :, 0:64], rawin[:, j, 128:256].rearrange("p (h d) -> p h d", h=2), [b_rawin], [b_vw[j]])
                        if s + 1 < NS:
                            issue_gathers(s + 1)
                        compress(0, 127, list(b_kvc))
                pend = [None]
                issue_gathers(0)
                prep(0)
                for s in range(NS):
                    vn, bvn = vnew[s % 2], b_vnew[s % 2]
                    gs_, bgs = gsm[s % 2], b_gsm[s % 2]
                    DMA("sp", vn[0:4, :], zsb[s * 4:(s + 1) * 4, 1024:1280], [b_zsb], [bvn])
                    DMA("sp", gs_[0:4, :], gt[0][s * 4:(s + 1) * 4, :], [b_gt[0]], [bgs])
                    CP("pool", Vs[0:4, 16, :, 0:64], vn[0:4, 0:128].rearrange("p (h d) -> p h d", h=2), [bvn], [b_vs[16]])
                    CP("pool", Vwh[0][0:4, 4, :, 0:64], vn[0:4, 128:256].rearrange("p (h d) -> p h d", h=2), [bvn], [b_vw[4]])
                    DMA("act", wins_d[s, 508:512, 0:128], zsb[s * 4:(s + 1) * 4, 640:768], [b_zsb], ())
                    DMA("act", wins_d[s, 508:512, 128:256], zsb[s * 4:(s + 1) * 4, 1152:1280], [b_zsb], ())
                    if pend[0] is not None:
                        pend[0](); pend[0] = None
                    slc, win = [], []
                    for kv in range(2):
                        s_ = []
                        for kt in range(16):
                            s_.append((KS[0:97, kv, kt * 128:(kt + 1) * 128], 128, Vs[:, kt, kv, :], None,
                                       [b_ks[kv][kt], b_ksE, b_vs[kt], b_vs[0]]))
                        s_.append((KN[0:97, kv, s * 4:(s + 1) * 4], 4, Vs[0:4, 16, kv, :], (tri_le[0:4, 0:4], b_trile),
                                   [b_kn, b_vs[16], b_vs[0]]))
                        slc.append(s_)
                        w_ = []
                        for kt in range(4):
                            mk = (tri_gt[:, 0:4], b_trigt) if kt == 0 else None
                            w_.append((KWh[0][0:64, kv, kt * 128:(kt + 1) * 128], 128, Vwh[0][:, kt, kv, :], mk,
                                       [b_kw[kv][kt], b_vw[kt], b_vw[0]]))
                        w_.append((KWN[0:64, kv, s * 4:(s + 1) * 4], 4, Vwh[0][0:4, 4, kv, :], (tri_le[0:4, 0:4], b_trile),
                                   [b_kn, b_vw[4], b_vw[0]]))
                        win.append(w_)
                    epi_ = attn_tile(4, s * 4, None, [], slc, win, selvs[:, :], selcs[:, :], b_sels, gs_[0:4, :], bgs,
                                        NT + s * 4, b_mix[NCHUNK])
                    if s + 1 < NS:
                        prep(s + 1)
                    pend[0] = epi_()
                if pend[0] is not None:
                    pend[0](); pend[0] = None
                P.emit()
        P.barrier()

        if stage >= 4:
          with contextlib.ExitStack() as cs:
            NTI = 17
            x1 = sbt(cs, "x1", [128, NTI, 1024], F32); b_x1 = [P.buf() for _ in range(NTI)]
            vT = sbt(cs, "vT", [128, 8, NTOK], BF16); b_vT = [P.buf() for _ in range(NTI)]
            wout = sbt(cs, "wout", [128, 8, 1024], BF16); b_wout = [P.buf() for _ in range(8)]
            nfin = sbt(cs, "nfin", [128, 1024], F32); b_nfin = P.buf()
            wu = [sbt(cs, "wu%d" % i, [128, 8, 512], BF16) for i in range(2)]; b_wu = [[P.buf() for _ in range(8)] for _ in range(2)]
            wd = [sbt(cs, "wd%d" % i, [128, 4, 1024], BF16) for i in range(2)]; b_wd = [[P.buf() for _ in range(4)] for _ in range(2)]
            hT = [sbt(cs, "hT%d" % i, [128, 4, 512], BF16) for i in range(2)]; b_hT = [P.buf(), P.buf()]
            rl = [sbt(cs, "rl%d" % i, [128, 512], F32) for i in range(2)]; b_rl = [P.buf(), P.buf()]
            xnc = sbt(cs, "xnc", [128, 1024], BF16); b_xnc = P.buf()
            st5 = sbt(cs, "st5", [128, 8], F32); b_st5 = P.buf()
            yb1 = sbt(cs, "yb", [128, 1024], F32); yb = [yb1, yb1]; b_yb1 = P.buf(); b_yb = [b_yb1, b_yb1]
            for kc in range(8):
                DMA("pool", wout[:, kc, :], wout_d[kc * 128:(kc + 1) * 128, :], (), [b_wout[kc]])
            DMA("sp", nfin[:], nfin_d.broadcast_to([128, 1024]), (), [b_nfin])

            def load_w(q, bi):
                for kc in range(8):
                    DMA("pool", wu[bi][:, kc, :], wup_d[kc * 128:(kc + 1) * 128, q * 512:(q + 1) * 512], (), [b_wu[bi][kc]])
                for fc in range(4):
                    DMA("pool", wd[bi][:, fc, :], wdown_d[q * 512 + fc * 128:q * 512 + (fc + 1) * 128, :], (), [b_wd[bi][fc]])
            load_w(0, 0)

            def tile_info(ti):
                if ti < 16:
                    return 128, ti * 128, xp_d[ti * 128:(ti + 1) * 128, :], yp_d[ti * 128:(ti + 1) * 128, :], ti // 2
                return 64, NT, xs_d, ys_d, NCHUNK

            for ti in range(NTI):
                nt, col0, xsrc, ydst, mch = tile_info(ti)
                DMA("sp", x1[0:nt, ti, :], xsrc, (), [b_x1[ti]])
                for half in range(2):
                    bk, bbk = abank()
                    for kc in range(8):
                        MM(bk[0:nt, 0:512], mixedT[:, kc, col0:col0 + nt], wout[:, kc, half * 512:(half + 1) * 512],
                           kc == 0, kc == 7, [b_mix[mch], b_wout[kc]], [bbk])
                    TT("dve", x1[0:nt, ti, half * 512:(half + 1) * 512], x1[0:nt, ti, half * 512:(half + 1) * 512],
                       bk[0:nt, 0:512], ALU.add, [b_x1[ti], bbk], [b_x1[ti]])
                ACT(yb1[0:nt, :], x1[0:nt, ti, :], AF.Square, [b_x1[ti]], [b_yb1, b_st5], accum=st5[0:nt, 0:1])
                ACT(st5[0:nt, 1:2], st5[0:nt, 0:1], AF.Sqrt, [b_st5], [b_st5], bias=EPS, scale=1.0 / 1024)
                P.op("dve", lambda e, nt=nt: e.reciprocal(out=st5[0:nt, 2:3], in_=st5[0:nt, 1:2]), [b_st5], [b_st5])
                TS("dve", xnc[0:nt, :], x1[0:nt, ti, :], st5[0:nt, 2:3], None, ALU.mult, None, [b_x1[ti], b_st5], [b_xnc])
                bk, bbk = abank()
                pv = bk[:].bitcast(BF16).rearrange("p (c t) -> p c t", c=8)
                for c in range(8):
                    TR(pv[:, c, 0:nt], xnc[0:nt, c * 128:(c + 1) * 128], ident_b[0:nt, 0:nt], [b_xnc, b_identb], [bbk])
                TT("dve", vT[:, :, col0:col0 + nt], pv[:, :, 0:nt], V_NMLP.unsqueeze(2).broadcast_to([128, 8, nt]),
                   ALU.mult, [bbk, b_vecs], [b_vT[ti]])

            tchunks = [(0, 512, [0, 1, 2, 3]), (512, 512, [4, 5, 6, 7]), (1024, 512, [8, 9, 10, 11]),
                       (1536, 512, [12, 13, 14, 15]), (2048, 64, [16])]
            items = [(q, ci) for q in range(8) for ci in range(len(tchunks))]

            def up(i):
                q, ci = items[i]
                bi = q % 2
                c0, ncol, tiles = tchunks[ci]
                hb, bhb = hT[i % 2], b_hT[i % 2]
                for fc in range(4):
                    bk, bbk = abank()
                    for kc in range(8):
                        MM(bk[:, 0:ncol], wu[bi][:, kc, fc * 128:(fc + 1) * 128], vT[:, kc, c0:c0 + ncol], kc == 0, kc == 7,
                           [b_wu[bi][kc]] + [b_vT[t] for t in tiles], [bbk])
                    r_, br_ = rl[fc % 2], b_rl[fc % 2]
                    ACT(r_[:, 0:ncol], bk[:, 0:ncol], AF.Relu, [bbk], [br_])
                    TT("pool", hb[:, fc, 0:ncol], r_[:, 0:ncol], r_[:, 0:ncol], ALU.mult, [br_], [bhb])

            def down(i):
                q, ci = items[i]
                bi = q % 2
                c0, ncol, tiles = tchunks[ci]
                hb, bhb = hT[i % 2], b_hT[i % 2]
                for k, ti in enumerate(tiles):
                    nt = 128 if ti < 16 else 64
                    for half in range(2):
                        bk, bbk = banks[3 + (2 * k + half) % 4], bbank[3 + (2 * k + half) % 4]
                        for fc in range(4):
                            MM(bk[0:nt, 0:512], hb[:, fc, k * 128:k * 128 + nt], wd[bi][:, fc, half * 512:(half + 1) * 512],
                               fc == 0, fc == 3, [bhb, b_wd[bi][fc]], [bbk])
                        TT("dve", x1[0:nt, ti, half * 512:(half + 1) * 512], x1[0:nt, ti, half * 512:(half + 1) * 512],
                           bk[0:nt, 0:512], ALU.add, [b_x1[ti], bbk], [b_x1[ti]])
            load_w(1, 1)
            up(0)
            for i in range(len(items)):
                if i + 1 < len(items):
                    up(i + 1)
                down(i)
                q_, ci_ = items[i]
                if ci_ == len(tchunks) - 1 and q_ + 2 < 8:
                    load_w(q_ + 2, q_ % 2)
            for ti in range(NTI):
                nt, col0, xsrc, ydst, mch = tile_info(ti)
                ACT(xnc[0:nt, :], x1[0:nt, ti, :], AF.Square, [b_x1[ti]], [b_xnc, b_st5], accum=st5[0:nt, 4:5])
                ACT(st5[0:nt, 5:6], st5[0:nt, 4:5], AF.Sqrt, [b_st5], [b_st5], bias=EPS, scale=1.0 / 1024)
                P.op("dve", lambda e, nt=nt: e.reciprocal(out=st5[0:nt, 6:7], in_=st5[0:nt, 5:6]), [b_st5], [b_st5])
                y_, by_ = yb[ti % 2], b_yb[ti % 2]
                STT(y_[0:nt, :], x1[0:nt, ti, :], st5[0:nt, 6:7], nfin[0:nt, :], ALU.mult, ALU.mult,
                    [b_x1[ti], b_st5, b_nfin], [by_])
                DMA("sp", ydst, y_[0:nt, :], [by_], ())
            P.finish()
            P.emit()
        else:
            P.finish()
            P.emit()
    nc._used_inputs = used_inputs
    return nc, used_inputs


_STAGE = 99


def _consts():
    f32 = np.float32
    c = {}
    c["identf"] = np.eye(128, dtype=f32)
    k = np.arange(128)
    c["trile"] = (k[:, None] <= k[None, :]).astype(f32)
    c["trigt"] = (k[:, None] > k[None, :]).astype(f32)
    cc = np.arange(128)[:, None]
    t = np.arange(NT)[None, :]
    cm = ((16 * cc + 31) <= t).astype(f32)
    cm[127] = 0.0
    c["cm"] = cm
    j = np.arange(33)[:, None]
    kk = np.arange(2056)[None, :]
    c["ec"] = ((kk // 64) == j).astype(f32)
    c0 = (np.arange(128) * 16)[:, None]
    j0 = (np.arange(33) * 64)[None, :]
    ov = ((c0 < j0 + 64) & (c0 + 32 > j0)).astype(f32)
    ov[127] = 0.0
    c["ov"] = ov
    inv = (np.float32(500000.0) ** (-np.arange(8, dtype=f32) / np.float32(8))).astype(f32)

    def rope(pos):
        ang = pos.astype(f32)[:, None] * inv[None, :]
        return np.concatenate([np.cos(ang), np.sin(ang)], axis=1).astype(f32)
    pos = (np.arange(16)[None, :] * 128 + np.arange(128)[:, None])
    c["ropep"] = rope(pos.reshape(-1)).reshape(128, 16, 16)
    c["ropes"] = rope(2048 + (np.arange(64) % 4))

    def sel(tpos, nsel):
        cur = (tpos // 64)[:, None]
        jj = np.arange(33)[None, :]
        valid = (jj <= cur) & (jj < nsel)
        forced = ((jj == 0) | (jj == cur) | (jj == cur - 1)) & (jj < nsel)
        V = valid.astype(f32)
        C = ((V - 1.0) * 1e4 + forced.astype(f32) * 1e4).astype(f32)
        return V, C
    V, C = sel(pos.reshape(-1), 32)
    c["selvp"] = V.reshape(128, 16, 33); c["selcp"] = C.reshape(128, 16, 33)
    V, C = sel(2048 + np.arange(4), 33)
    c["selvs"] = V; c["selcs"] = C
    c["iotap"] = np.arange(128, dtype=f32).reshape(128, 1)
    return c


_NC_CACHE = {}


def kernel(**inp):
    f32 = np.float32
    g = lambda k: np.asarray(inp[k])
    w_in = g("w_in")[0].astype(f32)
    r = np.arange
    cols = np.concatenate([r(0, 512), r(768, 896), r(1024, 1152), r(1280, 1304), r(512, 640), r(640, 768),
                           r(896, 1024), r(1152, 1280)])
    shared = {}
    shared["wtm"] = np.ascontiguousarray(w_in[:, cols])
    shared["wfm"] = np.ascontiguousarray(w_in[:, 1304:2328])
    shared["wout"] = np.ascontiguousarray(g("w_out")[0].astype(f32))
    shared["wup"] = np.ascontiguousarray(g("w_up")[0].astype(f32))
    shared["wdown"] = np.ascontiguousarray(g("w_down")[0].astype(f32))
    for nm, w1n, w2n, pen in (("k", "w_ck1", "w_ck2", "pe_ck"), ("v", "w_cv1", "w_cv2", "pe_cv")):
        w1 = g(w1n)[0].astype(f32); w2 = g(w2n)[0].astype(f32); pe = g(pen)[0].astype(f32)
        w1bd = np.zeros((128, 32, 128), f32); w2p = np.zeros((128, 2, 64), f32); pe2 = np.zeros((128, 32), f32)
        for h in range(2):
            w1bd[h * 64:(h + 1) * 64, :, h * 64:(h + 1) * 64] = w1.transpose(1, 0, 2)
            w2p[h * 64:(h + 1) * 64, h, :] = w2
            pe2[h * 64:(h + 1) * 64, :] = pe.T
        shared["w1" + nm] = w1bd; shared["w2" + nm] = w2p; shared["pe" + nm] = pe2
    for nm, wn in (("wra", "w_ra"), ("wri", "w_ri")):
        w = g(wn)[0].astype(f32)
        bd = np.zeros((128, 4, 128), f32)
        for c in range(4):
            for b in range(2):
                bd[b * 64:(b + 1) * 64, c, b * 64:(b + 1) * 64] = w[2 * c + b]
        shared[nm] = bd
    vecs = np.zeros((128, 56), f32)
    T8 = lambda v: np.asarray(v, f32).reshape(-1, 128).T
    vecs[:, 0:8] = T8(g("norm_mix")[0]); vecs[:, 8:16] = T8(g("norm_mlp")[0])
    vecs[:, 16:20] = T8(g("g_attn")[0]); vecs[:, 20:24] = T8(g("g_rnn")[0])
    vecs[:, 24:28] = T8(g("conv_b")[0]); vecs[:, 28:32] = T8(g("b_ra")[0]); vecs[:, 32:36] = T8(g("b_ri")[0])
    vecs[:, 36:40] = T8(g("lam")[0])
    cw = g("conv_w")[0].astype(f32)
    for c in range(4):
        for tap in range(4):
            vecs[:, 40 + c * 4 + tap] = cw[tap, c * 128:(c + 1) * 128]
    shared["vecs"] = vecs
    shared["nfin"] = np.asarray(g("norm_final"), f32).reshape(1, 1024)
    shared["ckv"] = np.ascontiguousarray(g("cache_kv")[0].astype(f32).reshape(2560 * 128, 512))
    shared.update(_consts())

    xp = g("x_prompt").astype(f32); xs = g("x_sample").astype(f32)
    cwin = g("cache_win")[0].astype(f32).reshape(128, 512, 256)
    sconv = g("state_conv")[0].astype(f32); srnn = g("state_rnn")[0].astype(f32)
    pt = g("page_table").astype(np.int32)
    in_maps = []
    for c in range(8):
        m = dict(shared)
        m["xp"] = np.ascontiguousarray(xp[c])
        m["xs"] = np.ascontiguousarray(xs[c * 16:(c + 1) * 16].reshape(64, 1024))
        m["cwin"] = np.ascontiguousarray(cwin[c * 16:(c + 1) * 16])
        m["sconv"] = np.ascontiguousarray(sconv[c * 16:(c + 1) * 16].reshape(48, 512))
        m["srnn"] = np.ascontiguousarray(srnn[c * 16:(c + 1) * 16])
        m["ptab"] = np.ascontiguousarray(pt[c * 16:(c + 1) * 16].reshape(1, 256))
        in_maps.append(m)
    if _STAGE not in _NC_CACHE:
        _NC_CACHE[_STAGE] = build_nc(_STAGE)
    nc, used = _NC_CACHE[_STAGE]
    in_maps = [{k: v for k, v in m.items() if k in used} for m in in_maps]
    res = run_bass_kernel_spmd(nc, in_maps, core_ids=list(range(8)))
    R = res.results
    cat = lambda k: np.stack([np.asarray(R[c][k], f32) for c in range(8)])
    y_p = cat("y_p")
    y_s = cat("y_s").reshape(128, 4, 1024)
    kv_p = cat("kv_p").reshape(1, 8, 2048, 4, 2, 64)
    kv_s = cat("kv_s").reshape(1, 128, 4, 4, 2, 64)
    win_p = cat("win_p").reshape(1, 8, 512, 2, 2, 64)
    win_s = cat("win_s").reshape(1, 128, 512, 2, 2, 64)
    conv_p = cat("conv_p").reshape(1, 8, 3, 512)
    conv_s = cat("conv_s").reshape(1, 128, 3, 512)
    h_p = cat("h_p").reshape(1, 8, 512)
    h_s = cat("h_s").reshape(1, 128, 512)
    return (y_p, y_s, kv_p, kv_s, win_p, win_s, conv_p, conv_s, h_p, h_s)
```
